# Optimizing a Trainium2 kernel written in Bass

```python
import jax
import jax.numpy as jnp
from jax import lax
import numpy as np

D_MODEL = 1024
BATCH = 8
SEQ = 4096
DEPTH = 2

GRID_W = 64
CTX_LEN = 256
HEAD_DIM = 64
D_MIX = D_MODEL
RWKV_W = 3 * D_MIX // 8
HGRN_W = D_MIX // 4
ATTN_W = D_MIX - RWKV_W - HGRN_W
RWKV_HEADS = RWKV_W // HEAD_DIM
HGRN_HEADS = HGRN_W // HEAD_DIM
ATTN_Q_HEADS = ATTN_W // HEAD_DIM
ATTN_KV_HEADS = 2
ATTN_GROUP = ATTN_Q_HEADS // ATTN_KV_HEADS
ATTN_KV_W = ATTN_KV_HEADS * HEAD_DIM
DECAY_LORA = 64
ICLR_LORA = 64
VRES_LORA = 32
GN_EPS = 64e-5
NORM_EPS = 1e-6
HGRN_CHUNK = 32
ATTN_WINDOW = 128
ATTN_BLOCK = 128
ROPE_THETA = 10000.0
RWKV_SPLITS = (RWKV_W, RWKV_W, RWKV_W, DECAY_LORA, DECAY_LORA, ICLR_LORA, ICLR_LORA, RWKV_W)
HGRN_SPLITS = (HGRN_W, HGRN_W, HGRN_W, HGRN_W, HGRN_W)
ATTN_SPLITS = (ATTN_W, ATTN_KV_W, ATTN_KV_W, ATTN_W)
RWKV_COLS = sum(RWKV_SPLITS)
HGRN_COLS = sum(HGRN_SPLITS)
ATTN_COLS = sum(ATTN_SPLITS)
IN_COLS = RWKV_COLS + HGRN_COLS + ATTN_COLS
F32 = jnp.float32

kernel_name = 'hybrid_rwkv7_hgrn2_swa_diffusion_trunk'


def rms_norm(x, w):
    xf = x.astype(F32)
    y = xf * lax.rsqrt(jnp.mean(xf * xf, axis=-1, keepdims=True) + NORM_EPS)
    return (y * w.astype(F32)).astype(x.dtype)


def _split(p, sizes):
    cuts, acc = [], 0
    for s in sizes[:-1]:
        acc += s
        cuts.append(acc)
    return jnp.split(p, cuts, axis=-1)


def _heads(t, n):
    return t.reshape(t.shape[:-1] + (n, t.shape[-1] // n))


def _dirs_shared(t):
    return jnp.stack([t, jnp.flip(t, axis=1)])


def _dirs_own(t):
    return jnp.stack([t[0], jnp.flip(t[1], axis=1)])


def centred_shift(p, mu_prev, mu_next):
    prev = jnp.pad(p, ((0, 0), (1, 0), (0, 0)))[:, :-1]
    nxt = jnp.pad(p, ((0, 0), (0, 1), (0, 0)))[:, 1:]
    return p + mu_prev * (prev - p) + mu_next * (nxt - p)


def rope_1d(x, pos):
    half = x.shape[-1] // 2
    inv = ROPE_THETA ** (-jnp.arange(half, dtype=F32) / half)
    ang = pos.astype(F32)[:, None] * inv[None, :]
    cos = jnp.cos(ang)[None, :, None, :]
    sin = jnp.sin(ang)[None, :, None, :]
    x1, x2 = x[..., :half], x[..., half:]
    return jnp.concatenate([x1 * cos - x2 * sin, x1 * sin + x2 * cos], axis=-1)


def axial_rope(x, row, col):
    half = x.shape[-1] // 2
    return jnp.concatenate([rope_1d(x[..., :half], row), rope_1d(x[..., half:], col)], axis=-1)


def rwkv_features(pa, h, v_first, mu_prev, mu_next, w0, w2, a0, a2, k_k, k_a, vres):
    pa = centred_shift(pa.astype(F32), mu_prev, mu_next)
    r, k, v, wd_f, wd_b, ad_f, ad_b, g = _split(pa, RWKV_SPLITS)
    wd = jnp.stack([wd_f, wd_b])
    ad = jnp.stack([ad_f, ad_b])
    w_log = -jax.nn.softplus(-(w0[:, None, None, :] + jnp.einsum('zbtr,zrc->zbtc', jnp.tanh(wd), w2))) - 0.5
    decay = jnp.exp(-jnp.exp(w_log))
    a = jax.nn.sigmoid(a0[:, None, None, :] + jnp.einsum('zbtr,zrc->zbtc', ad, a2))
    if vres is not None:
        vw1, vw2, vb = vres
        v = v + (v_first - v) * jax.nn.sigmoid(vb + (h.astype(F32) @ vw1) @ vw2)
    B, T, _ = k.shape
    kk = _heads(k * k_k, RWKV_HEADS)
    kk = (kk / jnp.maximum(jnp.linalg.norm(kk, axis=-1, keepdims=True), 1e-12)).reshape(B, T, RWKV_W)
    k_dir = k * (1.0 + (a - 1.0) * k_a)
    b_dir = kk * a
    return (r, decay, k_dir, v, kk, b_dir, g, k)


def rwkv_mix(feats, s0, with_output):
    r, decay, k_dir, v, kk, b_dir = feats[:6]
    seqs = (_dirs_shared(r), _dirs_own(decay), _dirs_own(k_dir), _dirs_shared(v), _dirs_shared(kk), _dirs_own(b_dir))
    xs = tuple(jnp.transpose(_heads(t, RWKV_HEADS), (2, 0, 1, 3, 4)) for t in seqs)

    def step(state, inp):
        r_t, w_t, k_t, v_t, kk_t, b_t = inp
        sa = jnp.einsum('zbhvk,zbhk->zbhv', state, -kk_t)
        state = state * w_t[..., None, :] + sa[..., :, None] * b_t[..., None, :] + v_t[..., :, None] * k_t[..., None, :]
        y = jnp.einsum('zbhvk,zbhk->zbhv', state, r_t) if with_output else None
        return state, y

    state, ys = lax.scan(step, s0, xs)
    if not with_output:
        return state, None
    y = ys[:, 0] + jnp.flip(ys[:, 1], axis=0)
    return state, jnp.transpose(y, (1, 0, 2, 3))


def rwkv_readout(y, feats, r_k, gn_w, gn_b):
    r, v, g, k = feats[0], feats[3], feats[6], feats[7]
    B, T, _ = r.shape
    yc = y - jnp.mean(y, axis=-1, keepdims=True)
    yn = yc * lax.rsqrt(jnp.mean(yc * yc, axis=-1, keepdims=True) + GN_EPS)
    yn = yn.reshape(B, T, RWKV_W) * gn_w + gn_b
    bonus = jnp.sum(_heads(r, RWKV_HEADS) * _heads(k, RWKV_HEADS) * r_k, axis=-1, keepdims=True) * _heads(v, RWKV_HEADS)
    return (yn + bonus.reshape(B, T, RWKV_W)) * jax.nn.silu(g)


def hgrn_features(pb, lb):
    q, f_fwd, f_bwd, i, g = _split(pb.astype(F32), HGRN_SPLITS)
    lbb = lb[:, None, None, :]
    forget = lbb + (1.0 - lbb) * jax.nn.sigmoid(jnp.stack([f_fwd, f_bwd]))
    return (q, jnp.log(forget), 1.0 - forget, i, g)


def hgrn_mix(feats, s0, with_output):
    q, log_f, key, i = feats[:4]
    B, T, _ = q.shape

    def chunks(t):
        z, b, n, _ = t.shape
        return jnp.transpose(t.reshape(z, b, n // HGRN_CHUNK, HGRN_CHUNK, HGRN_HEADS, HEAD_DIM), (2, 0, 1, 4, 3, 5))

    xs = (chunks(_dirs_shared(q)), chunks(_dirs_own(key)), chunks(_dirs_shared(i)), chunks(_dirs_own(log_f)))
    lower_tri = jnp.tril(jnp.ones((HGRN_CHUNK, HGRN_CHUNK), dtype=bool))

    def step(state, inp):
        q_c, k_c, v_c, g_c = inp
        b = jnp.cumsum(g_c, axis=-2)
        b_last = b[..., -1, :]
        new_state = state * jnp.exp(b_last)[..., :, None] + jnp.einsum('zbhsk,zbhsv->zbhkv', k_c * jnp.exp(b_last[..., None, :] - b), v_c)
        if not with_output:
            return new_state, None
        o_inter = jnp.einsum('zbhtk,zbhkv->zbhtv', q_c * jnp.exp(b), state)
        diff = jnp.where(lower_tri[:, :, None], b[..., :, None, :] - b[..., None, :, :], -jnp.inf)
        att = jnp.sum(q_c[..., :, None, :] * k_c[..., None, :, :] * jnp.exp(diff), axis=-1)
        return new_state, o_inter + jnp.einsum('zbhts,zbhsv->zbhtv', att, v_c)

    state, outs = lax.scan(step, s0, xs)
    if not with_output:
        return state, None
    o = jnp.transpose(outs, (1, 2, 0, 4, 3, 5)).reshape(2, B, T, HGRN_HEADS, HEAD_DIM)
    return state, o[0] + jnp.flip(o[1], axis=1)


def hgrn_readout(o, g, norm_w):
    B, T, _ = g.shape
    on = o * lax.rsqrt(jnp.mean(o * o, axis=-1, keepdims=True) + NORM_EPS)
    return on.reshape(B, T, HGRN_W) * norm_w * jax.nn.silu(g)


def banded_attention(q, k, v, kc, vc, sink):
    B, T = q.shape[:2]
    nb = T // ATTN_BLOCK
    qb = q.reshape(B, nb, ATTN_BLOCK, ATTN_KV_HEADS, ATTN_GROUP, HEAD_DIM)

    def band(t):
        tb = t.reshape(B, nb, ATTN_BLOCK, ATTN_KV_HEADS, HEAD_DIM)
        tp = jnp.pad(tb, ((0, 0), (1, 1), (0, 0), (0, 0), (0, 0)))
        return jnp.concatenate([tp[:, :-2], tp[:, 1:-1], tp[:, 2:]], axis=2)

    kband, vband = band(k), band(v)
    blk = jnp.arange(nb)[:, None]
    q_pos = blk * ATTN_BLOCK + jnp.arange(ATTN_BLOCK)[None, :]
    k_pos = (blk - 1) * ATTN_BLOCK + jnp.arange(3 * ATTN_BLOCK)[None, :]
    kp = k_pos[:, None, :]
    valid = (kp >= 0) & (kp < T) & (jnp.abs(kp - q_pos[:, :, None]) <= ATTN_WINDOW)
    scale = HEAD_DIM ** -0.5
    s_loc = jnp.einsum('bnqhgd,bnkhd->bnhgqk', qb, kband) * scale
    s_loc = jnp.where(valid[None, :, None, None], s_loc, -jnp.inf)
    s_ctx = jnp.einsum('bnqhgd,bchd->bnhgqc', qb, kc) * scale
    s_sink = jnp.broadcast_to(sink.reshape(ATTN_KV_HEADS, ATTN_GROUP)[None, None, :, :, None, None], s_loc.shape[:-1] + (1,))
    probs = jax.nn.softmax(jnp.concatenate([s_loc, s_ctx, s_sink], axis=-1).astype(F32), axis=-1)
    nk, nc = 3 * ATTN_BLOCK, kc.shape[1]
    o = jnp.einsum('bnhgqk,bnkhd->bnqhgd', probs[..., :nk], vband) + jnp.einsum('bnhgqc,bchd->bnqhgd', probs[..., nk:nk + nc], vc)
    return o.reshape(B, T, ATTN_W)


def context_attention(qc, kc, vc, sink):
    B, C = qc.shape[:2]
    qg = qc.reshape(B, C, ATTN_KV_HEADS, ATTN_GROUP, HEAD_DIM)
    s = jnp.einsum('bqhgd,bkhd->bhgqk', qg, kc) * HEAD_DIM ** -0.5
    s_sink = jnp.broadcast_to(sink.reshape(ATTN_KV_HEADS, ATTN_GROUP)[None, :, :, None, None], s.shape[:-1] + (1,))
    probs = jax.nn.softmax(jnp.concatenate([s, s_sink], axis=-1).astype(F32), axis=-1)
    o = jnp.einsum('bhgqk,bkhd->bqhgd', probs[..., :C], vc)
    return o.reshape(B, C, ATTN_W)


def setup_inputs(seed: int = 0) -> dict:
    key = jax.random.key(seed)
    ks = jax.random.split(key, 27)
    L, D = DEPTH, D_MODEL

    def nrm(k, shape, s):
        return s * jax.random.normal(k, shape, F32)

    return {
        'x': nrm(ks[0], (BATCH, SEQ, D), 1.0),
        'c': nrm(ks[1], (BATCH, D), 1.0),
        'ctx': nrm(ks[2], (BATCH, CTX_LEN, D), 1.0),
        'c_ctx': nrm(ks[3], (D,), 1.0),
        'mod_w': nrm(ks[4], (L, D, 3 * D), 0.5 * D ** -0.5),
        'mod_b': nrm(ks[5], (L, 3 * D), 0.02),
        'norm_w': 1.0 + nrm(ks[6], (L, D), 0.02),
        'w_in': nrm(ks[7], (L, D, IN_COLS), D ** -0.5),
        'w_out': nrm(ks[8], (L, D_MIX, D), D_MIX ** -0.5),
        'rwkv_mu_prev': jax.random.uniform(ks[9], (L, RWKV_COLS), F32, 0.0, 0.5),
        'rwkv_mu_next': jax.random.uniform(ks[10], (L, RWKV_COLS), F32, 0.0, 0.5),
        'rwkv_w0': nrm(ks[11], (L, 2, RWKV_W), 0.5),
        'rwkv_w2': nrm(ks[12], (L, 2, DECAY_LORA, RWKV_W), 0.5 * DECAY_LORA ** -0.5),
        'rwkv_a0': nrm(ks[13], (L, 2, RWKV_W), 0.5),
        'rwkv_a2': nrm(ks[14], (L, 2, ICLR_LORA, RWKV_W), 0.5 * ICLR_LORA ** -0.5),
        'rwkv_k_k': 0.85 + nrm(ks[15], (L, RWKV_W), 0.05),
        'rwkv_k_a': 1.0 + nrm(ks[16], (L, RWKV_W), 0.05),
        'rwkv_r_k': nrm(ks[17], (L, RWKV_HEADS, HEAD_DIM), 0.1),
        'rwkv_gn_w': 1.0 + nrm(ks[18], (L, RWKV_W), 0.02),
        'rwkv_gn_b': nrm(ks[19], (L, RWKV_W), 0.02),
        'rwkv_vres_w1': nrm(ks[20], (L - 1, D, VRES_LORA), D ** -0.5),
        'rwkv_vres_w2': nrm(ks[21], (L - 1, VRES_LORA, RWKV_W), VRES_LORA ** -0.5),
        'rwkv_vres_b': nrm(ks[22], (L - 1, RWKV_W), 0.02),
        'hgrn_lb_logits': nrm(ks[23], (2, L, HGRN_W), 0.5),
        'hgrn_norm_w': 1.0 + nrm(ks[24], (L, HGRN_W), 0.02),
        'attn_sink': nrm(ks[25], (L, ATTN_Q_HEADS), 0.5),
        'final_norm_w': 1.0 + nrm(ks[26], (D,), 0.02),
    }


def reference(x, c, ctx, c_ctx, mod_w, mod_b, norm_w, w_in, w_out,
              rwkv_mu_prev, rwkv_mu_next, rwkv_w0, rwkv_w2, rwkv_a0, rwkv_a2,
              rwkv_k_k, rwkv_k_a, rwkv_r_k, rwkv_gn_w, rwkv_gn_b,
              rwkv_vres_w1, rwkv_vres_w2, rwkv_vres_b,
              hgrn_lb_logits, hgrn_norm_w, attn_sink, final_norm_w):
    B, T, _ = x.shape
    rows = T // GRID_W
    row = jnp.broadcast_to(jnp.arange(rows)[:, None], (rows, GRID_W)).reshape(-1)
    col = jnp.broadcast_to(jnp.arange(GRID_W)[None, :], (rows, GRID_W)).reshape(-1)
    lb_p = jax.nn.softmax(hgrn_lb_logits.astype(F32), axis=1)
    lower_bounds = jnp.cumsum(lb_p, axis=1) - lb_p[:, :1]
    silu_c = jax.nn.silu(c)
    silu_cc = jax.nn.silu(c_ctx)
    xc = ctx
    v_first = None
    vc_first = None
    for l in range(DEPTH):
        last = l == DEPTH - 1
        mod = silu_c @ mod_w[l] + mod_b[l]
        mod_c = silu_cc @ mod_w[l] + mod_b[l]
        shift, scale, gate = jnp.split(mod[:, None, :], 3, axis=-1)
        shift_c, scale_c, gate_c = jnp.split(mod_c, 3, axis=-1)
        h = rms_norm(x, norm_w[l]) * (1.0 + scale) + shift
        hc = rms_norm(xc, norm_w[l]) * (1.0 + scale_c) + shift_c
        pa, pb, pt = _split(h @ w_in[l], (RWKV_COLS, HGRN_COLS, ATTN_COLS))
        pa_c, pb_c, pt_c = _split(hc @ w_in[l], (RWKV_COLS, HGRN_COLS, ATTN_COLS))

        vres = None if l == 0 else (rwkv_vres_w1[l - 1], rwkv_vres_w2[l - 1], rwkv_vres_b[l - 1])
        rw = (rwkv_mu_prev[l], rwkv_mu_next[l], rwkv_w0[l], rwkv_w2[l], rwkv_a0[l], rwkv_a2[l], rwkv_k_k[l], rwkv_k_a[l])
        fa_c = rwkv_features(pa_c, hc, vc_first, *rw, vres)
        fa = rwkv_features(pa, h, v_first, *rw, vres)
        if l == 0:
            vc_first, v_first = fa_c[3], fa[3]
        s0 = jnp.zeros((2, B, RWKV_HEADS, HEAD_DIM, HEAD_DIM), F32)
        sa_c, ya_c = rwkv_mix(fa_c, s0, not last)
        _, ya = rwkv_mix(fa, sa_c, True)
        ya = rwkv_readout(ya, fa, rwkv_r_k[l], rwkv_gn_w[l], rwkv_gn_b[l])

        fb_c = hgrn_features(pb_c, lower_bounds[:, l])
        fb = hgrn_features(pb, lower_bounds[:, l])
        h0 = jnp.zeros((2, B, HGRN_HEADS, HEAD_DIM, HEAD_DIM), F32)
        sb_c, ob_c = hgrn_mix(fb_c, h0, not last)
        _, ob = hgrn_mix(fb, sb_c, True)
        yb = hgrn_readout(ob, fb[4], hgrn_norm_w[l])

        qa, ka, va, ga = _split(pt.astype(F32), ATTN_SPLITS)
        qa_c, ka_c, va_c, ga_c = _split(pt_c.astype(F32), ATTN_SPLITS)
        q = axial_rope(_heads(qa, ATTN_Q_HEADS), row, col)
        k = axial_rope(_heads(ka, ATTN_KV_HEADS), row, col)
        v = _heads(va, ATTN_KV_HEADS)
        k_c = _heads(ka_c, ATTN_KV_HEADS)
        v_c = _heads(va_c, ATTN_KV_HEADS)
        sink = attn_sink[l].astype(F32)
        yt = banded_attention(q, k, v, k_c, v_c, sink) * jax.nn.silu(ga)

        y = jnp.concatenate([ya, yb, yt], axis=-1).astype(x.dtype) @ w_out[l]
        if not last:
            ya_c = rwkv_readout(ya_c, fa_c, rwkv_r_k[l], rwkv_gn_w[l], rwkv_gn_b[l])
            yb_c = hgrn_readout(ob_c, fb_c[4], hgrn_norm_w[l])
            yt_c = context_attention(_heads(qa_c, ATTN_Q_HEADS), k_c, v_c, sink) * jax.nn.silu(ga_c)
            y_c = jnp.concatenate([ya_c, yb_c, yt_c], axis=-1).astype(xc.dtype) @ w_out[l]
            xc = xc + gate_c * y_c
        x = x + gate * y
    return rms_norm(x, final_norm_w)
```

```python
import contextlib
import math
import numpy as np
import concourse.bass as bass
import concourse.mybir as mybir
from concourse.bass_utils import run_bass_kernel_spmd

F32 = mybir.dt.float32
BF16 = mybir.dt.bfloat16
AF = mybir.ActivationFunctionType
ALU = mybir.AluOpType
AX = mybir.AxisListType

D = 1024
HD = 64
RW = 384
HW = 256
AW = 384
NCH = 39
NCOL = NCH * 128
GN_EPS = 64e-5
NORM_EPS = 1e-6
DECAY_K = -math.exp(-0.5)
ENGINES = ("pe", "act", "dve", "pool", "sp")


class Buf:
    def __init__(self, name, ap, dsem=None):
        self.name = name
        self.ap = ap
        self.st = {}
        self.dsem = dsem

    def __getitem__(self, idx):
        return self.ap[idx]


class DSem:
    def __init__(self, sem, key):
        self.sem = sem
        self.cnt = 0
        self.key = key


class Prog:
    def __init__(self, nc, arena_cols):
        self.nc = nc
        self.stack = contextlib.ExitStack()
        self.ops = {e: [] for e in ENGINES}
        self.cnt = {e: 0 for e in ENGINES}
        self.known = {e: {} for e in ENGINES}
        self.sems = {}
        for e in ("pe", "act", "dve", "pool"):
            self.sems[e] = self.stack.enter_context(nc.semaphore("s_" + e))
        self.dsems = {}
        self.arena = self.stack.enter_context(nc.sbuf_tensor("arena", [128, arena_cols], F32))
        self.arena_cols = arena_cols
        self.bump = 0
        self.nops = 0

    def dsem(self, name):
        if name not in self.dsems:
            s = self.stack.enter_context(self.nc.semaphore("d_" + name))
            ds = DSem(s, ("dma", name))
            self.dsems[name] = ds
            self.sems[ds.key] = s
        return self.dsems[name]

    def alloc(self, name, shape, dt=F32, dsem=None):
        n = int(np.prod(shape[1:]))
        cols = n if dt == F32 else (n + 1) // 2
        assert self.bump + cols <= self.arena_cols, (name, self.bump, cols, self.arena_cols)
        ap = self.arena[:][:, self.bump:self.bump + cols]
        self.bump += cols
        if dt != F32:
            ap = ap.bitcast(dt)
            if n % 2:
                ap = ap[:, 0:n]
        if shape[0] < 128:
            ap = ap[0:shape[0], :]
        if len(shape) == 3:
            ap = ap.rearrange("p (a b) -> p a b", b=shape[2])
        elif len(shape) == 4:
            ap = ap.rearrange("p (a b c) -> p a b c", b=shape[2], c=shape[3])
        return Buf(name, ap, self.dsem(dsem) if dsem else None)

    def ps(self, name, shape, dt=F32):
        h = self.stack.enter_context(self.nc.psum_tensor(name, list(shape), dt))
        return Buf(name, h[:])

    def dram(self, name, shape, dt=F32, kind="Internal", dsem=None):
        h = self.nc.dram_tensor(name, list(shape), dt, kind=kind)
        return Buf(name, h.ap(), self.dsem(dsem) if dsem else None)

    def _states(self, buf, key):
        if key is None:
            return list(buf.st.values())
        out = []
        if key in buf.st:
            out.append(buf.st[key])
        if None in buf.st:
            out.append(buf.st[None])
        return out

    def _getst(self, buf, key):
        if key not in buf.st:
            buf.st[key] = dict(w={}, r={})
        return buf.st[key]

    @staticmethod
    def _norm(lst):
        out = []
        for x in lst:
            if isinstance(x, Buf):
                out.append((x, None))
            elif hasattr(x, "ref"):
                r = x.ref()
                out.append((r, None) if isinstance(r, Buf) else r)
            else:
                a, k = x
                if hasattr(a, "ref"):
                    r = a.ref(k)
                    out.append((r, None) if isinstance(r, Buf) else r)
                else:
                    out.append((a, k))
        return out

    def op(self, eng, fn, reads=(), writes=(), dsem=None):
        reads = self._norm(reads)
        writes = self._norm(writes)
        need = {}

        def req(d):
            for k, v in d.items():
                if need.get(k, 0) < v:
                    need[k] = v

        for b, key in reads:
            for st in self._states(b, key):
                req(st["w"])
        for b, key in writes:
            for st in self._states(b, key):
                req(st["w"])
                req(st["r"])
        if dsem is not None:
            dsem.cnt += 16
            mykey, myval = dsem.key, dsem.cnt
            inc = (dsem.sem, 16)
        else:
            self.cnt[eng] += 1
            mykey, myval = eng, self.cnt[eng]
            inc = (self.sems[eng], 1)
            if eng == "pe":
                need.pop("pe", None)
        waits = []
        kn = self.known[eng]
        for k, v in need.items():
            if kn.get(k, 0) < v:
                kn[k] = v
                waits.append((self.sems[k], v))
        self.ops[eng].append((waits, fn, inc))
        self.nops += 1
        for b, key in reads:
            st = self._getst(b, key)
            if st["r"].get(mykey, 0) < myval:
                st["r"][mykey] = myval
        for b, key in writes:
            if key is None:
                b.st = {None: dict(w={mykey: myval}, r={})}
            else:
                st = self._getst(b, key)
                st["w"] = {mykey: myval}
                st["r"] = {}

    def dma(self, out_ap, in_ap, reads, writes, dsem, eng="sp"):
        self.op(eng, lambda e: e.dma_start(out=out_ap, in_=in_ap), reads, writes, dsem=dsem)

    def barrier(self, engines=ENGINES):
        need = {}
        for ds in self.dsems.values():
            if ds.cnt:
                need[ds.key] = ds.cnt
        for e in ("pe", "act", "dve", "pool"):
            if self.cnt[e]:
                need[e] = self.cnt[e]
        for eng in engines:
            waits = []
            kn = self.known[eng]
            for k, v in need.items():
                if k == eng:
                    continue
                if kn.get(k, 0) < v:
                    kn[k] = v
                    waits.append((self.sems[k], v))
            if waits:
                self.ops[eng].append((waits, None, None))

    def emit(self):
        nc = self.nc
        prog = self
        with nc.Block() as block:
            def run(engname, eobj):
                for waits, fn, inc in prog.ops[engname]:
                    for s, v in waits:
                        eobj.wait_ge(s, v)
                    if fn is not None:
                        fn(eobj).then_inc(inc[0], inc[1])

            @block.sync
            def _(e):
                run("sp", e)

            @block.tensor
            def _(e):
                run("pe", e)

            @block.scalar
            def _(e):
                run("act", e)

            @block.vector
            def _(e):
                run("dve", e)

            @block.gpsimd
            def _(e):
                run("pool", e)
        self.stack.close()


class Packer:
    def __init__(self):
        self.items = []
        self.off = {}
        self.n = 0

    def add(self, name, arr):
        arr = np.ascontiguousarray(arr, dtype=np.float32)
        assert arr.shape[0] == 128, (name, arr.shape)
        a2 = arr.reshape(128, -1)
        self.off[name] = (self.n, a2.shape[1], arr.shape[1:])
        self.items.append(a2)
        self.n += a2.shape[1]

    def build(self):
        return np.ascontiguousarray(np.concatenate(self.items, axis=1))


def pl(v, nch):
    return np.ascontiguousarray(np.asarray(v).reshape(nch, 128).T)


def col_index():
    idx = list(range(0, 1792))
    idx += list(range(1792, 3072))
    q0, k0, v0, g0 = 3072, 3456, 3584, 3712
    idx += list(range(q0, q0 + 384))
    idx += list(range(k0, k0 + 128))
    idx += list(range(v0, v0 + 128))
    idx += list(range(g0, g0 + 384))

    def sw(d):
        g, j = d // 32, d % 32
        return g * 32 + (1 - j // 16) * 16 + (j % 16)

    idx += [q0 + (c // 64) * 64 + sw(c % 64) for c in range(384)]
    idx += [k0 + (c // 64) * 64 + sw(c % 64) for c in range(128)]
    kb = [k0 + ((1 - c // 64) * 64) + (c % 64) for c in range(128)]
    idx += kb
    idx += [k0 + ((1 - c // 64) * 64) + sw(c % 64) for c in range(128)]
    return np.array(idx, dtype=np.int64)


def rope_tables(CTX, TL):
    NT = CTX + TL
    cos = np.ones((128, NT), np.float64)
    sins = np.zeros((128, NT), np.float64)
    t = np.arange(TL)
    row, col = t // 64, t % 64
    for p in range(128):
        d = p % 64
        g, j = d // 32, d % 32
        inv = 10000.0 ** (-(j % 16) / 16.0)
        pos = row if g == 0 else col
        ang = pos.astype(np.float32).astype(np.float64) * np.float32(inv).astype(np.float64)
        ang = (pos.astype(np.float32) * np.float32(inv)).astype(np.float64)
        cos[p, CTX:] = np.cos(ang)
        s = np.sin(ang)
        sins[p, CTX:] = -s if (j // 16) == 0 else s
    return np.stack([cos, sins], axis=1).astype(np.float32)


def build_consts():
    pk = Packer()
    pk.add("ident", np.eye(128))
    p = np.arange(128)[:, None]
    c = np.arange(128)[None, :]
    bo = (p // 64 == c // 64).astype(np.float32)
    pk.add("bo1", bo)
    pk.add("bo64", bo / 64.0)
    pk.add("sel0", np.broadcast_to((p // 64 == 0) / 64.0, (128, 128)))
    pk.add("sel1", np.broadcast_to((p // 64 == 1) / 64.0, (128, 128)))
    pk.add("ones", np.ones((128, 128)))
    pk.add("ident2", (np.arange(128)[:, None] % 64 == np.arange(64)[None, :]).astype(np.float32))
    s = (np.arange(128) % 64)[:, None]
    t = np.arange(64)[None, :]
    m = np.concatenate([s < t, s <= t, s < t, s <= t, t < s], axis=1).astype(np.float32)
    pk.add("mrw", m)
    pk.add("mhg", (s <= np.arange(32)[None, :]).astype(np.float32))
    kk = np.arange(128)[:, None]
    qq = np.arange(128)[None, :]
    pk.add("mprev", (kk >= qq).astype(np.float32))
    pk.add("mnext", (kk <= qq).astype(np.float32))
    tt = np.arange(512)[None, :]
    pk.add("rs64", np.broadcast_to((tt % 64 != 0).astype(np.float32), (128, 512)))
    pk.add("rs32", np.broadcast_to((tt % 32 != 0).astype(np.float32), (128, 512)))
    return pk


def build_params(inp, b, L):
    pk = Packer()
    cc = np.stack([pl(inp["c"][b], 8), pl(inp["c_ctx"], 8)], axis=2)
    pk.add("c", cc)
    pk.add("mod_b", np.stack([pl(inp["mod_b"][l], 24) for l in range(L)], axis=1))
    pk.add("norm_w", np.stack([pl(inp["norm_w"][l], 8) for l in range(L)], axis=1))
    pk.add("fnw", pl(inp["final_norm_w"], 8))
    pk.add("mu_p", np.stack([pl(inp["rwkv_mu_prev"][l], 14) for l in range(L)], axis=1))
    pk.add("mu_n", np.stack([pl(inp["rwkv_mu_next"][l], 14) for l in range(L)], axis=1))
    for nm, key in (("w0", "rwkv_w0"), ("a0", "rwkv_a0")):
        pk.add(nm, np.stack([np.stack([pl(inp[key][l, z], 3) for z in range(2)], axis=1) for l in range(L)], axis=1))
    for nm, key in (("k_k", "rwkv_k_k"), ("k_a", "rwkv_k_a"), ("gn_w", "rwkv_gn_w"), ("gn_b", "rwkv_gn_b")):
        pk.add(nm, np.stack([pl(inp[key][l], 3) for l in range(L)], axis=1))
    pk.add("r_k", np.stack([pl(inp["rwkv_r_k"][l].reshape(-1), 3) for l in range(L)], axis=1))
    pk.add("vres_b", pl(inp["rwkv_vres_b"][0], 3))
    lbl = np.stack([np.stack([pl(inp["hgrn_lb_logits"][z, l], 2) for l in range(L)], axis=1) for z in range(2)], axis=1)
    pk.add("lbl", lbl)
    pk.add("hnw", np.stack([pl(inp["hgrn_norm_w"][l], 2) for l in range(L)], axis=1))
    pk.add("sink", np.broadcast_to(np.asarray(inp["attn_sink"])[None, :, :], (128, L, 6)))
    pk.add("w2", np.stack([np.asarray(inp["rwkv_w2"][l]).reshape(128, 384) for l in range(L)], axis=1))
    pk.add("a2", np.stack([np.asarray(inp["rwkv_a2"][l]).reshape(128, 384) for l in range(L)], axis=1))
    vw2 = np.zeros((128, 384), np.float32)
    vw2[0:32] = np.asarray(inp["rwkv_vres_w2"][0])
    pk.add("vw2", vw2)
    return pk


class StopBuild(Exception):
    pass


def build_program(CTX, TL, L, cst_pk, prm_pk, dbg=(), upto=None):
    NT = CTX + TL
    assert TL % 512 == 0 and CTX % 128 == 0 and CTX <= 512
    groups = [(0, CTX)] + [(CTX + 512 * j, 512) for j in range(TL // 512)]
    NG = len(groups)
    nc = bass.Bass("TRN2", target_bir_lowering=False)
    P = Prog(nc, 46000)

    xin = P.dram("xin", [NT, D], F32, kind="ExternalInput")
    prm_d = P.dram("prm", [128, prm_pk.n], F32, kind="ExternalInput")
    cst_d = P.dram("cst", [128, cst_pk.n], F32, kind="ExternalInput")
    rope_d = P.dram("rope", [128, 2, NT], F32, kind="ExternalInput")
    wext_d = P.dram("wext", [L, D, NCOL], F32, kind="ExternalInput")
    modw_d = P.dram("modw", [L, D, 3 * D], F32, kind="ExternalInput")
    wout_d = P.dram("wout", [L, D, D], F32, kind="ExternalInput")
    out_d = P.dram("out", [TL, D], F32, kind="ExternalOutput", dsem="out")
    raw_d = P.dram("raw", [NCOL, NT], F32, dsem="raw")
    feat_d = [P.dram("feat%d" % l, [9 * RW, NT], F32, dsem="feat") for l in range(L)]
    yf_d = P.dram("yf", [RW, NT], F32, dsem="yf")
    of_d = P.dram("of", [HW, NT], F32, dsem="yf")
    yt_d = P.dram("yt", [D, NT], BF16, dsem="yt")
    x1_d = P.dram("x1", [NT, D], F32, dsem="x1")
    dbg_d = {}

    cst = P.alloc("cst", [128, cst_pk.n], dsem="cst")
    prm = P.alloc("prm", [128, prm_pk.n], dsem="prm")
    P.dma(cst.ap, cst_d.ap, [], [cst], cst.dsem)
    P.dma(prm.ap, prm_d.ap, [], [prm], prm.dsem)

    def C(name):
        o, n, shp = cst_pk.off[name]
        ap = cst.ap[:, o:o + n]
        return ap

    def PR(name):
        o, n, shp = prm_pk.off[name]
        ap = prm.ap[:, o:o + n]
        if len(shp) == 2:
            ap = ap.rearrange("p (a b) -> p a b", b=shp[1])
        elif len(shp) == 3:
            ap = ap.rearrange("p (a b c) -> p a b c", b=shp[1], c=shp[2])
        return ap

    ident = C("ident")
    bo1, bo64 = C("bo1"), C("bo64")
    ones_f = C("ones")
    modT = P.alloc("modT", [128, 24, 2])
    g1 = P.alloc("g1", [128, 8, 2])
    c0 = P.alloc("c0", [128, 14])
    lb = P.alloc("lb", [128, 2, 2])
    oml = P.alloc("oml", [128, 2, 2])
    ones_b = P.alloc("ones_b", [128, 64], BF16)
    P.op("pool", lambda e: e.memset(ones_b.ap, 1.0), [], [ones_b])
    pers_mark = P.bump

    PSW = [P.ps("psw%d" % i, [128, 1024]) for i in range(3)]
    PSB2 = [P.ps("psb%d" % i, [128, 512]) for i in range(2)]

    class Bank:
        def __init__(self, buf, key, ap):
            self.buf, self.key, self.ap = buf, key, ap

        def ref(self, sub=None):
            if self.key is None:
                return self.buf
            return (self.buf, self.key)

    PSB = [Bank(PSW[2], 0, PSW[2].ap[:, 0:512]), Bank(PSW[2], 1, PSW[2].ap[:, 512:1024]),
           Bank(PSB2[0], None, PSB2[0].ap), Bank(PSB2[1], None, PSB2[1].ap)]

    def mm(out, lhsT, rhs, rd, wr, start=True, stop=True):
        P.op("pe", lambda e: e.matmul(out, lhsT=lhsT, rhs=rhs, start=start, stop=stop), rd, wr)

    def tr(out, in_, idn, rd, wr):
        P.op("pe", lambda e: e.transpose(out, in_, idn), rd, wr)

    def act(out, in_, func, rd, wr, bias=None, scale=None, accum=None):
        kw = {}
        if bias is not None:
            kw["bias"] = bias
        if scale is not None:
            kw["scale"] = scale
        if accum is not None:
            kw["accum_out"] = accum
        P.op("act", lambda e: e.activation(out=out, in_=in_, func=func, **kw), rd, wr)

    def tt(eng, out, in0, in1, op, rd, wr):
        P.op(eng, lambda e: e.tensor_tensor(out=out, in0=in0, in1=in1, op=op), rd, wr)

    def ts(eng, out, in0, s1, s2, op0, op1, rd, wr):
        if s2 is None:
            P.op(eng, lambda e: e.tensor_scalar(out=out, in0=in0, scalar1=s1, scalar2=None, op0=op0), rd, wr)
        else:
            P.op(eng, lambda e: e.tensor_scalar(out=out, in0=in0, scalar1=s1, scalar2=s2, op0=op0, op1=op1), rd, wr)

    def stt(out, in0, scalar, in1, op0, op1, rd, wr):
        P.op("dve", lambda e: e.scalar_tensor_tensor(out=out, in0=in0, scalar=scalar, in1=in1, op0=op0, op1=op1), rd, wr)

    def cp(eng, out, in_, rd, wr):
        if eng == "act":
            act(out, in_, AF.Copy, rd, wr)
        else:
            P.op(eng, lambda e: e.tensor_copy(out=out, in_=in_), rd, wr)

    def recip(out, in_, rd, wr):
        P.op("dve", lambda e: e.reciprocal(out=out, in_=in_), rd, wr)

    rr = {"i": 0}

    def rot(engs=("act", "dve")):
        rr["i"] += 1
        return engs[rr["i"] % len(engs)]

    def ckpt(name):
        if upto == name:
            raise StopBuild()

    def layer(l):
        last = (l == L - 1)
        P.barrier()
        P.bump = pers_mark
        sc = P.alloc("sc", [128, 8, 2])
        act(sc.ap, PR("c"), AF.Silu, [prm], [sc])
        mws = [P.alloc("mws%d" % i, [128, 8, 256], dsem="mws%d" % i) for i in range(2)]
        psm = PSB[2]
        for q in range(12):
            st = mws[q % 2]
            src = modw_d.ap[l].rearrange("(kc p) n -> p kc n", p=128)[:, :, q * 256:(q + 1) * 256]
            P.dma(st.ap, src, [], [st], st.dsem)
            for jj in range(2):
                jc = q * 2 + jj
                for kc in range(8):
                    mm(psm.ap[:, jc * 2:jc * 2 + 2], st.ap[:, kc, jj * 128:(jj + 1) * 128], sc.ap[:, kc, :],
                       [st, sc], [(psm, jc)], start=(kc == 0), stop=(kc == 7))
        tt("dve", modT.ap, psm.ap[:, 0:48].rearrange("p (a b) -> p a b", b=2),
           PR("mod_b")[:, l, :].unsqueeze(2).to_broadcast([128, 24, 2]), ALU.add, [psm, prm], [modT])
        ts("dve", g1.ap, modT.ap[:, 8:16, :], 1.0, None, ALU.add, None, [modT], [g1])
        tt("dve", g1.ap, g1.ap, PR("norm_w")[:, l, :].unsqueeze(2).to_broadcast([128, 8, 2]), ALU.mult, [g1, prm], [g1])
        tt("dve", c0.ap, PR("mu_p")[:, l, :], PR("mu_n")[:, l, :], ALU.add, [prm], [c0])
        ts("dve", c0.ap, c0.ap, -1.0, 1.0, ALU.mult, ALU.add, [c0], [c0])
        if l == 0:
            P.op("dve", lambda e: e.memset(lb.ap, 0.0), [], [lb])
        else:
            tt("dve", lb.ap, PR("lbl")[:, :, 1, :], PR("lbl")[:, :, 0, :], ALU.subtract, [prm], [lb])
            act(lb.ap, lb.ap, AF.Sigmoid, [lb], [lb])
        ts("dve", oml.ap, lb.ap, -1.0, 1.0, ALU.mult, ALU.add, [lb], [oml])
        shiftv = modT.ap[:, 0:8, :]
        gatev = modT.ap[:, 16:24, :]

        ckpt('M%d' % l)
        P.barrier()
        P.bump = pers_mark
        wbf = P.alloc("wbf", [128, 8, NCOL], BF16)
        wst = [P.alloc("wst%d" % i, [128, 8, 256], dsem="wst%d" % i) for i in range(2)]
        npiece = NCOL // 256 + (1 if NCOL % 256 else 0)
        for q in range(npiece):
            st = wst[q % 2]
            c_lo = q * 256
            w = min(256, NCOL - c_lo)
            src = wext_d.ap[l].rearrange("(kc p) n -> p kc n", p=128)[:, :, c_lo:c_lo + w]
            P.dma(st.ap[:, :, 0:w], src, [], [st], st.dsem)
            cp(rot(("act", "dve", "pool")), wbf.ap[:, :, c_lo:c_lo + w], st.ap[:, :, 0:w], [st], [(wbf, q)])
        a_mark = P.bump
        ckpt('W%d' % l)

        xsrc = xin if l == 0 else x1_d
        xt = [P.alloc("xt%d" % i, [128, D], dsem="xt%d" % i) for i in range(2)]
        junk = P.alloc("junk", [128, D])
        xn = [P.alloc("xn%d" % i, [128, D]) for i in range(2)]
        st_ss = [P.alloc("ss%d" % i, [128, 2]) for i in range(2)]
        hT = [P.alloc("hT%d" % i, [128, 8, 512], BF16) for i in range(2)]
        evs = [P.alloc("evs%d" % i, [128, 512], dsem="evs%d" % i) for i in range(4)]
        ti = 0
        ei = 0
        for gi, (t0, N) in enumerate(groups):
            w = 1 if gi == 0 else 0
            hg = hT[gi % 2]
            for s in range(N // 128):
                xb, xnb, ssb = xt[ti % 2], xn[ti % 2], st_ss[ti % 2]
                psT = PSW[ti % 2]
                ti += 1
                tok = t0 + s * 128
                P.dma(xb.ap, xsrc.ap[tok:tok + 128, :], [], [xb], xb.dsem)
                act(junk.ap, xb.ap, AF.Square, [xb], [junk, (ssb, 0)], accum=ssb.ap[:, 0:1])
                ts("dve", ssb.ap[:, 1:2], ssb.ap[:, 0:1], 1.0 / D, NORM_EPS, ALU.mult, ALU.add, [(ssb, 0)], [(ssb, 1)])
                act(ssb.ap[:, 1:2], ssb.ap[:, 1:2], AF.Sqrt, [(ssb, 1)], [(ssb, 1)])
                recip(ssb.ap[:, 1:2], ssb.ap[:, 1:2], [(ssb, 1)], [(ssb, 1)])
                ts("dve", xnb.ap, xb.ap, ssb.ap[:, 1:2], None, ALU.mult, None, [xb, (ssb, 1)], [xnb])
                for j in range(8):
                    tr(psT.ap[:, j * 128:(j + 1) * 128], xnb.ap[:, j * 128:(j + 1) * 128], ident, [xnb, cst], [(psT, j // 4)])
                for j in range(8):
                    o = hg.ap[:, j, s * 128:(s + 1) * 128]
                    i_ = psT.ap[:, j * 128:(j + 1) * 128]
                    if j % 2 == 0:
                        act(o, i_, AF.Identity, [(psT, j // 4), g1, modT], [(hg, s)], scale=g1.ap[:, j, w:w + 1], bias=shiftv[:, j, w:w + 1])
                    else:
                        ts("dve", o, i_, g1.ap[:, j, w:w + 1], shiftv[:, j, w:w + 1], ALU.mult, ALU.add, [(psT, j // 4), g1, modT], [(hg, s)])
            nch = NCH if l > 0 else NCH - 1
            for c in range(nch):
                pb = PSB[c % 4]
                for kc in range(8):
                    mm(pb.ap[:, 0:N], wbf.ap[:, kc, c * 128:(c + 1) * 128], hg.ap[:, kc, 0:N], [wbf, hg], [pb],
                       start=(kc == 0), stop=(kc == 7))
                ev = evs[ei % 4]
                ei += 1
                cp(rot(), ev.ap[:, 0:N], pb.ap[:, 0:N], [pb], [ev])
                P.dma(raw_d.ap[c * 128:(c + 1) * 128, t0:t0 + N], ev.ap[:, 0:N], [ev], [], ev.dsem)

        ckpt('A%d' % l)
        P.barrier()
        P.bump = pers_mark
        FT = feat_d[l]

        def frow(arr, j):
            return arr * RW + j * 128

        RH = [P.alloc("rh%d" % i, [128, 514], dsem="rh%d" % i) for i in range(4)]
        SH = [P.alloc("sh%d" % i, [128, 512], dsem="sh%d" % i) for i in range(14)]
        tmpA = [P.alloc("tmpA%d" % i, [128, 512], dsem="tmpA%d" % i) for i in range(4)]
        vf = [P.alloc("vf%d" % i, [128, 512], dsem="vf%d" % i) for i in range(2)]
        hv = P.alloc("hv", [32, 512], dsem="hv")
        w2v, a2v = PR("w2"), PR("a2")
        hi = 0
        tai = 0
        for gi, (t0, N) in enumerate(groups):
            seq_lo, seq_hi = (0, CTX) if gi == 0 else (CTX, NT)
            for c in range(14):
                rh = RH[hi % 4]
                hi += 1
                lo = max(t0 - 1, seq_lo)
                hi_ = min(t0 + N + 1, seq_hi)
                P.dma(rh.ap[:, (lo - (t0 - 1)):(hi_ - (t0 - 1))], raw_d.ap[c * 128:(c + 1) * 128, lo:hi_], [], [rh], rh.dsem)
                if lo != t0 - 1:
                    P.op("pool", lambda e, rh=rh: e.memset(rh.ap[:, 0:1], 0.0), [], [rh])
                if hi_ != t0 + N + 1:
                    P.op("pool", lambda e, rh=rh, N=N: e.memset(rh.ap[:, N + 1:N + 2], 0.0), [], [rh])
                sh = SH[c]
                act(sh.ap[:, 0:N], rh.ap[:, 1:N + 1], AF.Identity, [rh, c0], [sh], scale=c0.ap[:, c:c + 1])
                stt(sh.ap[:, 0:N], rh.ap[:, 0:N], PR("mu_p")[:, l, c:c + 1], sh.ap[:, 0:N], ALU.mult, ALU.add, [rh, sh, prm], [sh])
                stt(sh.ap[:, 0:N], rh.ap[:, 2:N + 2], PR("mu_n")[:, l, c:c + 1], sh.ap[:, 0:N], ALU.mult, ALU.add, [rh, sh, prm], [sh])
            act(SH[9].ap[:, 0:N], SH[9].ap[:, 0:N], AF.Tanh, [SH[9]], [SH[9]])
            for (src, wv, b0, arr0, is_w) in ((SH[9], w2v, PR("w0"), 5, True), (SH[10], a2v, PR("a0"), 7, False)):
                for z in range(2):
                    for j in range(3):
                        pb = PSB[(z * 3 + j) % 4]
                        mm(pb.ap[:, 0:N], wv[z * 64:(z + 1) * 64, l, j * 128:(j + 1) * 128], src.ap[z * 64:(z + 1) * 64, 0:N],
                           [prm, src], [pb])
                        ta = tmpA[tai % 4]
                        tai += 1
                        act(ta.ap[:, 0:N], pb.ap[:, 0:N], AF.Sigmoid, [pb, prm], [ta], bias=b0[:, l, z, j:j + 1])
                        if is_w:
                            ts("dve", ta.ap[:, 0:N], ta.ap[:, 0:N], DECAY_K, None, ALU.mult, None, [ta], [ta])
                        r0 = frow(arr0 + z, j)
                        P.dma(FT.ap[r0:r0 + 128, t0:t0 + N], ta.ap[:, 0:N], [ta], [], ta.dsem)
            if l > 0:
                P.dma(hv.ap[:, 0:N], raw_d.ap[38 * 128:38 * 128 + 32, t0:t0 + N], [], [hv], hv.dsem)
                for j in range(3):
                    pb = PSB[j % 4]
                    mm(pb.ap[:, 0:N], PR("vw2")[0:32, j * 128:(j + 1) * 128], hv.ap[:, 0:N], [prm, hv], [pb])
                    ta = tmpA[tai % 4]
                    tai += 1
                    act(ta.ap[:, 0:N], pb.ap[:, 0:N], AF.Sigmoid, [pb, prm], [ta], bias=PR("vres_b")[:, j:j + 1])
                    v1 = vf[j % 2]
                    r0 = frow(2, j)
                    P.dma(v1.ap[:, 0:N], feat_d[0].ap[r0:r0 + 128, t0:t0 + N], [], [v1], v1.dsem)
                    vv = SH[6 + j]
                    tt("dve", v1.ap[:, 0:N], v1.ap[:, 0:N], vv.ap[:, 0:N], ALU.subtract, [v1, vv], [v1])
                    tt("dve", v1.ap[:, 0:N], v1.ap[:, 0:N], ta.ap[:, 0:N], ALU.mult, [v1, ta], [v1])
                    tt("dve", vv.ap[:, 0:N], vv.ap[:, 0:N], v1.ap[:, 0:N], ALU.add, [vv, v1], [vv])
            for j in range(3):
                ta = tmpA[tai % 4]
                tb = tmpA[(tai + 1) % 4]
                tai += 2
                pb = PSB[(j + 2) % 4]
                ts("dve", ta.ap[:, 0:N], SH[3 + j].ap[:, 0:N], PR("k_k")[:, l, j:j + 1], None, ALU.mult, None, [SH[3 + j], prm], [ta])
                tt("dve", tb.ap[:, 0:N], ta.ap[:, 0:N], ta.ap[:, 0:N], ALU.mult, [ta], [tb])
                mm(pb.ap[:, 0:N], bo1, tb.ap[:, 0:N], [cst, tb], [pb])
                act(tb.ap[:, 0:N], pb.ap[:, 0:N], AF.Sqrt, [pb], [tb])
                ts("dve", tb.ap[:, 0:N], tb.ap[:, 0:N], 1e-12, None, ALU.max, None, [tb], [tb])
                recip(tb.ap[:, 0:N], tb.ap[:, 0:N], [tb], [tb])
                tt("dve", ta.ap[:, 0:N], ta.ap[:, 0:N], tb.ap[:, 0:N], ALU.mult, [ta, tb], [ta])
                r0 = frow(3, j)
                P.dma(FT.ap[r0:r0 + 128, t0:t0 + N], ta.ap[:, 0:N], [ta], [], ta.dsem)
            for (arr, c_lo) in ((0, 0), (1, 3), (2, 6), (4, 11)):
                for j in range(3):
                    r0 = frow(arr, j)
                    sh = SH[c_lo + j]
                    P.dma(FT.ap[r0:r0 + 128, t0:t0 + N], sh.ap[:, 0:N], [sh], [], sh.dsem)

        ckpt('B1%d' % l)
        def scan_pass(kind, z):
            rw = kind == "rw"
            Cn = 64 if rw else 32
            nhp = 3 if rw else 2
            yfd = yf_d if rw else of_d
            rs = C("rs64") if rw else C("rs32")
            mrw, mhg = C("mrw"), C("mhg")
            P.barrier()
            P.bump = pers_mark
            WS = []
            for hp in range(nhp):
                d = {}
                names = ["r", "k", "v", "vs", "kk", "a", "lw", "Lc", "E", "T1", "T2", "bt", "kt", "bh", "kh", "gg", "yfl", "yo"] if rw else \
                        ["r", "k", "v", "vs", "lw", "Lc", "E", "T1", "kt", "kh", "gg", "yfl", "yo"]
                for nm in names:
                    d[nm] = P.alloc("%s%d" % (nm, hp), [128, 512], dsem=("ws_%s%d" % (nm, hp)) if nm in ("r", "k", "v", "kk", "a", "lw", "gg", "yfl", "yo") else None)
                d["AR"] = P.alloc("AR%d" % hp, [128, 8, 128] if rw else [128, 16, 32])
                d["Pc"] = P.alloc("Pc%d" % hp, [128, 16])
                d["A"] = [P.alloc("A%d_%d" % (hp, i), [128, 64]) for i in range(2)]
                d["Z"] = P.alloc("Z%d" % hp, [128, 64])
                d["U"] = P.alloc("U%d" % hp, [128, 64])
                d["VBK"] = [P.alloc("VBK%d_%d" % (hp, i), [128, 192]) for i in range(2)]
                d["G"] = [P.alloc("G%d_%d" % (hp, i), [128, 320]) for i in range(2)]
                d["X"] = [P.alloc("X%d_%d" % (hp, i), [128, 64]) for i in range(2)]
                d["XT"] = [P.alloc("XT%d_%d" % (hp, i), [128, 64]) for i in range(2)]
                d["TT"] = [P.alloc("TT%d_%d" % (hp, i), [128, 64]) for i in range(2)]
                d["ai"] = 0
                P.op("pool", lambda e, d=d: e.memset(d["A"][0].ap, 0.0), [], [d["A"][0]])
                for i_ in range(2):
                    P.op("pool", lambda e, d=d, i_=i_: e.memset(d["VBK"][i_].ap, 0.0), [], [d["VBK"][i_]])
                    P.op("pool", lambda e, d=d, i_=i_: e.memset(d["G"][i_].ap, 0.0), [], [d["G"][i_]])
                WS.append(d)
            psS = [PSW[0], PSW[1], PSW[2]]
            def U_(pS, u, n=1):
                return pS.ap[:, u * 64:(u + n) * 64]

            order = list(range(NG)) if z == 0 else [0] + list(range(NG - 1, 0, -1))
            for gi in order:
                t0, N = groups[gi]
                nck = N // Cn
                want_y = not (last and gi == 0)

                def V(ap, N=N):
                    return ap[:, 0:N] if z == 0 else ap[:, 0:N][:, ::-1]

                def c3(ap, N=N):
                    return ap[:, 0:N].rearrange("p (c t) -> p c t", t=Cn)

                for hp in range(nhp):
                    d = WS[hp]
                    if rw:
                        srcs = (("r", 0), ("k", 1), ("v", 2), ("kk", 3), ("a", 7 + z), ("lw", 5 + z))
                        for nm, arr in srcs:
                            r0 = frow(arr, hp)
                            P.dma(d[nm].ap[:, 0:N], FT.ap[r0:r0 + 128, t0:t0 + N], [], [d[nm]], d[nm].dsem)
                    else:
                        for nm, ch in (("r", 14), ("lw", 16 + 2 * z), ("v", 20)):
                            r0 = (ch + hp) * 128
                            P.dma(d[nm].ap[:, 0:N], raw_d.ap[r0:r0 + 128, t0:t0 + N], [], [d[nm]], d[nm].dsem)
                        act(d["lw"].ap[:, 0:N], d["lw"].ap[:, 0:N], AF.Sigmoid, [d["lw"]], [d["lw"]])
                        ts("dve", d["lw"].ap[:, 0:N], d["lw"].ap[:, 0:N], oml.ap[:, z, hp:hp + 1], lb.ap[:, z, hp:hp + 1], ALU.mult, ALU.add,
                           [d["lw"], oml, lb], [d["lw"]])
                        ts("dve", d["k"].ap[:, 0:N], d["lw"].ap[:, 0:N], -1.0, 1.0, ALU.mult, ALU.add, [d["lw"]], [d["k"]])
                        act(d["lw"].ap[:, 0:N], d["lw"].ap[:, 0:N], AF.Ln, [d["lw"]], [d["lw"]])
                    if z == 1 and want_y:
                        r0 = hp * 128
                        P.dma(d["yfl"].ap[:, 0:N], yfd.ap[r0:r0 + 128, t0:t0 + N], [], [d["yfl"]], d["yfl"].dsem)
                        if rw:
                            r0 = frow(4, hp)
                            P.dma(d["gg"].ap[:, 0:N], FT.ap[r0:r0 + 128, t0:t0 + N], [], [d["gg"]], d["gg"].dsem)
                        else:
                            r0 = (22 + hp) * 128
                            P.dma(d["gg"].ap[:, 0:N], raw_d.ap[r0:r0 + 128, t0:t0 + N], [], [d["gg"]], d["gg"].dsem)
                    Lc, E, T1 = d["Lc"], d["E"], d["T1"]
                    cp("pool" if z == 0 else "dve", d["vs"].ap[:, 0:N], V(d["v"].ap), [d["v"]], [d["vs"]])
                    P.op("dve", lambda e, Lc=Lc, d=d, N=N, V=V: e.tensor_tensor_scan(out=Lc.ap[:, 0:N], data0=rs[:, 0:N], data1=V(d["lw"].ap),
                                                                                     initial=0.0, op0=ALU.mult, op1=ALU.add), [d["lw"], cst], [Lc])
                    AR = d["AR"]
                    bc = c3(Lc.ap)[:, :, Cn - 1:Cn].to_broadcast([128, nck, Cn])
                    if rw:
                        T2 = d["T2"]
                        act(E.ap[:, 0:N], Lc.ap[:, 0:N], AF.Exp, [Lc], [E])
                        tt("dve", AR.ap[:, 0:nck, 64:128], c3(V(d["r"].ap)), c3(E.ap), ALU.mult, [d["r"], E], [AR])
                        tt("dve", T1.ap[:, 0:N], Lc.ap[:, 0:N], V(d["lw"].ap), ALU.subtract, [Lc, d["lw"]], [T1])
                        act(T1.ap[:, 0:N], T1.ap[:, 0:N], AF.Exp, [T1], [T1])
                        stt(AR.ap[:, 0:nck, 0:64], c3(V(d["kk"].ap)), -1.0, c3(T1.ap), ALU.mult, ALU.mult, [d["kk"], T1], [AR])
                        ts("dve", T1.ap[:, 0:N], V(d["a"].ap), -1.0, PR("k_a")[:, l, hp:hp + 1], ALU.add, ALU.mult, [d["a"], prm], [T1])
                        stt(T1.ap[:, 0:N], T1.ap[:, 0:N], 1.0, V(d["k"].ap), ALU.add, ALU.mult, [T1, d["k"]], [T1])
                        tt("dve", T2.ap[:, 0:N], V(d["kk"].ap), V(d["a"].ap), ALU.mult, [d["kk"], d["a"]], [T2])
                        act(E.ap[:, 0:N], Lc.ap[:, 0:N], AF.Exp, [Lc], [E], scale=-1.0)
                        tt("dve", d["bt"].ap[:, 0:N], T2.ap[:, 0:N], E.ap[:, 0:N], ALU.mult, [T2, E], [d["bt"]])
                        tt("dve", d["kt"].ap[:, 0:N], T1.ap[:, 0:N], E.ap[:, 0:N], ALU.mult, [T1, E], [d["kt"]])
                        tt("dve", c3(E.ap), bc, c3(Lc.ap), ALU.subtract, [Lc], [E])
                        act(E.ap[:, 0:N], E.ap[:, 0:N], AF.Exp, [E], [E])
                        tt("dve", d["bh"].ap[:, 0:N], T2.ap[:, 0:N], E.ap[:, 0:N], ALU.mult, [T2, E], [d["bh"]])
                        tt("dve", d["kh"].ap[:, 0:N], T1.ap[:, 0:N], E.ap[:, 0:N], ALU.mult, [T1, E], [d["kh"]])
                    else:
                        act(E.ap[:, 0:N], Lc.ap[:, 0:N], AF.Exp, [Lc], [E])
                        tt("dve", AR.ap[:, 0:nck, :], c3(V(d["r"].ap)), c3(E.ap), ALU.mult, [d["r"], E], [AR])
                        act(E.ap[:, 0:N], Lc.ap[:, 0:N], AF.Exp, [Lc], [E], scale=-1.0)
                        tt("dve", d["kt"].ap[:, 0:N], V(d["k"].ap), E.ap[:, 0:N], ALU.mult, [d["k"], E], [d["kt"]])
                        tt("dve", c3(E.ap), bc, c3(Lc.ap), ALU.subtract, [Lc], [E])
                        act(E.ap[:, 0:N], E.ap[:, 0:N], AF.Exp, [E], [E])
                        tt("dve", d["kh"].ap[:, 0:N], V(d["k"].ap), E.ap[:, 0:N], ALU.mult, [d["k"], E], [d["kh"]])
                    act(d["Pc"].ap[:, 0:nck], c3(Lc.ap)[:, :, Cn - 1], AF.Exp, [Lc], [d["Pc"]])

                ckpt('s_prep')
                for ck in range(nck):
                    cs = slice(ck * Cn, (ck + 1) * Cn)
                    yu = 14 + (ck % 2)
                    for hp in range(nhp):
                        d = WS[hp]
                        pS = psS[hp]
                        vbk = d["VBK"][ck % 2]
                        G = d["G"][ck % 2]
                        tl = (d["vs"], d["bh"], d["kh"]) if rw else (d["vs"], d["kh"])
                        nt_ = len(tl)
                        for e_ in range(2):
                            ps_ = slice(e_ * 64, e_ * 64 + 64)
                            po_ = slice(e_ * 64, e_ * 64 + Cn)
                            for ti_, sb_ in enumerate(tl):
                                mm(pS.ap[po_, ti_ * 64:(ti_ + 1) * 64], sb_.ap[ps_, cs], ident[ps_, ps_], [sb_, cst], [(pS, 0)])
                        ckpt('s_a1')
                        for e_ in range(2):
                            po_ = slice(e_ * 64, e_ * 64 + Cn)
                            cp("dve", vbk.ap[po_, 0:nt_ * 64], pS.ap[po_, 0:nt_ * 64], [(pS, 0)], [(vbk, e_)])
                        ckpt('s_a2')
                        for e_ in range(2):
                            ps_ = slice(e_ * 64, e_ * 64 + 64)
                            po_ = slice(e_ * 64, e_ * 64 + Cn)
                            if rw:
                                arv = d["AR"].ap[ps_, ck, :]
                                mm(pS.ap[po_, 192:320], d["bt"].ap[ps_, cs], arv, [d["bt"], d["AR"]], [(pS, 0)])
                                mm(pS.ap[po_, 320:448], d["kt"].ap[ps_, cs], arv, [d["kt"], d["AR"]], [(pS, 0)])
                                mm(pS.ap[po_, 448:512], d["AR"].ap[ps_, ck, 0:64], d["bt"].ap[ps_, cs], [d["bt"], d["AR"]], [(pS, 0)])
                            else:
                                mm(pS.ap[po_, 192:192 + Cn], d["kt"].ap[ps_, cs], d["AR"].ap[ps_, ck, :], [d["kt"], d["AR"]], [(pS, 0)])
                        ckpt('s_a3')
                        if rw:
                            tt("dve", G.ap[:, 0:320], pS.ap[:, 192:512], mrw, ALU.mult, [(pS, 0), cst], [G])
                        else:
                            for e_ in range(2):
                                po_ = slice(e_ * 64, e_ * 64 + Cn)
                                tt("dve", G.ap[po_, 0:Cn], pS.ap[po_, 192:192 + Cn], mhg[po_, :], ALU.mult, [(pS, 0), cst], [(G, e_)])
                    ckpt('s_a')
                    if rw:
                        Xc = [None] * nhp
                        XTc = [None] * nhp
                        for hp in range(nhp):
                            d = WS[hp]
                            G = d["G"][ck % 2]
                            tt("dve", d["TT"][0].ap, G.ap[:, 0:64], C("ident2"), ALU.add, [G, cst], [d["TT"][0]])
                            Xc[hp] = (G.ap[:, 256:320], G)
                            XTc[hp] = (G.ap[:, 0:64], G)
                        for lev in range(1, 6):
                            for hp in range(nhp):
                                pS = psS[hp]
                                Xa, Xb = Xc[hp]
                                XTa, XTb = XTc[hp]
                                for e_ in range(2):
                                    ps_ = slice(e_ * 64, e_ * 64 + 64)
                                    mm(pS.ap[ps_, 512:576], XTa[ps_, :], Xa[ps_, :], [XTb, Xb], [(pS, 1)])
                                    if lev < 5:
                                        mm(pS.ap[ps_, 576:640], Xa[ps_, :], XTa[ps_, :], [XTb, Xb], [(pS, 1)])
                            for hp in range(nhp):
                                d = WS[hp]
                                pS = psS[hp]
                                Xn = d["X"][lev % 2]
                                XTn = d["XT"][lev % 2]
                                cp("dve", Xn.ap, U_(pS, 8), [(pS, 1)], [Xn])
                                Xc[hp] = (Xn.ap, Xn)
                                if lev < 5:
                                    cp("dve", XTn.ap, U_(pS, 9), [(pS, 1)], [XTn])
                                    XTc[hp] = (XTn.ap, XTn)
                            for hp in range(nhp):
                                d = WS[hp]
                                pS = psS[hp]
                                Xa, Xb = Xc[hp]
                                TTo = d["TT"][(lev - 1) % 2]
                                for e_ in range(2):
                                    ps_ = slice(e_ * 64, e_ * 64 + 64)
                                    mm(pS.ap[ps_, 640:704], Xa[ps_, :], TTo.ap[ps_, :], [Xb, TTo], [(pS, 1)])
                            for hp in range(nhp):
                                d = WS[hp]
                                pS = psS[hp]
                                TTo = d["TT"][(lev - 1) % 2]
                                TTn = d["TT"][lev % 2]
                                tt("dve", TTn.ap, U_(pS, 10), TTo.ap, ALU.add, [(pS, 1), TTo], [TTn])
                    ckpt('s_b')
                    if rw:
                        for hp in range(nhp):
                            d = WS[hp]
                            pS = psS[hp]
                            A0 = d["A"][d["ai"] % 2]
                            vbk = d["VBK"][ck % 2]
                            G = d["G"][ck % 2]
                            for e_ in range(2):
                                ps_ = slice(e_ * 64, e_ * 64 + 64)
                                mm(pS.ap[ps_, 704:768], d["AR"].ap[ps_, ck, 0:64], A0.ap[ps_, :], [d["AR"], A0], [(pS, 1)], start=True, stop=False)
                                mm(pS.ap[ps_, 704:768], G.ap[ps_, 128:192], vbk.ap[ps_, 0:64], [G, vbk], [(pS, 1)], start=False, stop=True)
                        for hp in range(nhp):
                            d = WS[hp]
                            cp("dve", d["Z"].ap, U_(psS[hp], 11), [(psS[hp], 1)], [d["Z"]])
                        for hp in range(nhp):
                            d = WS[hp]
                            pS = psS[hp]
                            TTf = d["TT"][1]
                            for e_ in range(2):
                                ps_ = slice(e_ * 64, e_ * 64 + 64)
                                mm(pS.ap[ps_, 768:832], TTf.ap[ps_, :], d["Z"].ap[ps_, :], [TTf, d["Z"]], [(pS, 1)])
                        for hp in range(nhp):
                            d = WS[hp]
                            cp("dve", d["U"].ap, U_(psS[hp], 12), [(psS[hp], 1)], [d["U"]])
                    for hp in range(nhp):
                        d = WS[hp]
                        pS = psS[hp]
                        A0 = d["A"][d["ai"] % 2]
                        A1 = d["A"][(d["ai"] + 1) % 2]
                        vbk = d["VBK"][ck % 2]
                        G = d["G"][ck % 2]
                        ycol = slice(yu * 64, yu * 64 + Cn)
                        for e_ in range(2):
                            ps_ = slice(e_ * 64, e_ * 64 + 64)
                            po_ = slice(e_ * 64, e_ * 64 + Cn)
                            if rw:
                                if want_y:
                                    mm(pS.ap[ps_, ycol], A0.ap[ps_, :], d["AR"].ap[ps_, ck, 64:128], [A0, d["AR"]], [(pS, 1)], start=True, stop=False)
                                    mm(pS.ap[ps_, ycol], d["U"].ap[ps_, :], G.ap[ps_, 64:128], [d["U"], G], [(pS, 1)], start=False, stop=False)
                                    mm(pS.ap[ps_, ycol], vbk.ap[ps_, 0:64], G.ap[ps_, 192:256], [vbk, G], [(pS, 1)], start=False, stop=True)
                                mm(pS.ap[ps_, 832:896], vbk.ap[ps_, 64:128], d["U"].ap[ps_, :], [vbk, d["U"]], [(pS, 1)], start=True, stop=False)
                                mm(pS.ap[ps_, 832:896], vbk.ap[ps_, 128:192], vbk.ap[ps_, 0:64], [vbk], [(pS, 1)], start=False, stop=True)
                            else:
                                if want_y:
                                    mm(pS.ap[ps_, ycol], A0.ap[ps_, :], d["AR"].ap[ps_, ck, :], [A0, d["AR"]], [(pS, 1)], start=True, stop=False)
                                    mm(pS.ap[ps_, ycol], vbk.ap[po_, 0:64], G.ap[po_, 0:Cn], [vbk, G], [(pS, 1)], start=False, stop=True)
                                mm(pS.ap[ps_, 832:896], vbk.ap[po_, 64:128], vbk.ap[po_, 0:64], [vbk], [(pS, 1)])
                        stt(A1.ap, A0.ap, d["Pc"].ap[:, ck:ck + 1], U_(pS, 13), ALU.mult, ALU.add, [A0, d["Pc"], (pS, 1)], [A1])
                        d["ai"] += 1
                        if want_y:
                            yo = d["yo"]
                            if z == 0:
                                cp("dve", yo.ap[:, cs], pS.ap[:, ycol], [(pS, 1)], [(yo, ck)])
                            else:
                                nat = slice(N - (ck + 1) * Cn, N - ck * Cn)
                                cp("dve", yo.ap[:, nat], pS.ap[:, ycol][:, ::-1], [(pS, 1)], [(yo, ck)])

                ckpt('s_c')
                if not want_y:
                    continue
                for hp in range(nhp):
                    d = WS[hp]
                    yo = d["yo"]
                    if z == 0:
                        r0 = hp * 128
                        P.dma(yfd.ap[r0:r0 + 128, t0:t0 + N], yo.ap[:, 0:N], [yo], [], yo.dsem)
                        continue
                    T1, E = d["T1"], d["E"]
                    y = d["Lc"]
                    pR = psS[hp]
                    act(E.ap[:, 0:N], d["gg"].ap[:, 0:N], AF.Silu, [d["gg"]], [E])
                    tt("dve", y.ap[:, 0:N], d["yfl"].ap[:, 0:N], yo.ap[:, 0:N], ALU.add, [d["yfl"], yo], [y])
                    if rw:
                        mm(pR.ap[:, 0:N], bo64, y.ap[:, 0:N], [cst, y], [pR])
                        tt("dve", y.ap[:, 0:N], y.ap[:, 0:N], pR.ap[:, 0:N], ALU.subtract, [y, pR], [y])
                    act(T1.ap[:, 0:N], y.ap[:, 0:N], AF.Square, [y], [T1])
                    mm(pR.ap[:, 512:512 + N], bo64, T1.ap[:, 0:N], [cst, T1], [pR])
                    ts("dve", T1.ap[:, 0:N], pR.ap[:, 512:512 + N], GN_EPS if rw else NORM_EPS, None, ALU.add, None, [pR], [T1])
                    act(T1.ap[:, 0:N], T1.ap[:, 0:N], AF.Sqrt, [T1], [T1])
                    recip(T1.ap[:, 0:N], T1.ap[:, 0:N], [T1], [T1])
                    tt("dve", y.ap[:, 0:N], y.ap[:, 0:N], T1.ap[:, 0:N], ALU.mult, [y, T1], [y])
                    if rw:
                        ts("dve", y.ap[:, 0:N], y.ap[:, 0:N], PR("gn_w")[:, l, hp:hp + 1], PR("gn_b")[:, l, hp:hp + 1], ALU.mult, ALU.add, [y, prm], [y])
                        stt(T1.ap[:, 0:N], d["r"].ap[:, 0:N], PR("r_k")[:, l, hp:hp + 1], d["k"].ap[:, 0:N], ALU.mult, ALU.mult, [d["r"], d["k"], prm], [T1])
                        mm(pR.ap[:, 0:N], bo1, T1.ap[:, 0:N], [cst, T1], [pR])
                        tt("dve", T1.ap[:, 0:N], pR.ap[:, 0:N], d["v"].ap[:, 0:N], ALU.mult, [pR, d["v"]], [T1])
                        tt("dve", y.ap[:, 0:N], y.ap[:, 0:N], T1.ap[:, 0:N], ALU.add, [y, T1], [y])
                    else:
                        ts("dve", y.ap[:, 0:N], y.ap[:, 0:N], PR("hnw")[:, l, hp:hp + 1], None, ALU.mult, None, [y, prm], [y])
                    yob = yo.ap.bitcast(BF16)
                    tt("dve", yob[:, 0:N], y.ap[:, 0:N], E.ap[:, 0:N], ALU.mult, [y, E], [yo])
                    r0 = (hp if rw else 3 + hp) * 128
                    P.dma(yt_d.ap[r0:r0 + 128, t0:t0 + N], yob[:, 0:N], [yo], [], yo.dsem)

        scan_pass("rw", 0)
        ckpt('rw0%d' % l)
        scan_pass("rw", 1)
        ckpt('rw1%d' % l)
        scan_pass("hg", 0)
        ckpt('hg0%d' % l)
        scan_pass("hg", 1)
        ckpt('hg1%d' % l)

        P.barrier()
        P.bump = pers_mark
        Qr = P.alloc("Qr", [128, 3, NT], BF16)
        Kr = P.alloc("Kr", [128, 2, NT], BF16)
        Vt = P.alloc("Vt", [128, NT // 128, 128], BF16)
        Gs = P.alloc("Gs", [128, 3, NT], BF16)
        mx = P.alloc("mx", [128, 8])
        P.op("pool", lambda e: e.memset(mx.ap, 0.0), [], [mx])
        ld = [P.alloc("ald%d" % i, [128, 512], dsem="ald%d" % i) for i in range(4)]
        rp = [P.alloc("arp%d" % i, [128, 2, 512], dsem="arp%d" % i) for i in range(2)]
        tq = [P.alloc("atq%d" % i, [128, 512]) for i in range(3)]
        li = 0
        for gi, (t0, N) in enumerate(groups):
            rpb = rp[gi % 2]
            P.dma(rpb.ap[:, :, 0:N], rope_d.ap[:, :, t0:t0 + N], [], [rpb], rpb.dsem)
            for (dst, di, c_main, c_sw, mxi) in ((Qr, 0, 24, 32, 0), (Qr, 1, 25, 33, 1), (Qr, 2, 26, 34, 2), (Kr, 0, 27, 35, 3), (Kr, 1, 36, 37, 4)):
                a_ = ld[li % 4]
                b_ = ld[(li + 1) % 4]
                li += 2
                P.dma(a_.ap[:, 0:N], raw_d.ap[c_main * 128:(c_main + 1) * 128, t0:t0 + N], [], [a_], a_.dsem)
                P.dma(b_.ap[:, 0:N], raw_d.ap[c_sw * 128:(c_sw + 1) * 128, t0:t0 + N], [], [b_], b_.dsem)
                tt("dve", a_.ap[:, 0:N], a_.ap[:, 0:N], rpb.ap[:, 0, 0:N], ALU.mult, [a_, rpb], [a_])
                tt("dve", b_.ap[:, 0:N], b_.ap[:, 0:N], rpb.ap[:, 1, 0:N], ALU.mult, [b_, rpb], [b_])
                tt("dve", tq[0].ap[:, 0:N], a_.ap[:, 0:N], b_.ap[:, 0:N], ALU.add, [a_, b_], [tq[0]])
                cp("pool", dst.ap[:, di, t0:t0 + N], tq[0].ap[:, 0:N], [tq[0]], [(dst, (di, gi))])
                tt("dve", tq[1].ap[:, 0:N], tq[0].ap[:, 0:N], tq[0].ap[:, 0:N], ALU.mult, [tq[0]], [tq[1]])
                pb = PSB[mxi % 4]
                mm(pb.ap[:, 0:N], bo1, tq[1].ap[:, 0:N], [cst, tq[1]], [pb])
                P.op("dve", lambda e, pb=pb, N=N: e.reduce_max(out=tq[2].ap[:, 0:1], in_=pb.ap[:, 0:N], axis=AX.X), [pb], [tq[2]])
                tt("dve", mx.ap[:, mxi:mxi + 1], mx.ap[:, mxi:mxi + 1], tq[2].ap[:, 0:1], ALU.max, [mx, tq[2]], [mx])
            a_ = ld[li % 4]
            li += 1
            P.dma(a_.ap[:, 0:N], raw_d.ap[28 * 128:29 * 128, t0:t0 + N], [], [a_], a_.dsem)
            for s in range(N // 128):
                pb = PSB[s % 4]
                tr(pb.ap[:, 0:128], a_.ap[:, s * 128:(s + 1) * 128], ident, [a_, cst], [pb])
                cp("dve", Vt.ap[:, (t0 // 128) + s, :], pb.ap[:, 0:128], [pb], [(Vt, (t0 // 128) + s)])
            for j in range(3):
                a_ = ld[li % 4]
                li += 1
                P.dma(a_.ap[:, 0:N], raw_d.ap[(29 + j) * 128:(30 + j) * 128, t0:t0 + N], [], [a_], a_.dsem)
                act(Gs.ap[:, j, t0:t0 + N], a_.ap[:, 0:N], AF.Silu, [a_], [(Gs, (j, gi))])
        act(tq[2].ap[:, 1:2], mx.ap[:, 0:1], AF.Square, [mx], [tq[2]])
        act(tq[2].ap[:, 1:2], tq[2].ap[:, 1:2], AF.Sqrt, [tq[2]], [tq[2]])
        negM = P.alloc("negM", [128, 6])
        sinkE = P.alloc("sinkE", [128, 6])
        msum = P.alloc("msum", [128, 6])
        kAB = {0: 0, 1: 1, 2: 0, 3: 0, 4: 1, 5: 0}
        for h in range(6):
            tt("dve", msum.ap[:, h:h + 1], mx.ap[:, h // 2:h // 2 + 1], mx.ap[:, 3 + kAB[h]:4 + kAB[h]], ALU.add, [mx], [(msum, h)])
        pb = PSB[2]
        for h in range(6):
            sel = C("sel0") if h % 2 == 0 else C("sel1")
            mm(pb.ap[:, h:h + 1], sel, msum.ap[:, h:h + 1], [cst, msum], [(pb, h)])
        ts("dve", negM.ap, pb.ap[:, 0:6], -1.0 / 16.0, None, ALU.mult, None, [pb], [negM])
        tt("dve", sinkE.ap, PR("sink")[:, l, :], negM.ap, ALU.add, [prm, negM], [sinkE])
        act(sinkE.ap, sinkE.ap, AF.Exp, [sinkE], [sinkE])

        Eb = [P.alloc("Eb%d" % i, [128, 640], BF16) for i in range(3)]
        den = [P.alloc("den%d" % i, [128, 128]) for i in range(2)]
        yto = [P.alloc("yto%d" % i, [128, 128], BF16, dsem="yto%d" % i) for i in range(3)]
        mprev_b = P.alloc("mprev_b", [128, 128], BF16)
        mnext_b = P.alloc("mnext_b", [128, 128], BF16)
        cp("dve", mprev_b.ap, C("mprev"), [cst], [mprev_b])
        cp("dve", mnext_b.ap, C("mnext"), [cst], [mnext_b])
        nctx = CTX // 128
        nlat = TL // 128
        qblocks = [("l", n) for n in range(nlat)] + ([] if last else [("c", n) for n in range(nctx)])
        bi = 0
        for (kind_, n) in qblocks:
            if kind_ == "l":
                qtok = CTX + n * 128
                kb = []
                if n > 0:
                    kb.append((qtok - 128, "prev"))
                kb.append((qtok, None))
                if n < nlat - 1:
                    kb.append((qtok + 128, "next"))
                kb += [(c_ * 128, None) for c_ in range(nctx)]
            else:
                qtok = n * 128
                kb = [(c_ * 128, None) for c_ in range(nctx)]
            nk = len(kb)
            for hpair in range(3):
                pN = PSB[(bi) % 2]
                pD = PSB[2 + (bi % 2)]
                yb_ = yto[bi % 3]
                dn = den[bi % 2]
                for e_ in range(2):
                    h = hpair * 2 + e_
                    ps_ = slice(e_ * 64, e_ * 64 + 64)
                    pS = PSW[h % 2]
                    eb = Eb[h % 3]
                    for ki, (kt0, mk) in enumerate(kb):
                        mm(pS.ap[:, ki * 128:(ki + 1) * 128], Kr.ap[ps_, kAB[h], kt0:kt0 + 128], Qr.ap[ps_, hpair, qtok:qtok + 128],
                           [Kr, Qr], [(pS, ki // 4)])
                    act(eb.ap[:, 0:nk * 128], pS.ap[:, 0:nk * 128], AF.Exp, [pS, negM], [eb], scale=0.125, bias=negM.ap[:, h:h + 1])
                    for ki, (kt0, mk) in enumerate(kb):
                        if mk is not None:
                            mb = mprev_b if mk == "prev" else mnext_b
                            tt("dve", eb.ap[:, ki * 128:(ki + 1) * 128], eb.ap[:, ki * 128:(ki + 1) * 128], mb.ap, ALU.mult, [eb, mb], [eb])
                    kvh = h // 3
                    for ki, (kt0, mk) in enumerate(kb):
                        mm(pN.ap[ps_, 0:128], Vt.ap[:, kt0 // 128, kvh * 64:(kvh + 1) * 64], eb.ap[:, ki * 128:(ki + 1) * 128],
                           [Vt, eb], [pN], start=(ki == 0), stop=(ki == nk - 1))
                    for ki, (kt0, mk) in enumerate(kb):
                        mm(pD.ap[ps_, 0:128], ones_b.ap, eb.ap[:, ki * 128:(ki + 1) * 128],
                           [ones_b, eb], [(pD, e_)], start=(ki == 0), stop=(ki == nk - 1))
                    ts("dve", dn.ap[ps_, :], pD.ap[ps_, 0:128], sinkE.ap[ps_, h:h + 1], None, ALU.add, None, [(pD, e_), sinkE], [(dn, e_)])
                recip(dn.ap, dn.ap, [dn], [dn])
                tt("dve", dn.ap, dn.ap, pN.ap[:, 0:128], ALU.mult, [dn, pN], [dn])
                tt("dve", yb_.ap, dn.ap, Gs.ap[:, hpair, qtok:qtok + 128], ALU.mult, [dn, Gs], [yb_])
                r0 = (5 + hpair) * 128
                P.dma(yt_d.ap[r0:r0 + 128, qtok:qtok + 128], yb_.ap, [yb_], [], yb_.dsem)
                bi += 1

        ckpt('att%d' % l)
        P.barrier()
        P.bump = pers_mark
        gbc = P.alloc("gbc", [128, 2, D])
        dg = [P.alloc("dg%d" % i, [128, 128]) for i in range(2)]
        nw = 1 if last else 2
        for w in range(nw):
            for j in range(8):
                dgb = dg[j % 2]
                ts("dve", dgb.ap, ident, gatev[:, j, w:w + 1], None, ALU.mult, None, [cst, modT], [dgb])
                pw = PSW[w]
                mm(pw.ap[:, j * 128:(j + 1) * 128], ones_f, dgb.ap, [cst, dgb], [(pw, j // 4)])
            cp("dve", gbc.ap[:, w, :], PSW[w].ap, [PSW[w]], [(gbc, w)])
        fnb = P.alloc("fnb", [128, D])
        if last:
            for j in range(8):
                dgb = dg[j % 2]
                ts("dve", dgb.ap, ident, PR("fnw")[:, j:j + 1], None, ALU.mult, None, [cst, prm], [dgb])
                pw = PSW[1]
                mm(pw.ap[:, j * 128:(j + 1) * 128], ones_f, dgb.ap, [cst, dgb], [(pw, j // 4)])
            cp("dve", fnb.ap, PSW[1].ap, [PSW[1]], [fnb])
        wog = P.alloc("wog", [128, 2, 8, D], BF16)
        wos = [P.alloc("wos%d" % i, [128, 8, 256], dsem="wos%d" % i) for i in range(2)]
        for q in range(4):
            st = wos[q % 2]
            src = wout_d.ap[l].rearrange("(kc p) n -> p kc n", p=128)[:, :, q * 256:(q + 1) * 256]
            P.dma(st.ap, src, [], [st], st.dsem)
            for w in range(nw):
                for kc in range(8):
                    tt("dve", wog.ap[:, w, kc, q * 256:(q + 1) * 256], st.ap[:, kc, :], gbc.ap[:, w, q * 256:(q + 1) * 256], ALU.mult,
                       [st, (gbc, w)], [(wog, (w, q))])
        ytl = [P.alloc("ytl%d" % i, [128, 8, 128], BF16, dsem="ytl%d" % i) for i in range(2)]
        xc = [P.alloc("xc%d" % i, [128, D], dsem="xc%d" % i) for i in range(2)]
        xo = [P.alloc("xo%d" % i, [128, D], dsem="xo%d" % i) for i in range(2)]
        sq = P.alloc("csq", [128, D])
        cs2 = [P.alloc("cs2_%d" % i, [128, 2]) for i in range(2)]
        tiles = list(range(CTX // 128, NT // 128)) + ([] if last else list(range(CTX // 128)))
        for ii, tI in enumerate(tiles):
            tok = tI * 128
            w = 1 if tok < CTX else 0
            yl, xcb, xob, pw, ssb = ytl[ii % 2], xc[ii % 2], xo[ii % 2], PSW[ii % 2], cs2[ii % 2]
            P.dma(yl.ap, yt_d.ap.rearrange("(c p) t -> p c t", p=128)[:, :, tok:tok + 128], [], [yl], yl.dsem)
            P.dma(xcb.ap, xsrc.ap[tok:tok + 128, :], [], [xcb], xcb.dsem)
            for half in range(2):
                for kc in range(8):
                    mm(pw.ap[:, half * 512:(half + 1) * 512], yl.ap[:, kc, :], wog.ap[:, w, kc, half * 512:(half + 1) * 512], [yl, wog], [(pw, half)],
                       start=(kc == 0), stop=(kc == 7))
            tt("dve", xob.ap, pw.ap, xcb.ap, ALU.add, [pw, xcb], [xob])
            if not last:
                P.dma(x1_d.ap[tok:tok + 128, :], xob.ap, [xob], [], xob.dsem)
            else:
                act(sq.ap, xob.ap, AF.Square, [xob], [sq, (ssb, 0)], accum=ssb.ap[:, 0:1])
                ts("dve", ssb.ap[:, 1:2], ssb.ap[:, 0:1], 1.0 / D, NORM_EPS, ALU.mult, ALU.add, [(ssb, 0)], [(ssb, 1)])
                act(ssb.ap[:, 1:2], ssb.ap[:, 1:2], AF.Sqrt, [(ssb, 1)], [(ssb, 1)])
                recip(ssb.ap[:, 1:2], ssb.ap[:, 1:2], [(ssb, 1)], [(ssb, 1)])
                stt(xob.ap, xob.ap, ssb.ap[:, 1:2], fnb.ap, ALU.mult, ALU.mult, [xob, (ssb, 1), fnb], [xob])
                P.dma(out_d.ap[tok - CTX:tok - CTX + 128, :], xob.ap, [xob], [], xob.dsem)

    try:
        for l_ in range(L):
            layer(l_)
    except StopBuild:
        pass
    P.barrier(("sp",))
    P.emit()
    return nc


def prepare_inputs(inp, CTX, TL, L):
    idx = col_index()
    B = inp["x"].shape[0]
    cst_pk = build_consts()
    cst = cst_pk.build()
    rope = rope_tables(CTX, TL)
    wext = np.zeros((L, D, NCOL), np.float32)
    for l in range(L):
        wext[l, :, 0:idx.size] = np.asarray(inp["w_in"][l])[:, idx]
        if l > 0:
            wext[l, :, 38 * 128:38 * 128 + 32] = np.asarray(inp["rwkv_vres_w1"][l - 1])
    modw = np.ascontiguousarray(inp["mod_w"], dtype=np.float32)
    wout = np.ascontiguousarray(inp["w_out"], dtype=np.float32)
    maps = []
    prm_pk = None
    for b in range(B):
        prm_pk = build_params(inp, b, L)
        xin = np.ascontiguousarray(np.concatenate([np.asarray(inp["ctx"][b]), np.asarray(inp["x"][b])], axis=0), dtype=np.float32)
        maps.append({"xin": xin, "prm": prm_pk.build(), "cst": cst, "rope": rope, "wext": wext, "modw": modw, "wout": wout})
    return maps, cst_pk, prm_pk


def kernel(**inputs):
    inp = {k: np.asarray(v) for k, v in inputs.items()}
    B, TL, _ = inp["x"].shape
    CTX = inp["ctx"].shape[1]
    L = inp["mod_w"].shape[0]
    maps, cst_pk, prm_pk = prepare_inputs(inp, CTX, TL, L)
    nc = build_program(CTX, TL, L, cst_pk, prm_pk)
    res = run_bass_kernel_spmd(nc, maps, core_ids=list(range(B)))
    return np.stack([np.asarray(r["out"], dtype=np.float32) for r in res.results], axis=0)
```

```python
import contextlib
import math
import numpy as np
import concourse.bass as bass
import concourse.mybir as mybir
from concourse.bass_utils import run_bass_kernel_spmd

F32 = mybir.dt.float32
BF16 = mybir.dt.bfloat16
AF = mybir.ActivationFunctionType
ALU = mybir.AluOpType
AX = mybir.AxisListType

D = 1024
HD = 64
RW = 384
HW = 256
AW = 384
NCH = 39
NCOL = NCH * 128
GN_EPS = 64e-5
NORM_EPS = 1e-6
DECAY_K = -math.exp(-0.5)
ENGINES = ("pe", "act", "dve", "pool", "sp")


class Buf:
    def __init__(self, name, ap, dsem=None):
        self.name = name
        self.ap = ap
        self.st = {}
        self.dsem = dsem

    def __getitem__(self, idx):
        return self.ap[idx]


class DSem:
    def __init__(self, sem, key):
        self.sem = sem
        self.cnt = 0
        self.key = key


class Prog:
    def __init__(self, nc, arena_cols):
        self.nc = nc
        self.stack = contextlib.ExitStack()
        self.ops = {e: [] for e in ENGINES}
        self.cnt = {e: 0 for e in ENGINES}
        self.known = {e: {} for e in ENGINES}
        self.sems = {}
        for e in ("pe", "act", "dve", "pool"):
            self.sems[e] = self.stack.enter_context(nc.semaphore("s_" + e))
        self.dsems = {}
        self.arena = self.stack.enter_context(nc.sbuf_tensor("arena", [128, arena_cols], F32))
        self.arena_cols = arena_cols
        self.bump = 0
        self.nops = 0

    def dsem(self, name):
        if name not in self.dsems:
            s = self.stack.enter_context(self.nc.semaphore("d_" + name))
            ds = DSem(s, ("dma", name))
            self.dsems[name] = ds
            self.sems[ds.key] = s
        return self.dsems[name]

    def alloc(self, name, shape, dt=F32, dsem=None):
        n = int(np.prod(shape[1:]))
        cols = n if dt == F32 else (n + 1) // 2
        assert self.bump + cols <= self.arena_cols, (name, self.bump, cols, self.arena_cols)
        ap = self.arena[:][:, self.bump:self.bump + cols]
        self.bump += cols
        if dt != F32:
            ap = ap.bitcast(dt)
            if n % 2:
                ap = ap[:, 0:n]
        if shape[0] < 128:
            ap = ap[0:shape[0], :]
        if len(shape) == 3:
            ap = ap.rearrange("p (a b) -> p a b", b=shape[2])
        elif len(shape) == 4:
            ap = ap.rearrange("p (a b c) -> p a b c", b=shape[2], c=shape[3])
        return Buf(name, ap, self.dsem(dsem) if dsem else None)

    def ps(self, name, shape, dt=F32):
        h = self.stack.enter_context(self.nc.psum_tensor(name, list(shape), dt))
        return Buf(name, h[:])

    def dram(self, name, shape, dt=F32, kind="Internal", dsem=None):
        h = self.nc.dram_tensor(name, list(shape), dt, kind=kind)
        return Buf(name, h.ap(), self.dsem(dsem) if dsem else None)

    def _states(self, buf, key):
        if key is None:
            return list(buf.st.values())
        out = []
        if key in buf.st:
            out.append(buf.st[key])
        if None in buf.st:
            out.append(buf.st[None])
        return out

    def _getst(self, buf, key):
        if key not in buf.st:
            buf.st[key] = dict(w={}, r={})
        return buf.st[key]

    @staticmethod
    def _norm(lst):
        out = []
        for x in lst:
            if isinstance(x, Buf):
                out.append((x, None))
            elif hasattr(x, "ref"):
                r = x.ref()
                out.append((r, None) if isinstance(r, Buf) else r)
            else:
                a, k = x
                if hasattr(a, "ref"):
                    r = a.ref(k)
                    out.append((r, None) if isinstance(r, Buf) else r)
                else:
                    out.append((a, k))
        return out

    def op(self, eng, fn, reads=(), writes=(), dsem=None):
        reads = self._norm(reads)
        writes = self._norm(writes)
        need = {}

        def req(d):
            for k, v in d.items():
                if need.get(k, 0) < v:
                    need[k] = v

        for b, key in reads:
            for st in self._states(b, key):
                req(st["w"])
        for b, key in writes:
            for st in self._states(b, key):
                req(st["w"])
                req(st["r"])
        if dsem is not None:
            dsem.cnt += 16
            mykey, myval = dsem.key, dsem.cnt
            inc = (dsem.sem, 16)
        else:
            self.cnt[eng] += 1
            mykey, myval = eng, self.cnt[eng]
            inc = (self.sems[eng], 1)
            if eng == "pe":
                need.pop("pe", None)
        waits = []
        kn = self.known[eng]
        for k, v in need.items():
            if kn.get(k, 0) < v:
                kn[k] = v
                waits.append((self.sems[k], v))
        self.ops[eng].append((waits, fn, inc))
        self.nops += 1
        for b, key in reads:
            st = self._getst(b, key)
            if st["r"].get(mykey, 0) < myval:
                st["r"][mykey] = myval
        for b, key in writes:
            if key is None:
                b.st = {None: dict(w={mykey: myval}, r={})}
            else:
                st = self._getst(b, key)
                st["w"] = {mykey: myval}
                st["r"] = {}

    def dma(self, out_ap, in_ap, reads, writes, dsem, eng="sp"):
        self.op(eng, lambda e: e.dma_start(out=out_ap, in_=in_ap), reads, writes, dsem=dsem)

    def barrier(self, engines=ENGINES):
        need = {}
        for ds in self.dsems.values():
            if ds.cnt:
                need[ds.key] = ds.cnt
        for e in ("pe", "act", "dve", "pool"):
            if self.cnt[e]:
                need[e] = self.cnt[e]
        for eng in engines:
            waits = []
            kn = self.known[eng]
            for k, v in need.items():
                if k == eng:
                    continue
                if kn.get(k, 0) < v:
                    kn[k] = v
                    waits.append((self.sems[k], v))
            if waits:
                self.ops[eng].append((waits, None, None))

    def emit(self):
        nc = self.nc
        prog = self
        with nc.Block() as block:
            def run(engname, eobj):
                for waits, fn, inc in prog.ops[engname]:
                    for s, v in waits:
                        eobj.wait_ge(s, v)
                    if fn is not None:
                        fn(eobj).then_inc(inc[0], inc[1])

            @block.sync
            def _(e):
                run("sp", e)

            @block.tensor
            def _(e):
                run("pe", e)

            @block.scalar
            def _(e):
                run("act", e)

            @block.vector
            def _(e):
                run("dve", e)

            @block.gpsimd
            def _(e):
                run("pool", e)
        self.stack.close()


class Packer:
    def __init__(self):
        self.items = []
        self.off = {}
        self.n = 0

    def add(self, name, arr):
        arr = np.ascontiguousarray(arr, dtype=np.float32)
        assert arr.shape[0] == 128, (name, arr.shape)
        a2 = arr.reshape(128, -1)
        self.off[name] = (self.n, a2.shape[1], arr.shape[1:])
        self.items.append(a2)
        self.n += a2.shape[1]

    def build(self):
        return np.ascontiguousarray(np.concatenate(self.items, axis=1))


def pl(v, nch):
    return np.ascontiguousarray(np.asarray(v).reshape(nch, 128).T)


def col_index():
    idx = list(range(0, 1792))
    idx += list(range(1792, 3072))
    q0, k0, v0, g0 = 3072, 3456, 3584, 3712
    idx += list(range(q0, q0 + 384))
    idx += list(range(k0, k0 + 128))
    idx += list(range(v0, v0 + 128))
    idx += list(range(g0, g0 + 384))

    def sw(d):
        g, j = d // 32, d % 32
        return g * 32 + (1 - j // 16) * 16 + (j % 16)

    idx += [q0 + (c // 64) * 64 + sw(c % 64) for c in range(384)]
    idx += [k0 + (c // 64) * 64 + sw(c % 64) for c in range(128)]
    kb = [k0 + ((1 - c // 64) * 64) + (c % 64) for c in range(128)]
    idx += kb
    idx += [k0 + ((1 - c // 64) * 64) + sw(c % 64) for c in range(128)]
    return np.array(idx, dtype=np.int64)


def rope_tables(CTX, TL):
    NT = CTX + TL
    cos = np.ones((128, NT), np.float64)
    sins = np.zeros((128, NT), np.float64)
    t = np.arange(TL)
    row, col = t // 64, t % 64
    for p in range(128):
        d = p % 64
        g, j = d // 32, d % 32
        inv = 10000.0 ** (-(j % 16) / 16.0)
        pos = row if g == 0 else col
        ang = pos.astype(np.float32).astype(np.float64) * np.float32(inv).astype(np.float64)
        ang = (pos.astype(np.float32) * np.float32(inv)).astype(np.float64)
        cos[p, CTX:] = np.cos(ang)
        s = np.sin(ang)
        sins[p, CTX:] = -s if (j // 16) == 0 else s
    return np.stack([cos, sins], axis=1).astype(np.float32)


def build_consts():
    pk = Packer()
    pk.add("ident", np.eye(128))
    p = np.arange(128)[:, None]
    c = np.arange(128)[None, :]
    bo = (p // 64 == c // 64).astype(np.float32)
    pk.add("bo1", bo)
    pk.add("bo64", bo / 64.0)
    pk.add("sel0", np.broadcast_to((p // 64 == 0) / 64.0, (128, 128)))
    pk.add("sel1", np.broadcast_to((p // 64 == 1) / 64.0, (128, 128)))
    pk.add("ones", np.ones((128, 128)))
    pk.add("ident2", (np.arange(128)[:, None] % 64 == np.arange(64)[None, :]).astype(np.float32))
    s = (np.arange(128) % 64)[:, None]
    t = np.arange(64)[None, :]
    m = np.concatenate([s < t, s <= t, s < t, s <= t, t < s], axis=1).astype(np.float32)
    pk.add("mrw", m)
    pk.add("mhg", (s <= np.arange(32)[None, :]).astype(np.float32))
    kk = np.arange(128)[:, None]
    qq = np.arange(128)[None, :]
    pk.add("mprev", (kk >= qq).astype(np.float32))
    pk.add("mnext", (kk <= qq).astype(np.float32))
    tt = np.arange(512)[None, :]
    pk.add("rs64", np.broadcast_to((tt % 64 != 0).astype(np.float32), (128, 512)))
    pk.add("rs32", np.broadcast_to((tt % 32 != 0).astype(np.float32), (128, 512)))
    return pk


def build_params(inp, b, L):
    pk = Packer()
    cc = np.stack([pl(inp["c"][b], 8), pl(inp["c_ctx"], 8)], axis=2)
    pk.add("c", cc)
    pk.add("mod_b", np.stack([pl(inp["mod_b"][l], 24) for l in range(L)], axis=1))
    pk.add("norm_w", np.stack([pl(inp["norm_w"][l], 8) for l in range(L)], axis=1))
    pk.add("fnw", pl(inp["final_norm_w"], 8))
    pk.add("mu_p", np.stack([pl(inp["rwkv_mu_prev"][l], 14) for l in range(L)], axis=1))
    pk.add("mu_n", np.stack([pl(inp["rwkv_mu_next"][l], 14) for l in range(L)], axis=1))
    for nm, key in (("w0", "rwkv_w0"), ("a0", "rwkv_a0")):
        pk.add(nm, np.stack([np.stack([pl(inp[key][l, z], 3) for z in range(2)], axis=1) for l in range(L)], axis=1))
    for nm, key in (("k_k", "rwkv_k_k"), ("k_a", "rwkv_k_a"), ("gn_w", "rwkv_gn_w"), ("gn_b", "rwkv_gn_b")):
        pk.add(nm, np.stack([pl(inp[key][l], 3) for l in range(L)], axis=1))
    pk.add("r_k", np.stack([pl(inp["rwkv_r_k"][l].reshape(-1), 3) for l in range(L)], axis=1))
    pk.add("vres_b", pl(inp["rwkv_vres_b"][0], 3))
    lbl = np.stack([np.stack([pl(inp["hgrn_lb_logits"][z, l], 2) for l in range(L)], axis=1) for z in range(2)], axis=1)
    pk.add("lbl", lbl)
    pk.add("hnw", np.stack([pl(inp["hgrn_norm_w"][l], 2) for l in range(L)], axis=1))
    pk.add("sink", np.broadcast_to(np.asarray(inp["attn_sink"])[None, :, :], (128, L, 6)))
    pk.add("w2", np.stack([np.asarray(inp["rwkv_w2"][l]).reshape(128, 384) for l in range(L)], axis=1))
    pk.add("a2", np.stack([np.asarray(inp["rwkv_a2"][l]).reshape(128, 384) for l in range(L)], axis=1))
    vw2 = np.zeros((128, 384), np.float32)
    vw2[0:32] = np.asarray(inp["rwkv_vres_w2"][0])
    pk.add("vw2", vw2)
    return pk


class StopBuild(Exception):
    pass


def build_program(CTX, TL, L, cst_pk, prm_pk, dbg=(), upto=None):
    NT = CTX + TL
    assert TL % 512 == 0 and CTX % 128 == 0 and CTX <= 512
    groups = [(0, CTX)] + [(CTX + 512 * j, 512) for j in range(TL // 512)]
    NG = len(groups)
    nc = bass.Bass("TRN2", target_bir_lowering=False)
    P = Prog(nc, 46000)

    xin = P.dram("xin", [NT, D], F32, kind="ExternalInput")
    prm_d = P.dram("prm", [128, prm_pk.n], F32, kind="ExternalInput")
    cst_d = P.dram("cst", [128, cst_pk.n], F32, kind="ExternalInput")
    rope_d = P.dram("rope", [128, 2, NT], F32, kind="ExternalInput")
    wext_d = P.dram("wext", [L, D, NCOL], F32, kind="ExternalInput")
    modw_d = P.dram("modw", [L, D, 3 * D], F32, kind="ExternalInput")
    wout_d = P.dram("wout", [L, D, D], F32, kind="ExternalInput")
    out_d = P.dram("out", [TL, D], F32, kind="ExternalOutput", dsem="out")
    raw_d = P.dram("raw", [NCOL, NT], F32, dsem="raw")
    feat_d = [P.dram("feat%d" % l, [9 * RW, NT], F32, dsem="feat") for l in range(L)]
    yf_d = P.dram("yf", [RW, NT], F32, dsem="yf")
    of_d = P.dram("of", [HW, NT], F32, dsem="yf")
    yt_d = P.dram("yt", [D, NT], BF16, dsem="yt")
    x1_d = P.dram("x1", [NT, D], F32, dsem="x1")
    dbg_d = {}

    cst = P.alloc("cst", [128, cst_pk.n], dsem="cst")
    prm = P.alloc("prm", [128, prm_pk.n], dsem="prm")
    P.dma(cst.ap, cst_d.ap, [], [cst], cst.dsem)
    P.dma(prm.ap, prm_d.ap, [], [prm], prm.dsem)

    def C(name):
        o, n, shp = cst_pk.off[name]
        ap = cst.ap[:, o:o + n]
        return ap

    def PR(name):
        o, n, shp = prm_pk.off[name]
        ap = prm.ap[:, o:o + n]
        if len(shp) == 2:
            ap = ap.rearrange("p (a b) -> p a b", b=shp[1])
        elif len(shp) == 3:
            ap = ap.rearrange("p (a b c) -> p a b c", b=shp[1], c=shp[2])
        return ap

    ident = C("ident")
    bo1, bo64 = C("bo1"), C("bo64")
    ones_f = C("ones")
    modT = P.alloc("modT", [128, 24, 2])
    g1 = P.alloc("g1", [128, 8, 2])
    c0 = P.alloc("c0", [128, 14])
    lb = P.alloc("lb", [128, 2, 2])
    oml = P.alloc("oml", [128, 2, 2])
    ones_b = P.alloc("ones_b", [128, 64], BF16)
    P.op("pool", lambda e: e.memset(ones_b.ap, 1.0), [], [ones_b])
    pers_mark = P.bump

    PSW = [P.ps("psw%d" % i, [128, 1024]) for i in range(3)]
    PSB2 = [P.ps("psb%d" % i, [128, 512]) for i in range(2)]

    class Bank:
        def __init__(self, buf, key, ap):
            self.buf, self.key, self.ap = buf, key, ap

        def ref(self, sub=None):
            if self.key is None:
                return self.buf
            return (self.buf, self.key)

    PSB = [Bank(PSW[2], 0, PSW[2].ap[:, 0:512]), Bank(PSW[2], 1, PSW[2].ap[:, 512:1024]),
           Bank(PSB2[0], None, PSB2[0].ap), Bank(PSB2[1], None, PSB2[1].ap)]

    def mm(out, lhsT, rhs, rd, wr, start=True, stop=True):
        P.op("pe", lambda e: e.matmul(out, lhsT=lhsT, rhs=rhs, start=start, stop=stop), rd, wr)

    def tr(out, in_, idn, rd, wr):
        P.op("pe", lambda e: e.transpose(out, in_, idn), rd, wr)

    def act(out, in_, func, rd, wr, bias=None, scale=None, accum=None):
        kw = {}
        if bias is not None:
            kw["bias"] = bias
        if scale is not None:
            kw["scale"] = scale
        if accum is not None:
            kw["accum_out"] = accum
        P.op("act", lambda e: e.activation(out=out, in_=in_, func=func, **kw), rd, wr)

    def tt(eng, out, in0, in1, op, rd, wr):
        P.op(eng, lambda e: e.tensor_tensor(out=out, in0=in0, in1=in1, op=op), rd, wr)

    def ts(eng, out, in0, s1, s2, op0, op1, rd, wr):
        if s2 is None:
            P.op(eng, lambda e: e.tensor_scalar(out=out, in0=in0, scalar1=s1, scalar2=None, op0=op0), rd, wr)
        else:
            P.op(eng, lambda e: e.tensor_scalar(out=out, in0=in0, scalar1=s1, scalar2=s2, op0=op0, op1=op1), rd, wr)

    def stt(out, in0, scalar, in1, op0, op1, rd, wr):
        P.op("dve", lambda e: e.scalar_tensor_tensor(out=out, in0=in0, scalar=scalar, in1=in1, op0=op0, op1=op1), rd, wr)

    def cp(eng, out, in_, rd, wr):
        if eng == "act":
            act(out, in_, AF.Copy, rd, wr)
        else:
            P.op(eng, lambda e: e.tensor_copy(out=out, in_=in_), rd, wr)

    def recip(out, in_, rd, wr):
        P.op("dve", lambda e: e.reciprocal(out=out, in_=in_), rd, wr)

    rr = {"i": 0}

    def rot(engs=("act", "dve")):
        rr["i"] += 1
        return engs[rr["i"] % len(engs)]

    def ckpt(name):
        if upto == name:
            raise StopBuild()

    def layer(l):
        last = (l == L - 1)
        P.barrier()
        P.bump = pers_mark
        sc = P.alloc("sc", [128, 8, 2])
        act(sc.ap, PR("c"), AF.Silu, [prm], [sc])
        mws = [P.alloc("mws%d" % i, [128, 8, 256], dsem="mws%d" % i) for i in range(2)]
        psm = PSB[2]
        for q in range(12):
            st = mws[q % 2]
            src = modw_d.ap[l].rearrange("(kc p) n -> p kc n", p=128)[:, :, q * 256:(q + 1) * 256]
            P.dma(st.ap, src, [], [st], st.dsem)
            for jj in range(2):
                jc = q * 2 + jj
                for kc in range(8):
                    mm(psm.ap[:, jc * 2:jc * 2 + 2], st.ap[:, kc, jj * 128:(jj + 1) * 128], sc.ap[:, kc, :],
                       [st, sc], [(psm, jc)], start=(kc == 0), stop=(kc == 7))
        tt("dve", modT.ap, psm.ap[:, 0:48].rearrange("p (a b) -> p a b", b=2),
           PR("mod_b")[:, l, :].unsqueeze(2).to_broadcast([128, 24, 2]), ALU.add, [psm, prm], [modT])
        ts("dve", g1.ap, modT.ap[:, 8:16, :], 1.0, None, ALU.add, None, [modT], [g1])
        tt("dve", g1.ap, g1.ap, PR("norm_w")[:, l, :].unsqueeze(2).to_broadcast([128, 8, 2]), ALU.mult, [g1, prm], [g1])
        tt("dve", c0.ap, PR("mu_p")[:, l, :], PR("mu_n")[:, l, :], ALU.add, [prm], [c0])
        ts("dve", c0.ap, c0.ap, -1.0, 1.0, ALU.mult, ALU.add, [c0], [c0])
        if l == 0:
            P.op("dve", lambda e: e.memset(lb.ap, 0.0), [], [lb])
        else:
            tt("dve", lb.ap, PR("lbl")[:, :, 1, :], PR("lbl")[:, :, 0, :], ALU.subtract, [prm], [lb])
            act(lb.ap, lb.ap, AF.Sigmoid, [lb], [lb])
        ts("dve", oml.ap, lb.ap, -1.0, 1.0, ALU.mult, ALU.add, [lb], [oml])
        shiftv = modT.ap[:, 0:8, :]
        gatev = modT.ap[:, 16:24, :]

        ckpt('M%d' % l)
        P.barrier()
        P.bump = pers_mark
        wbf = P.alloc("wbf", [128, 8, NCOL], BF16)
        wst = [P.alloc("wst%d" % i, [128, 8, 256], dsem="wst%d" % i) for i in range(2)]
        npiece = NCOL // 256 + (1 if NCOL % 256 else 0)
        for q in range(npiece):
            st = wst[q % 2]
            c_lo = q * 256
            w = min(256, NCOL - c_lo)
            src = wext_d.ap[l].rearrange("(kc p) n -> p kc n", p=128)[:, :, c_lo:c_lo + w]
            P.dma(st.ap[:, :, 0:w], src, [], [st], st.dsem)
            cp(rot(("act", "dve", "pool")), wbf.ap[:, :, c_lo:c_lo + w], st.ap[:, :, 0:w], [st], [(wbf, q)])
        a_mark = P.bump
        ckpt('W%d' % l)

        xsrc = xin if l == 0 else x1_d
        xt = [P.alloc("xt%d" % i, [128, D], dsem="xt%d" % i) for i in range(2)]
        junk = P.alloc("junk", [128, D])
        xn = [P.alloc("xn%d" % i, [128, D]) for i in range(2)]
        st_ss = [P.alloc("ss%d" % i, [128, 2]) for i in range(2)]
        hT = [P.alloc("hT%d" % i, [128, 8, 512], BF16) for i in range(2)]
        evs = [P.alloc("evs%d" % i, [128, 512], dsem="evs%d" % i) for i in range(4)]
        ti = 0
        ei = 0
        for gi, (t0, N) in enumerate(groups):
            w = 1 if gi == 0 else 0
            hg = hT[gi % 2]
            for s in range(N // 128):
                xb, xnb, ssb = xt[ti % 2], xn[ti % 2], st_ss[ti % 2]
                psT = PSW[ti % 2]
                ti += 1
                tok = t0 + s * 128
                P.dma(xb.ap, xsrc.ap[tok:tok + 128, :], [], [xb], xb.dsem)
                act(junk.ap, xb.ap, AF.Square, [xb], [junk, (ssb, 0)], accum=ssb.ap[:, 0:1])
                ts("dve", ssb.ap[:, 1:2], ssb.ap[:, 0:1], 1.0 / D, NORM_EPS, ALU.mult, ALU.add, [(ssb, 0)], [(ssb, 1)])
                act(ssb.ap[:, 1:2], ssb.ap[:, 1:2], AF.Sqrt, [(ssb, 1)], [(ssb, 1)])
                recip(ssb.ap[:, 1:2], ssb.ap[:, 1:2], [(ssb, 1)], [(ssb, 1)])
                ts("dve", xnb.ap, xb.ap, ssb.ap[:, 1:2], None, ALU.mult, None, [xb, (ssb, 1)], [xnb])
                for j in range(8):
                    tr(psT.ap[:, j * 128:(j + 1) * 128], xnb.ap[:, j * 128:(j + 1) * 128], ident, [xnb, cst], [(psT, j // 4)])
                for j in range(8):
                    o = hg.ap[:, j, s * 128:(s + 1) * 128]
                    i_ = psT.ap[:, j * 128:(j + 1) * 128]
                    if j % 2 == 0:
                        act(o, i_, AF.Identity, [(psT, j // 4), g1, modT], [(hg, s)], scale=g1.ap[:, j, w:w + 1], bias=shiftv[:, j, w:w + 1])
                    else:
                        ts("dve", o, i_, g1.ap[:, j, w:w + 1], shiftv[:, j, w:w + 1], ALU.mult, ALU.add, [(psT, j // 4), g1, modT], [(hg, s)])
            nch = NCH if l > 0 else NCH - 1
            for c in range(nch):
                pb = PSB[c % 4]
                for kc in range(8):
                    mm(pb.ap[:, 0:N], wbf.ap[:, kc, c * 128:(c + 1) * 128], hg.ap[:, kc, 0:N], [wbf, hg], [pb],
                       start=(kc == 0), stop=(kc == 7))
                ev = evs[ei % 4]
                ei += 1
                cp(rot(), ev.ap[:, 0:N], pb.ap[:, 0:N], [pb], [ev])
                P.dma(raw_d.ap[c * 128:(c + 1) * 128, t0:t0 + N], ev.ap[:, 0:N], [ev], [], ev.dsem)

        ckpt('A%d' % l)
        P.barrier()
        P.bump = pers_mark
        FT = feat_d[l]

        def frow(arr, j):
            return arr * RW + j * 128

        RH = [P.alloc("rh%d" % i, [128, 514], dsem="rh%d" % i) for i in range(4)]
        SH = [P.alloc("sh%d" % i, [128, 512], dsem="sh%d" % i) for i in range(14)]
        tmpA = [P.alloc("tmpA%d" % i, [128, 512], dsem="tmpA%d" % i) for i in range(4)]
        vf = [P.alloc("vf%d" % i, [128, 512], dsem="vf%d" % i) for i in range(2)]
        hv = P.alloc("hv", [32, 512], dsem="hv")
        w2v, a2v = PR("w2"), PR("a2")
        hi = 0
        tai = 0
        for gi, (t0, N) in enumerate(groups):
            seq_lo, seq_hi = (0, CTX) if gi == 0 else (CTX, NT)
            for c in range(14):
                rh = RH[hi % 4]
                hi += 1
                lo = max(t0 - 1, seq_lo)
                hi_ = min(t0 + N + 1, seq_hi)
                P.dma(rh.ap[:, (lo - (t0 - 1)):(hi_ - (t0 - 1))], raw_d.ap[c * 128:(c + 1) * 128, lo:hi_], [], [rh], rh.dsem)
                if lo != t0 - 1:
                    P.op("pool", lambda e, rh=rh: e.memset(rh.ap[:, 0:1], 0.0), [], [rh])
                if hi_ != t0 + N + 1:
                    P.op("pool", lambda e, rh=rh, N=N: e.memset(rh.ap[:, N + 1:N + 2], 0.0), [], [rh])
                sh = SH[c]
                act(sh.ap[:, 0:N], rh.ap[:, 1:N + 1], AF.Identity, [rh, c0], [sh], scale=c0.ap[:, c:c + 1])
                stt(sh.ap[:, 0:N], rh.ap[:, 0:N], PR("mu_p")[:, l, c:c + 1], sh.ap[:, 0:N], ALU.mult, ALU.add, [rh, sh, prm], [sh])
                stt(sh.ap[:, 0:N], rh.ap[:, 2:N + 2], PR("mu_n")[:, l, c:c + 1], sh.ap[:, 0:N], ALU.mult, ALU.add, [rh, sh, prm], [sh])
            act(SH[9].ap[:, 0:N], SH[9].ap[:, 0:N], AF.Tanh, [SH[9]], [SH[9]])
            for (src, wv, b0, arr0, is_w) in ((SH[9], w2v, PR("w0"), 5, True), (SH[10], a2v, PR("a0"), 7, False)):
                for z in range(2):
                    for j in range(3):
                        pb = PSB[(z * 3 + j) % 4]
                        mm(pb.ap[:, 0:N], wv[z * 64:(z + 1) * 64, l, j * 128:(j + 1) * 128], src.ap[z * 64:(z + 1) * 64, 0:N],
                           [prm, src], [pb])
                        ta = tmpA[tai % 4]
                        tai += 1
                        act(ta.ap[:, 0:N], pb.ap[:, 0:N], AF.Sigmoid, [pb, prm], [ta], bias=b0[:, l, z, j:j + 1])
                        if is_w:
                            ts("dve", ta.ap[:, 0:N], ta.ap[:, 0:N], DECAY_K, None, ALU.mult, None, [ta], [ta])
                        r0 = frow(arr0 + z, j)
                        P.dma(FT.ap[r0:r0 + 128, t0:t0 + N], ta.ap[:, 0:N], [ta], [], ta.dsem)
            if l > 0:
                P.dma(hv.ap[:, 0:N], raw_d.ap[38 * 128:38 * 128 + 32, t0:t0 + N], [], [hv], hv.dsem)
                for j in range(3):
                    pb = PSB[j % 4]
                    mm(pb.ap[:, 0:N], PR("vw2")[0:32, j * 128:(j + 1) * 128], hv.ap[:, 0:N], [prm, hv], [pb])
                    ta = tmpA[tai % 4]
                    tai += 1
                    act(ta.ap[:, 0:N], pb.ap[:, 0:N], AF.Sigmoid, [pb, prm], [ta], bias=PR("vres_b")[:, j:j + 1])
                    v1 = vf[j % 2]
                    r0 = frow(2, j)
                    P.dma(v1.ap[:, 0:N], feat_d[0].ap[r0:r0 + 128, t0:t0 + N], [], [v1], v1.dsem)
                    vv = SH[6 + j]
                    tt("dve", v1.ap[:, 0:N], v1.ap[:, 0:N], vv.ap[:, 0:N], ALU.subtract, [v1, vv], [v1])
                    tt("dve", v1.ap[:, 0:N], v1.ap[:, 0:N], ta.ap[:, 0:N], ALU.mult, [v1, ta], [v1])
                    tt("dve", vv.ap[:, 0:N], vv.ap[:, 0:N], v1.ap[:, 0:N], ALU.add, [vv, v1], [vv])
            for j in range(3):
                ta = tmpA[tai % 4]
                tb = tmpA[(tai + 1) % 4]
                tai += 2
                pb = PSB[(j + 2) % 4]
                ts("dve", ta.ap[:, 0:N], SH[3 + j].ap[:, 0:N], PR("k_k")[:, l, j:j + 1], None, ALU.mult, None, [SH[3 + j], prm], [ta])
                tt("dve", tb.ap[:, 0:N], ta.ap[:, 0:N], ta.ap[:, 0:N], ALU.mult, [ta], [tb])
                mm(pb.ap[:, 0:N], bo1, tb.ap[:, 0:N], [cst, tb], [pb])
                act(tb.ap[:, 0:N], pb.ap[:, 0:N], AF.Sqrt, [pb], [tb])
                ts("dve", tb.ap[:, 0:N], tb.ap[:, 0:N], 1e-12, None, ALU.max, None, [tb], [tb])
                recip(tb.ap[:, 0:N], tb.ap[:, 0:N], [tb], [tb])
                tt("dve", ta.ap[:, 0:N], ta.ap[:, 0:N], tb.ap[:, 0:N], ALU.mult, [ta, tb], [ta])
                r0 = frow(3, j)
                P.dma(FT.ap[r0:r0 + 128, t0:t0 + N], ta.ap[:, 0:N], [ta], [], ta.dsem)
            for (arr, c_lo) in ((0, 0), (1, 3), (2, 6), (4, 11)):
                for j in range(3):
                    r0 = frow(arr, j)
                    sh = SH[c_lo + j]
                    P.dma(FT.ap[r0:r0 + 128, t0:t0 + N], sh.ap[:, 0:N], [sh], [], sh.dsem)

        ckpt('B1%d' % l)
        def scan_pass(kind, z):
            rw = kind == "rw"
            Cn = 64 if rw else 32
            nhp = 3 if rw else 2
            yfd = yf_d if rw else of_d
            rs = C("rs64") if rw else C("rs32")
            mrw, mhg = C("mrw"), C("mhg")
            P.barrier()
            P.bump = pers_mark
            WS = []
            for hp in range(nhp):
                d = {}
                names = ["r", "k", "v", "vs", "kk", "a", "lw", "Lc", "E", "T1", "T2", "bt", "kt", "bh", "kh", "gg", "yfl", "yo"] if rw else \
                        ["r", "k", "v", "vs", "lw", "Lc", "E", "T1", "kt", "kh", "gg", "yfl", "yo"]
                for nm in names:
                    d[nm] = P.alloc("%s%d" % (nm, hp), [128, 512], dsem=("ws_%s%d" % (nm, hp)) if nm in ("r", "k", "v", "kk", "a", "lw", "gg", "yfl", "yo") else None)
                d["AR"] = P.alloc("AR%d" % hp, [128, 8, 128] if rw else [128, 16, 32])
                d["Pc"] = P.alloc("Pc%d" % hp, [128, 16])
                d["A"] = [P.alloc("A%d_%d" % (hp, i), [128, 64]) for i in range(2)]
                d["Z"] = P.alloc("Z%d" % hp, [128, 64])
                d["U"] = P.alloc("U%d" % hp, [128, 64])
                d["VBK"] = [P.alloc("VBK%d_%d" % (hp, i), [128, 192]) for i in range(2)]
                d["G"] = [P.alloc("G%d_%d" % (hp, i), [128, 320]) for i in range(2)]
                d["XX"] = [P.alloc("XX%d_%d" % (hp, i), [128, 128]) for i in range(2)]
                d["TT"] = [P.alloc("TT%d_%d" % (hp, i), [128, 64]) for i in range(2)]
                d["ai"] = 0
                P.op("pool", lambda e, d=d: e.memset(d["A"][0].ap, 0.0), [], [d["A"][0]])
                for i_ in range(2):
                    P.op("pool", lambda e, d=d, i_=i_: e.memset(d["VBK"][i_].ap, 0.0), [], [d["VBK"][i_]])
                    P.op("pool", lambda e, d=d, i_=i_: e.memset(d["G"][i_].ap, 0.0), [], [d["G"][i_]])
                WS.append(d)
            psS = [PSW[0], PSW[1], PSW[2]]
            def U_(pS, u, n=1):
                return pS.ap[:, u * 64:(u + n) * 64]

            order = list(range(NG)) if z == 0 else [0] + list(range(NG - 1, 0, -1))
            for gi in order:
                t0, N = groups[gi]
                nck = N // Cn
                want_y = not (last and gi == 0)

                def V(ap, N=N):
                    return ap[:, 0:N] if z == 0 else ap[:, 0:N][:, ::-1]

                def c3(ap, N=N):
                    return ap[:, 0:N].rearrange("p (c t) -> p c t", t=Cn)

                for hp in range(nhp):
                    d = WS[hp]
                    if rw:
                        srcs = (("r", 0), ("k", 1), ("v", 2), ("kk", 3), ("a", 7 + z), ("lw", 5 + z))
                        for nm, arr in srcs:
                            r0 = frow(arr, hp)
                            P.dma(d[nm].ap[:, 0:N], FT.ap[r0:r0 + 128, t0:t0 + N], [], [d[nm]], d[nm].dsem)
                    else:
                        for nm, ch in (("r", 14), ("lw", 16 + 2 * z), ("v", 20)):
                            r0 = (ch + hp) * 128
                            P.dma(d[nm].ap[:, 0:N], raw_d.ap[r0:r0 + 128, t0:t0 + N], [], [d[nm]], d[nm].dsem)
                        act(d["lw"].ap[:, 0:N], d["lw"].ap[:, 0:N], AF.Sigmoid, [d["lw"]], [d["lw"]])
                        ts("dve", d["lw"].ap[:, 0:N], d["lw"].ap[:, 0:N], oml.ap[:, z, hp:hp + 1], lb.ap[:, z, hp:hp + 1], ALU.mult, ALU.add,
                           [d["lw"], oml, lb], [d["lw"]])
                        ts("dve", d["k"].ap[:, 0:N], d["lw"].ap[:, 0:N], -1.0, 1.0, ALU.mult, ALU.add, [d["lw"]], [d["k"]])
                        act(d["lw"].ap[:, 0:N], d["lw"].ap[:, 0:N], AF.Ln, [d["lw"]], [d["lw"]])
                    if z == 1 and want_y:
                        r0 = hp * 128
                        P.dma(d["yfl"].ap[:, 0:N], yfd.ap[r0:r0 + 128, t0:t0 + N], [], [d["yfl"]], d["yfl"].dsem)
                        if rw:
                            r0 = frow(4, hp)
                            P.dma(d["gg"].ap[:, 0:N], FT.ap[r0:r0 + 128, t0:t0 + N], [], [d["gg"]], d["gg"].dsem)
                        else:
                            r0 = (22 + hp) * 128
                            P.dma(d["gg"].ap[:, 0:N], raw_d.ap[r0:r0 + 128, t0:t0 + N], [], [d["gg"]], d["gg"].dsem)
                    Lc, E, T1 = d["Lc"], d["E"], d["T1"]
                    cp("pool" if z == 0 else "dve", d["vs"].ap[:, 0:N], V(d["v"].ap), [d["v"]], [d["vs"]])
                    P.op("dve", lambda e, Lc=Lc, d=d, N=N, V=V: e.tensor_tensor_scan(out=Lc.ap[:, 0:N], data0=rs[:, 0:N], data1=V(d["lw"].ap),
                                                                                     initial=0.0, op0=ALU.mult, op1=ALU.add), [d["lw"], cst], [Lc])
                    AR = d["AR"]
                    bc = c3(Lc.ap)[:, :, Cn - 1:Cn].to_broadcast([128, nck, Cn])
                    if rw:
                        T2 = d["T2"]
                        act(E.ap[:, 0:N], Lc.ap[:, 0:N], AF.Exp, [Lc], [E])
                        tt("dve", AR.ap[:, 0:nck, 64:128], c3(V(d["r"].ap)), c3(E.ap), ALU.mult, [d["r"], E], [AR])
                        tt("dve", T1.ap[:, 0:N], Lc.ap[:, 0:N], V(d["lw"].ap), ALU.subtract, [Lc, d["lw"]], [T1])
                        act(T1.ap[:, 0:N], T1.ap[:, 0:N], AF.Exp, [T1], [T1])
                        stt(AR.ap[:, 0:nck, 0:64], c3(V(d["kk"].ap)), -1.0, c3(T1.ap), ALU.mult, ALU.mult, [d["kk"], T1], [AR])
                        ts("dve", T1.ap[:, 0:N], V(d["a"].ap), -1.0, PR("k_a")[:, l, hp:hp + 1], ALU.add, ALU.mult, [d["a"], prm], [T1])
                        stt(T1.ap[:, 0:N], T1.ap[:, 0:N], 1.0, V(d["k"].ap), ALU.add, ALU.mult, [T1, d["k"]], [T1])
                        tt("dve", T2.ap[:, 0:N], V(d["kk"].ap), V(d["a"].ap), ALU.mult, [d["kk"], d["a"]], [T2])
                        act(E.ap[:, 0:N], Lc.ap[:, 0:N], AF.Exp, [Lc], [E], scale=-1.0)
                        tt("dve", d["bt"].ap[:, 0:N], T2.ap[:, 0:N], E.ap[:, 0:N], ALU.mult, [T2, E], [d["bt"]])
                        tt("dve", d["kt"].ap[:, 0:N], T1.ap[:, 0:N], E.ap[:, 0:N], ALU.mult, [T1, E], [d["kt"]])
                        tt("dve", c3(E.ap), bc, c3(Lc.ap), ALU.subtract, [Lc], [E])
                        act(E.ap[:, 0:N], E.ap[:, 0:N], AF.Exp, [E], [E])
                        tt("dve", d["bh"].ap[:, 0:N], T2.ap[:, 0:N], E.ap[:, 0:N], ALU.mult, [T2, E], [d["bh"]])
                        tt("dve", d["kh"].ap[:, 0:N], T1.ap[:, 0:N], E.ap[:, 0:N], ALU.mult, [T1, E], [d["kh"]])
                    else:
                        act(E.ap[:, 0:N], Lc.ap[:, 0:N], AF.Exp, [Lc], [E])
                        tt("dve", AR.ap[:, 0:nck, :], c3(V(d["r"].ap)), c3(E.ap), ALU.mult, [d["r"], E], [AR])
                        act(E.ap[:, 0:N], Lc.ap[:, 0:N], AF.Exp, [Lc], [E], scale=-1.0)
                        tt("dve", d["kt"].ap[:, 0:N], V(d["k"].ap), E.ap[:, 0:N], ALU.mult, [d["k"], E], [d["kt"]])
                        tt("dve", c3(E.ap), bc, c3(Lc.ap), ALU.subtract, [Lc], [E])
                        act(E.ap[:, 0:N], E.ap[:, 0:N], AF.Exp, [E], [E])
                        tt("dve", d["kh"].ap[:, 0:N], V(d["k"].ap), E.ap[:, 0:N], ALU.mult, [d["k"], E], [d["kh"]])
                    act(d["Pc"].ap[:, 0:nck], c3(Lc.ap)[:, :, Cn - 1], AF.Exp, [Lc], [d["Pc"]])

                ckpt('s_prep')
                for ck in range(nck):
                    cs = slice(ck * Cn, (ck + 1) * Cn)
                    yu = 14 + (ck % 2)
                    for hp in range(nhp):
                        d = WS[hp]
                        pS = psS[hp]
                        vbk = d["VBK"][ck % 2]
                        G = d["G"][ck % 2]
                        tl = (d["vs"], d["bh"], d["kh"]) if rw else (d["vs"], d["kh"])
                        nt_ = len(tl)
                        for e_ in range(2):
                            ps_ = slice(e_ * 64, e_ * 64 + 64)
                            po_ = slice(e_ * 64, e_ * 64 + Cn)
                            for ti_, sb_ in enumerate(tl):
                                mm(pS.ap[po_, ti_ * 64:(ti_ + 1) * 64], sb_.ap[ps_, cs], ident[ps_, ps_], [sb_, cst], [(pS, 0)])
                        ckpt('s_a1')
                        if rw:
                            cp("dve", vbk.ap[:, 0:nt_ * 64], pS.ap[:, 0:nt_ * 64], [(pS, 0)], [vbk])
                        else:
                            for e_ in range(2):
                                po_ = slice(e_ * 64, e_ * 64 + Cn)
                                cp("dve", vbk.ap[po_, 0:nt_ * 64], pS.ap[po_, 0:nt_ * 64], [(pS, 0)], [(vbk, e_)])
                        ckpt('s_a2')
                        for e_ in range(2):
                            ps_ = slice(e_ * 64, e_ * 64 + 64)
                            po_ = slice(e_ * 64, e_ * 64 + Cn)
                            if rw:
                                arv = d["AR"].ap[ps_, ck, :]
                                mm(pS.ap[po_, 192:320], d["bt"].ap[ps_, cs], arv, [d["bt"], d["AR"]], [(pS, 0)])
                                mm(pS.ap[po_, 320:448], d["kt"].ap[ps_, cs], arv, [d["kt"], d["AR"]], [(pS, 0)])
                                mm(pS.ap[po_, 448:512], d["AR"].ap[ps_, ck, 0:64], d["bt"].ap[ps_, cs], [d["bt"], d["AR"]], [(pS, 0)])
                            else:
                                mm(pS.ap[po_, 192:192 + Cn], d["kt"].ap[ps_, cs], d["AR"].ap[ps_, ck, :], [d["kt"], d["AR"]], [(pS, 0)])
                        ckpt('s_a3')
                        if rw:
                            tt("dve", G.ap[:, 0:320], pS.ap[:, 192:512], mrw, ALU.mult, [(pS, 0), cst], [G])
                        else:
                            for e_ in range(2):
                                po_ = slice(e_ * 64, e_ * 64 + Cn)
                                tt("dve", G.ap[po_, 0:Cn], pS.ap[po_, 192:192 + Cn], mhg[po_, :], ALU.mult, [(pS, 0), cst], [(G, e_)])
                    ckpt('s_a')
                    if rw:
                        Xc = [None] * nhp
                        XTc = [None] * nhp
                        for hp in range(nhp):
                            d = WS[hp]
                            G = d["G"][ck % 2]
                            tt("dve", d["TT"][0].ap, G.ap[:, 0:64], C("ident2"), ALU.add, [G, cst], [d["TT"][0]])
                            Xc[hp] = (G.ap[:, 256:320], G)
                            XTc[hp] = (G.ap[:, 0:64], G)
                        for lev in range(1, 6):
                            for hp in range(nhp):
                                pS = psS[hp]
                                Xa, Xb = Xc[hp]
                                XTa, XTb = XTc[hp]
                                for e_ in range(2):
                                    ps_ = slice(e_ * 64, e_ * 64 + 64)
                                    mm(pS.ap[ps_, 512:576], XTa[ps_, :], Xa[ps_, :], [XTb, Xb], [(pS, 1)])
                                    if lev < 5:
                                        mm(pS.ap[ps_, 576:640], Xa[ps_, :], XTa[ps_, :], [XTb, Xb], [(pS, 1)])
                            for hp in range(nhp):
                                d = WS[hp]
                                pS = psS[hp]
                                XXn = d["XX"][lev % 2]
                                if lev < 5:
                                    cp("dve", XXn.ap[:, 0:128], pS.ap[:, 512:640], [(pS, 1)], [XXn])
                                    XTc[hp] = (XXn.ap[:, 64:128], XXn)
                                else:
                                    cp("dve", XXn.ap[:, 0:64], U_(pS, 8), [(pS, 1)], [XXn])
                                Xc[hp] = (XXn.ap[:, 0:64], XXn)
                            for hp in range(nhp):
                                d = WS[hp]
                                pS = psS[hp]
                                Xa, Xb = Xc[hp]
                                TTo = d["TT"][(lev - 1) % 2]
                                for e_ in range(2):
                                    ps_ = slice(e_ * 64, e_ * 64 + 64)
                                    mm(pS.ap[ps_, 640:704], Xa[ps_, :], TTo.ap[ps_, :], [Xb, TTo], [(pS, 1)])
                            for hp in range(nhp):
                                d = WS[hp]
                                pS = psS[hp]
                                TTo = d["TT"][(lev - 1) % 2]
                                TTn = d["TT"][lev % 2]
                                tt("dve", TTn.ap, U_(pS, 10), TTo.ap, ALU.add, [(pS, 1), TTo], [TTn])
                    ckpt('s_b')
                    if rw:
                        for hp in range(nhp):
                            d = WS[hp]
                            pS = psS[hp]
                            A0 = d["A"][d["ai"] % 2]
                            vbk = d["VBK"][ck % 2]
                            G = d["G"][ck % 2]
                            for e_ in range(2):
                                ps_ = slice(e_ * 64, e_ * 64 + 64)
                                mm(pS.ap[ps_, 704:768], d["AR"].ap[ps_, ck, 0:64], A0.ap[ps_, :], [d["AR"], A0], [(pS, 1)], start=True, stop=False)
                                mm(pS.ap[ps_, 704:768], G.ap[ps_, 128:192], vbk.ap[ps_, 0:64], [G, vbk], [(pS, 1)], start=False, stop=True)
                        for hp in range(nhp):
                            d = WS[hp]
                            cp("dve", d["Z"].ap, U_(psS[hp], 11), [(psS[hp], 1)], [d["Z"]])
                        for hp in range(nhp):
                            d = WS[hp]
                            pS = psS[hp]
                            TTf = d["TT"][1]
                            for e_ in range(2):
                                ps_ = slice(e_ * 64, e_ * 64 + 64)
                                mm(pS.ap[ps_, 768:832], TTf.ap[ps_, :], d["Z"].ap[ps_, :], [TTf, d["Z"]], [(pS, 1)])
                        for hp in range(nhp):
                            d = WS[hp]
                            cp("dve", d["U"].ap, U_(psS[hp], 12), [(psS[hp], 1)], [d["U"]])
                    for hp in range(nhp):
                        d = WS[hp]
                        pS = psS[hp]
                        A0 = d["A"][d["ai"] % 2]
                        A1 = d["A"][(d["ai"] + 1) % 2]
                        vbk = d["VBK"][ck % 2]
                        G = d["G"][ck % 2]
                        ycol = slice(yu * 64, yu * 64 + Cn)
                        for e_ in range(2):
                            ps_ = slice(e_ * 64, e_ * 64 + 64)
                            po_ = slice(e_ * 64, e_ * 64 + Cn)
                            if rw:
                                if want_y:
                                    mm(pS.ap[ps_, ycol], A0.ap[ps_, :], d["AR"].ap[ps_, ck, 64:128], [A0, d["AR"]], [(pS, 1)], start=True, stop=False)
                                    mm(pS.ap[ps_, ycol], d["U"].ap[ps_, :], G.ap[ps_, 64:128], [d["U"], G], [(pS, 1)], start=False, stop=False)
                                    mm(pS.ap[ps_, ycol], vbk.ap[ps_, 0:64], G.ap[ps_, 192:256], [vbk, G], [(pS, 1)], start=False, stop=True)
                                mm(pS.ap[ps_, 832:896], vbk.ap[ps_, 64:128], d["U"].ap[ps_, :], [vbk, d["U"]], [(pS, 1)], start=True, stop=False)
                                mm(pS.ap[ps_, 832:896], vbk.ap[ps_, 128:192], vbk.ap[ps_, 0:64], [vbk], [(pS, 1)], start=False, stop=True)
                            else:
                                if want_y:
                                    mm(pS.ap[ps_, ycol], A0.ap[ps_, :], d["AR"].ap[ps_, ck, :], [A0, d["AR"]], [(pS, 1)], start=True, stop=False)
                                    mm(pS.ap[ps_, ycol], vbk.ap[po_, 0:64], G.ap[po_, 0:Cn], [vbk, G], [(pS, 1)], start=False, stop=True)
                                mm(pS.ap[ps_, 832:896], vbk.ap[po_, 64:128], vbk.ap[po_, 0:64], [vbk], [(pS, 1)])
                        stt(A1.ap, A0.ap, d["Pc"].ap[:, ck:ck + 1], U_(pS, 13), ALU.mult, ALU.add, [A0, d["Pc"], (pS, 1)], [A1])
                        d["ai"] += 1
                        if want_y:
                            yo = d["yo"]
                            if z == 0:
                                cp("dve", yo.ap[:, cs], pS.ap[:, ycol], [(pS, 1)], [(yo, ck)])
                            else:
                                nat = slice(N - (ck + 1) * Cn, N - ck * Cn)
                                cp("dve", yo.ap[:, nat], pS.ap[:, ycol][:, ::-1], [(pS, 1)], [(yo, ck)])

                ckpt('s_c')
                if not want_y:
                    continue
                for hp in range(nhp):
                    d = WS[hp]
                    yo = d["yo"]
                    if z == 0:
                        r0 = hp * 128
                        P.dma(yfd.ap[r0:r0 + 128, t0:t0 + N], yo.ap[:, 0:N], [yo], [], yo.dsem)
                        continue
                    T1, E = d["T1"], d["E"]
                    y = d["Lc"]
                    pR = psS[hp]
                    act(E.ap[:, 0:N], d["gg"].ap[:, 0:N], AF.Silu, [d["gg"]], [E])
                    tt("dve", y.ap[:, 0:N], d["yfl"].ap[:, 0:N], yo.ap[:, 0:N], ALU.add, [d["yfl"], yo], [y])
                    if rw:
                        mm(pR.ap[:, 0:N], bo64, y.ap[:, 0:N], [cst, y], [pR])
                        tt("dve", y.ap[:, 0:N], y.ap[:, 0:N], pR.ap[:, 0:N], ALU.subtract, [y, pR], [y])
                    act(T1.ap[:, 0:N], y.ap[:, 0:N], AF.Square, [y], [T1])
                    mm(pR.ap[:, 512:512 + N], bo64, T1.ap[:, 0:N], [cst, T1], [pR])
                    ts("dve", T1.ap[:, 0:N], pR.ap[:, 512:512 + N], GN_EPS if rw else NORM_EPS, None, ALU.add, None, [pR], [T1])
                    act(T1.ap[:, 0:N], T1.ap[:, 0:N], AF.Sqrt, [T1], [T1])
                    recip(T1.ap[:, 0:N], T1.ap[:, 0:N], [T1], [T1])
                    tt("dve", y.ap[:, 0:N], y.ap[:, 0:N], T1.ap[:, 0:N], ALU.mult, [y, T1], [y])
                    if rw:
                        ts("dve", y.ap[:, 0:N], y.ap[:, 0:N], PR("gn_w")[:, l, hp:hp + 1], PR("gn_b")[:, l, hp:hp + 1], ALU.mult, ALU.add, [y, prm], [y])
                        stt(T1.ap[:, 0:N], d["r"].ap[:, 0:N], PR("r_k")[:, l, hp:hp + 1], d["k"].ap[:, 0:N], ALU.mult, ALU.mult, [d["r"], d["k"], prm], [T1])
                        mm(pR.ap[:, 0:N], bo1, T1.ap[:, 0:N], [cst, T1], [pR])
                        tt("dve", T1.ap[:, 0:N], pR.ap[:, 0:N], d["v"].ap[:, 0:N], ALU.mult, [pR, d["v"]], [T1])
                        tt("dve", y.ap[:, 0:N], y.ap[:, 0:N], T1.ap[:, 0:N], ALU.add, [y, T1], [y])
                    else:
                        ts("dve", y.ap[:, 0:N], y.ap[:, 0:N], PR("hnw")[:, l, hp:hp + 1], None, ALU.mult, None, [y, prm], [y])
                    yob = yo.ap.bitcast(BF16)
                    tt("dve", yob[:, 0:N], y.ap[:, 0:N], E.ap[:, 0:N], ALU.mult, [y, E], [yo])
                    r0 = (hp if rw else 3 + hp) * 128
                    P.dma(yt_d.ap[r0:r0 + 128, t0:t0 + N], yob[:, 0:N], [yo], [], yo.dsem)

        scan_pass("rw", 0)
        ckpt('rw0%d' % l)
        scan_pass("rw", 1)
        ckpt('rw1%d' % l)
        scan_pass("hg", 0)
        ckpt('hg0%d' % l)
        scan_pass("hg", 1)
        ckpt('hg1%d' % l)

        P.barrier()
        P.bump = pers_mark
        Qr = P.alloc("Qr", [128, 3, NT], BF16)
        Kr = P.alloc("Kr", [128, 2, NT], BF16)
        Vt = P.alloc("Vt", [128, NT // 128, 128], BF16)
        Gs = P.alloc("Gs", [128, 3, NT], BF16)
        mx = P.alloc("mx", [128, 8])
        P.op("pool", lambda e: e.memset(mx.ap, 0.0), [], [mx])
        ld = [P.alloc("ald%d" % i, [128, 512], dsem="ald%d" % i) for i in range(4)]
        rp = [P.alloc("arp%d" % i, [128, 2, 512], dsem="arp%d" % i) for i in range(2)]
        tq = [P.alloc("atq%d" % i, [128, 512]) for i in range(3)]
        li = 0
        for gi, (t0, N) in enumerate(groups):
            rpb = rp[gi % 2]
            P.dma(rpb.ap[:, :, 0:N], rope_d.ap[:, :, t0:t0 + N], [], [rpb], rpb.dsem)
            for (dst, di, c_main, c_sw, mxi) in ((Qr, 0, 24, 32, 0), (Qr, 1, 25, 33, 1), (Qr, 2, 26, 34, 2), (Kr, 0, 27, 35, 3), (Kr, 1, 36, 37, 4)):
                a_ = ld[li % 4]
                b_ = ld[(li + 1) % 4]
                li += 2
                P.dma(a_.ap[:, 0:N], raw_d.ap[c_main * 128:(c_main + 1) * 128, t0:t0 + N], [], [a_], a_.dsem)
                P.dma(b_.ap[:, 0:N], raw_d.ap[c_sw * 128:(c_sw + 1) * 128, t0:t0 + N], [], [b_], b_.dsem)
                tt("dve", a_.ap[:, 0:N], a_.ap[:, 0:N], rpb.ap[:, 0, 0:N], ALU.mult, [a_, rpb], [a_])
                tt("dve", b_.ap[:, 0:N], b_.ap[:, 0:N], rpb.ap[:, 1, 0:N], ALU.mult, [b_, rpb], [b_])
                tt("dve", tq[0].ap[:, 0:N], a_.ap[:, 0:N], b_.ap[:, 0:N], ALU.add, [a_, b_], [tq[0]])
                cp("pool", dst.ap[:, di, t0:t0 + N], tq[0].ap[:, 0:N], [tq[0]], [(dst, (di, gi))])
                tt("dve", tq[1].ap[:, 0:N], tq[0].ap[:, 0:N], tq[0].ap[:, 0:N], ALU.mult, [tq[0]], [tq[1]])
                pb = PSB[mxi % 4]
                mm(pb.ap[:, 0:N], bo1, tq[1].ap[:, 0:N], [cst, tq[1]], [pb])
                P.op("dve", lambda e, pb=pb, N=N: e.reduce_max(out=tq[2].ap[:, 0:1], in_=pb.ap[:, 0:N], axis=AX.X), [pb], [tq[2]])
                tt("dve", mx.ap[:, mxi:mxi + 1], mx.ap[:, mxi:mxi + 1], tq[2].ap[:, 0:1], ALU.max, [mx, tq[2]], [mx])
            a_ = ld[li % 4]
            li += 1
            P.dma(a_.ap[:, 0:N], raw_d.ap[28 * 128:29 * 128, t0:t0 + N], [], [a_], a_.dsem)
            for s in range(N // 128):
                pb = PSB[s % 4]
                tr(pb.ap[:, 0:128], a_.ap[:, s * 128:(s + 1) * 128], ident, [a_, cst], [pb])
                cp("dve", Vt.ap[:, (t0 // 128) + s, :], pb.ap[:, 0:128], [pb], [(Vt, (t0 // 128) + s)])
            for j in range(3):
                a_ = ld[li % 4]
                li += 1
                P.dma(a_.ap[:, 0:N], raw_d.ap[(29 + j) * 128:(30 + j) * 128, t0:t0 + N], [], [a_], a_.dsem)
                act(Gs.ap[:, j, t0:t0 + N], a_.ap[:, 0:N], AF.Silu, [a_], [(Gs, (j, gi))])
        act(tq[2].ap[:, 1:2], mx.ap[:, 0:1], AF.Square, [mx], [tq[2]])
        act(tq[2].ap[:, 1:2], tq[2].ap[:, 1:2], AF.Sqrt, [tq[2]], [tq[2]])
        negM = P.alloc("negM", [128, 6])
        sinkE = P.alloc("sinkE", [128, 6])
        msum = P.alloc("msum", [128, 6])
        kAB = {0: 0, 1: 1, 2: 0, 3: 0, 4: 1, 5: 0}
        for h in range(6):
            tt("dve", msum.ap[:, h:h + 1], mx.ap[:, h // 2:h // 2 + 1], mx.ap[:, 3 + kAB[h]:4 + kAB[h]], ALU.add, [mx], [(msum, h)])
        pb = PSB[2]
        for h in range(6):
            sel = C("sel0") if h % 2 == 0 else C("sel1")
            mm(pb.ap[:, h:h + 1], sel, msum.ap[:, h:h + 1], [cst, msum], [(pb, h)])
        ts("dve", negM.ap, pb.ap[:, 0:6], -1.0 / 16.0, None, ALU.mult, None, [pb], [negM])
        tt("dve", sinkE.ap, PR("sink")[:, l, :], negM.ap, ALU.add, [prm, negM], [sinkE])
        act(sinkE.ap, sinkE.ap, AF.Exp, [sinkE], [sinkE])

        Eb = [P.alloc("Eb%d" % i, [128, 640], BF16) for i in range(3)]
        den = [P.alloc("den%d" % i, [128, 128]) for i in range(2)]
        yto = [P.alloc("yto%d" % i, [128, 128], BF16, dsem="yto%d" % i) for i in range(3)]
        mprev_b = P.alloc("mprev_b", [128, 128], BF16)
        mnext_b = P.alloc("mnext_b", [128, 128], BF16)
        cp("dve", mprev_b.ap, C("mprev"), [cst], [mprev_b])
        cp("dve", mnext_b.ap, C("mnext"), [cst], [mnext_b])
        nctx = CTX // 128
        nlat = TL // 128
        qblocks = [("l", n) for n in range(nlat)] + ([] if last else [("c", n) for n in range(nctx)])
        bi = 0
        for (kind_, n) in qblocks:
            if kind_ == "l":
                qtok = CTX + n * 128
                kb = []
                if n > 0:
                    kb.append((qtok - 128, "prev"))
                kb.append((qtok, None))
                if n < nlat - 1:
                    kb.append((qtok + 128, "next"))
                kb += [(c_ * 128, None) for c_ in range(nctx)]
            else:
                qtok = n * 128
                kb = [(c_ * 128, None) for c_ in range(nctx)]
            nk = len(kb)
            for hpair in range(3):
                pN = PSB[(bi) % 2]
                pD = PSB[2 + (bi % 2)]
                yb_ = yto[bi % 3]
                dn = den[bi % 2]
                for e_ in range(2):
                    h = hpair * 2 + e_
                    ps_ = slice(e_ * 64, e_ * 64 + 64)
                    pS = PSW[h % 2]
                    eb = Eb[h % 3]
                    for ki, (kt0, mk) in enumerate(kb):
                        mm(pS.ap[:, ki * 128:(ki + 1) * 128], Kr.ap[ps_, kAB[h], kt0:kt0 + 128], Qr.ap[ps_, hpair, qtok:qtok + 128],
                           [Kr, Qr], [(pS, ki // 4)])
                    act(eb.ap[:, 0:nk * 128], pS.ap[:, 0:nk * 128], AF.Exp, [pS, negM], [eb], scale=0.125, bias=negM.ap[:, h:h + 1])
                    for ki, (kt0, mk) in enumerate(kb):
                        if mk is not None:
                            mb = mprev_b if mk == "prev" else mnext_b
                            tt("dve", eb.ap[:, ki * 128:(ki + 1) * 128], eb.ap[:, ki * 128:(ki + 1) * 128], mb.ap, ALU.mult, [eb, mb], [eb])
                    kvh = h // 3
                    for ki, (kt0, mk) in enumerate(kb):
                        mm(pN.ap[ps_, 0:128], Vt.ap[:, kt0 // 128, kvh * 64:(kvh + 1) * 64], eb.ap[:, ki * 128:(ki + 1) * 128],
                           [Vt, eb], [pN], start=(ki == 0), stop=(ki == nk - 1))
                    for ki, (kt0, mk) in enumerate(kb):
                        mm(pD.ap[ps_, 0:128], ones_b.ap, eb.ap[:, ki * 128:(ki + 1) * 128],
                           [ones_b, eb], [(pD, e_)], start=(ki == 0), stop=(ki == nk - 1))
                    ts("dve", dn.ap[ps_, :], pD.ap[ps_, 0:128], sinkE.ap[ps_, h:h + 1], None, ALU.add, None, [(pD, e_), sinkE], [(dn, e_)])
                recip(dn.ap, dn.ap, [dn], [dn])
                tt("dve", dn.ap, dn.ap, pN.ap[:, 0:128], ALU.mult, [dn, pN], [dn])
                tt("dve", yb_.ap, dn.ap, Gs.ap[:, hpair, qtok:qtok + 128], ALU.mult, [dn, Gs], [yb_])
                r0 = (5 + hpair) * 128
                P.dma(yt_d.ap[r0:r0 + 128, qtok:qtok + 128], yb_.ap, [yb_], [], yb_.dsem)
                bi += 1

        ckpt('att%d' % l)
        P.barrier()
        P.bump = pers_mark
        gbc = P.alloc("gbc", [128, 2, D])
        dg = [P.alloc("dg%d" % i, [128, 128]) for i in range(2)]
        nw = 1 if last else 2
        for w in range(nw):
            for j in range(8):
                dgb = dg[j % 2]
                ts("dve", dgb.ap, ident, gatev[:, j, w:w + 1], None, ALU.mult, None, [cst, modT], [dgb])
                pw = PSW[w]
                mm(pw.ap[:, j * 128:(j + 1) * 128], ones_f, dgb.ap, [cst, dgb], [(pw, j // 4)])
            cp("dve", gbc.ap[:, w, :], PSW[w].ap, [PSW[w]], [(gbc, w)])
        fnb = P.alloc("fnb", [128, D])
        if last:
            for j in range(8):
                dgb = dg[j % 2]
                ts("dve", dgb.ap, ident, PR("fnw")[:, j:j + 1], None, ALU.mult, None, [cst, prm], [dgb])
                pw = PSW[1]
                mm(pw.ap[:, j * 128:(j + 1) * 128], ones_f, dgb.ap, [cst, dgb], [(pw, j // 4)])
            cp("dve", fnb.ap, PSW[1].ap, [PSW[1]], [fnb])
        wog = P.alloc("wog", [128, 2, 8, D], BF16)
        wos = [P.alloc("wos%d" % i, [128, 8, 256], dsem="wos%d" % i) for i in range(2)]
        for q in range(4):
            st = wos[q % 2]
            src = wout_d.ap[l].rearrange("(kc p) n -> p kc n", p=128)[:, :, q * 256:(q + 1) * 256]
            P.dma(st.ap, src, [], [st], st.dsem)
            for w in range(nw):
                for kc in range(8):
                    tt("dve", wog.ap[:, w, kc, q * 256:(q + 1) * 256], st.ap[:, kc, :], gbc.ap[:, w, q * 256:(q + 1) * 256], ALU.mult,
                       [st, (gbc, w)], [(wog, (w, q))])
        ytl = [P.alloc("ytl%d" % i, [128, 8, 128], BF16, dsem="ytl%d" % i) for i in range(2)]
        xc = [P.alloc("xc%d" % i, [128, D], dsem="xc%d" % i) for i in range(2)]
        xo = [P.alloc("xo%d" % i, [128, D], dsem="xo%d" % i) for i in range(2)]
        sq = P.alloc("csq", [128, D])
        cs2 = [P.alloc("cs2_%d" % i, [128, 2]) for i in range(2)]
        tiles = list(range(CTX // 128, NT // 128)) + ([] if last else list(range(CTX // 128)))
        for ii, tI in enumerate(tiles):
            tok = tI * 128
            w = 1 if tok < CTX else 0
            yl, xcb, xob, pw, ssb = ytl[ii % 2], xc[ii % 2], xo[ii % 2], PSW[ii % 2], cs2[ii % 2]
            P.dma(yl.ap, yt_d.ap.rearrange("(c p) t -> p c t", p=128)[:, :, tok:tok + 128], [], [yl], yl.dsem)
            P.dma(xcb.ap, xsrc.ap[tok:tok + 128, :], [], [xcb], xcb.dsem)
            for half in range(2):
                for kc in range(8):
                    mm(pw.ap[:, half * 512:(half + 1) * 512], yl.ap[:, kc, :], wog.ap[:, w, kc, half * 512:(half + 1) * 512], [yl, wog], [(pw, half)],
                       start=(kc == 0), stop=(kc == 7))
            tt("dve", xob.ap, pw.ap, xcb.ap, ALU.add, [pw, xcb], [xob])
            if not last:
                P.dma(x1_d.ap[tok:tok + 128, :], xob.ap, [xob], [], xob.dsem)
            else:
                act(sq.ap, xob.ap, AF.Square, [xob], [sq, (ssb, 0)], accum=ssb.ap[:, 0:1])
                ts("dve", ssb.ap[:, 1:2], ssb.ap[:, 0:1], 1.0 / D, NORM_EPS, ALU.mult, ALU.add, [(ssb, 0)], [(ssb, 1)])
                act(ssb.ap[:, 1:2], ssb.ap[:, 1:2], AF.Sqrt, [(ssb, 1)], [(ssb, 1)])
                recip(ssb.ap[:, 1:2], ssb.ap[:, 1:2], [(ssb, 1)], [(ssb, 1)])
                stt(xob.ap, xob.ap, ssb.ap[:, 1:2], fnb.ap, ALU.mult, ALU.mult, [xob, (ssb, 1), fnb], [xob])
                P.dma(out_d.ap[tok - CTX:tok - CTX + 128, :], xob.ap, [xob], [], xob.dsem)

    try:
        for l_ in range(L):
            layer(l_)
    except StopBuild:
        pass
    P.barrier(("sp",))
    P.emit()
    return nc


def prepare_inputs(inp, CTX, TL, L):
    idx = col_index()
    B = inp["x"].shape[0]
    cst_pk = build_consts()
    cst = cst_pk.build()
    rope = rope_tables(CTX, TL)
    wext = np.zeros((L, D, NCOL), np.float32)
    for l in range(L):
        wext[l, :, 0:idx.size] = np.asarray(inp["w_in"][l])[:, idx]
        if l > 0:
            wext[l, :, 38 * 128:38 * 128 + 32] = np.asarray(inp["rwkv_vres_w1"][l - 1])
    modw = np.ascontiguousarray(inp["mod_w"], dtype=np.float32)
    wout = np.ascontiguousarray(inp["w_out"], dtype=np.float32)
    maps = []
    prm_pk = None
    for b in range(B):
        prm_pk = build_params(inp, b, L)
        xin = np.ascontiguousarray(np.concatenate([np.asarray(inp["ctx"][b]), np.asarray(inp["x"][b])], axis=0), dtype=np.float32)
        maps.append({"xin": xin, "prm": prm_pk.build(), "cst": cst, "rope": rope, "wext": wext, "modw": modw, "wout": wout})
    return maps, cst_pk, prm_pk


def kernel(**inputs):
    inp = {k: np.asarray(v) for k, v in inputs.items()}
    B, TL, _ = inp["x"].shape
    CTX = inp["ctx"].shape[1]
    L = inp["mod_w"].shape[0]
    maps, cst_pk, prm_pk = prepare_inputs(inp, CTX, TL, L)
    nc = build_program(CTX, TL, L, cst_pk, prm_pk)
    res = run_bass_kernel_spmd(nc, maps, core_ids=list(range(B)))
    return np.stack([np.asarray(r["out"], dtype=np.float32) for r in res.results], axis=0)
```

```python
import contextlib
import math
import numpy as np
import concourse.bass as bass
import concourse.mybir as mybir
from concourse.bass_utils import run_bass_kernel_spmd

F32 = mybir.dt.float32
BF16 = mybir.dt.bfloat16
AF = mybir.ActivationFunctionType
ALU = mybir.AluOpType
AX = mybir.AxisListType

D = 1024
HD = 64
RW = 384
HW = 256
AW = 384
NCH = 39
NCOL = NCH * 128
GN_EPS = 64e-5
NORM_EPS = 1e-6
DECAY_K = -math.exp(-0.5)
ENGINES = ("pe", "act", "dve", "pool", "sp")


class Buf:
    def __init__(self, name, ap, dsem=None):
        self.name = name
        self.ap = ap
        self.st = {}
        self.dsem = dsem

    def __getitem__(self, idx):
        return self.ap[idx]


class DSem:
    def __init__(self, sem, key):
        self.sem = sem
        self.cnt = 0
        self.key = key


class Prog:
    def __init__(self, nc, arena_cols):
        self.nc = nc
        self.stack = contextlib.ExitStack()
        self.ops = {e: [] for e in ENGINES}
        self.cnt = {e: 0 for e in ENGINES}
        self.known = {e: {} for e in ENGINES}
        self.sems = {}
        for e in ("pe", "act", "dve", "pool"):
            self.sems[e] = self.stack.enter_context(nc.semaphore("s_" + e))
        self.dsems = {}
        self.arena = self.stack.enter_context(nc.sbuf_tensor("arena", [128, arena_cols], F32))
        self.arena_cols = arena_cols
        self.bump = 0
        self.nops = 0

    def dsem(self, name):
        if name not in self.dsems:
            s = self.stack.enter_context(self.nc.semaphore("d_" + name))
            ds = DSem(s, ("dma", name))
            self.dsems[name] = ds
            self.sems[ds.key] = s
        return self.dsems[name]

    def alloc(self, name, shape, dt=F32, dsem=None):
        n = int(np.prod(shape[1:]))
        cols = n if dt == F32 else (n + 1) // 2
        assert self.bump + cols <= self.arena_cols, (name, self.bump, cols, self.arena_cols)
        ap = self.arena[:][:, self.bump:self.bump + cols]
        self.bump += cols
        if dt != F32:
            ap = ap.bitcast(dt)
            if n % 2:
                ap = ap[:, 0:n]
        if shape[0] < 128:
            ap = ap[0:shape[0], :]
        if len(shape) == 3:
            ap = ap.rearrange("p (a b) -> p a b", b=shape[2])
        elif len(shape) == 4:
            ap = ap.rearrange("p (a b c) -> p a b c", b=shape[2], c=shape[3])
        return Buf(name, ap, self.dsem(dsem) if dsem else None)

    def ps(self, name, shape, dt=F32):
        h = self.stack.enter_context(self.nc.psum_tensor(name, list(shape), dt))
        return Buf(name, h[:])

    def dram(self, name, shape, dt=F32, kind="Internal", dsem=None):
        h = self.nc.dram_tensor(name, list(shape), dt, kind=kind)
        return Buf(name, h.ap(), self.dsem(dsem) if dsem else None)

    def _states(self, buf, key):
        if key is None:
            return list(buf.st.values())
        out = []
        if key in buf.st:
            out.append(buf.st[key])
        if None in buf.st:
            out.append(buf.st[None])
        return out

    def _getst(self, buf, key):
        if key not in buf.st:
            buf.st[key] = dict(w={}, r={})
        return buf.st[key]

    @staticmethod
    def _norm(lst):
        out = []
        for x in lst:
            if isinstance(x, Buf):
                out.append((x, None))
            elif hasattr(x, "ref"):
                r = x.ref()
                out.append((r, None) if isinstance(r, Buf) else r)
            else:
                a, k = x
                if hasattr(a, "ref"):
                    r = a.ref(k)
                    out.append((r, None) if isinstance(r, Buf) else r)
                else:
                    out.append((a, k))
        return out

    def op(self, eng, fn, reads=(), writes=(), dsem=None):
        reads = self._norm(reads)
        writes = self._norm(writes)
        need = {}

        def req(d):
            for k, v in d.items():
                if need.get(k, 0) < v:
                    need[k] = v

        for b, key in reads:
            for st in self._states(b, key):
                req(st["w"])
        for b, key in writes:
            for st in self._states(b, key):
                req(st["w"])
                req(st["r"])
        if dsem is not None:
            dsem.cnt += 16
            mykey, myval = dsem.key, dsem.cnt
            inc = (dsem.sem, 16)
        else:
            self.cnt[eng] += 1
            mykey, myval = eng, self.cnt[eng]
            inc = (self.sems[eng], 1)
            if eng == "pe":
                need.pop("pe", None)
        waits = []
        kn = self.known[eng]
        for k, v in need.items():
            if kn.get(k, 0) < v:
                kn[k] = v
                waits.append((self.sems[k], v))
        self.ops[eng].append((waits, fn, inc))
        self.nops += 1
        for b, key in reads:
            st = self._getst(b, key)
            if st["r"].get(mykey, 0) < myval:
                st["r"][mykey] = myval
        for b, key in writes:
            if key is None:
                b.st = {None: dict(w={mykey: myval}, r={})}
            else:
                st = self._getst(b, key)
                st["w"] = {mykey: myval}
                st["r"] = {}

    def dma(self, out_ap, in_ap, reads, writes, dsem, eng="sp"):
        self.op(eng, lambda e: e.dma_start(out=out_ap, in_=in_ap), reads, writes, dsem=dsem)

    def barrier(self, engines=ENGINES):
        need = {}
        for ds in self.dsems.values():
            if ds.cnt:
                need[ds.key] = ds.cnt
        for e in ("pe", "act", "dve", "pool"):
            if self.cnt[e]:
                need[e] = self.cnt[e]
        for eng in engines:
            waits = []
            kn = self.known[eng]
            for k, v in need.items():
                if k == eng:
                    continue
                if kn.get(k, 0) < v:
                    kn[k] = v
                    waits.append((self.sems[k], v))
            if waits:
                self.ops[eng].append((waits, None, None))

    def emit(self):
        nc = self.nc
        prog = self
        with nc.Block() as block:
            def run(engname, eobj):
                for waits, fn, inc in prog.ops[engname]:
                    for s, v in waits:
                        eobj.wait_ge(s, v)
                    if fn is not None:
                        fn(eobj).then_inc(inc[0], inc[1])

            @block.sync
            def _(e):
                run("sp", e)

            @block.tensor
            def _(e):
                run("pe", e)

            @block.scalar
            def _(e):
                run("act", e)

            @block.vector
            def _(e):
                run("dve", e)

            @block.gpsimd
            def _(e):
                run("pool", e)
        self.stack.close()


class Packer:
    def __init__(self):
        self.items = []
        self.off = {}
        self.n = 0

    def add(self, name, arr):
        arr = np.ascontiguousarray(arr, dtype=np.float32)
        assert arr.shape[0] == 128, (name, arr.shape)
        a2 = arr.reshape(128, -1)
        self.off[name] = (self.n, a2.shape[1], arr.shape[1:])
        self.items.append(a2)
        self.n += a2.shape[1]

    def build(self):
        return np.ascontiguousarray(np.concatenate(self.items, axis=1))


def pl(v, nch):
    return np.ascontiguousarray(np.asarray(v).reshape(nch, 128).T)


def col_index():
    idx = list(range(0, 1792))
    idx += list(range(1792, 3072))
    q0, k0, v0, g0 = 3072, 3456, 3584, 3712
    idx += list(range(q0, q0 + 384))
    idx += list(range(k0, k0 + 128))
    idx += list(range(v0, v0 + 128))
    idx += list(range(g0, g0 + 384))

    def sw(d):
        g, j = d // 32, d % 32
        return g * 32 + (1 - j // 16) * 16 + (j % 16)

    idx += [q0 + (c // 64) * 64 + sw(c % 64) for c in range(384)]
    idx += [k0 + (c // 64) * 64 + sw(c % 64) for c in range(128)]
    kb = [k0 + ((1 - c // 64) * 64) + (c % 64) for c in range(128)]
    idx += kb
    idx += [k0 + ((1 - c // 64) * 64) + sw(c % 64) for c in range(128)]
    return np.array(idx, dtype=np.int64)


def rope_tables(CTX, TL):
    NT = CTX + TL
    cos = np.ones((128, NT), np.float64)
    sins = np.zeros((128, NT), np.float64)
    t = np.arange(TL)
    row, col = t // 64, t % 64
    for p in range(128):
        d = p % 64
        g, j = d // 32, d % 32
        inv = 10000.0 ** (-(j % 16) / 16.0)
        pos = row if g == 0 else col
        ang = pos.astype(np.float32).astype(np.float64) * np.float32(inv).astype(np.float64)
        ang = (pos.astype(np.float32) * np.float32(inv)).astype(np.float64)
        cos[p, CTX:] = np.cos(ang)
        s = np.sin(ang)
        sins[p, CTX:] = -s if (j // 16) == 0 else s
    return np.stack([cos, sins], axis=1).astype(np.float32)


def build_consts():
    pk = Packer()
    pk.add("ident", np.eye(128))
    p = np.arange(128)[:, None]
    c = np.arange(128)[None, :]
    bo = (p // 64 == c // 64).astype(np.float32)
    pk.add("bo1", bo)
    pk.add("bo64", bo / 64.0)
    pk.add("sel0", np.broadcast_to((p // 64 == 0) / 64.0, (128, 128)))
    pk.add("sel1", np.broadcast_to((p // 64 == 1) / 64.0, (128, 128)))
    pk.add("ones", np.ones((128, 128)))
    pk.add("ident2", (np.arange(128)[:, None] % 64 == np.arange(64)[None, :]).astype(np.float32))
    s = (np.arange(128) % 64)[:, None]
    t = np.arange(64)[None, :]
    m = np.concatenate([s < t, s <= t, s < t, s <= t, t < s], axis=1).astype(np.float32)
    pk.add("mrw", m)
    pk.add("mhg", (s <= np.arange(32)[None, :]).astype(np.float32))
    kk = np.arange(128)[:, None]
    qq = np.arange(128)[None, :]
    pk.add("mprev", (kk >= qq).astype(np.float32))
    pk.add("mnext", (kk <= qq).astype(np.float32))
    tt = np.arange(512)[None, :]
    pk.add("rs64", np.broadcast_to((tt % 64 != 0).astype(np.float32), (128, 512)))
    pk.add("rs32", np.broadcast_to((tt % 32 != 0).astype(np.float32), (128, 512)))
    return pk


def build_params(inp, b, L):
    pk = Packer()
    cc = np.stack([pl(inp["c"][b], 8), pl(inp["c_ctx"], 8)], axis=2)
    pk.add("c", cc)
    pk.add("mod_b", np.stack([pl(inp["mod_b"][l], 24) for l in range(L)], axis=1))
    pk.add("norm_w", np.stack([pl(inp["norm_w"][l], 8) for l in range(L)], axis=1))
    pk.add("fnw", pl(inp["final_norm_w"], 8))
    pk.add("mu_p", np.stack([pl(inp["rwkv_mu_prev"][l], 14) for l in range(L)], axis=1))
    pk.add("mu_n", np.stack([pl(inp["rwkv_mu_next"][l], 14) for l in range(L)], axis=1))
    for nm, key in (("w0", "rwkv_w0"), ("a0", "rwkv_a0")):
        pk.add(nm, np.stack([np.stack([pl(inp[key][l, z], 3) for z in range(2)], axis=1) for l in range(L)], axis=1))
    for nm, key in (("k_k", "rwkv_k_k"), ("k_a", "rwkv_k_a"), ("gn_w", "rwkv_gn_w"), ("gn_b", "rwkv_gn_b")):
        pk.add(nm, np.stack([pl(inp[key][l], 3) for l in range(L)], axis=1))
    pk.add("r_k", np.stack([pl(inp["rwkv_r_k"][l].reshape(-1), 3) for l in range(L)], axis=1))
    pk.add("vres_b", pl(inp["rwkv_vres_b"][0], 3))
    lbl = np.stack([np.stack([pl(inp["hgrn_lb_logits"][z, l], 2) for l in range(L)], axis=1) for z in range(2)], axis=1)
    pk.add("lbl", lbl)
    pk.add("hnw", np.stack([pl(inp["hgrn_norm_w"][l], 2) for l in range(L)], axis=1))
    pk.add("sink", np.broadcast_to(np.asarray(inp["attn_sink"])[None, :, :], (128, L, 6)))
    pk.add("w2", np.stack([np.asarray(inp["rwkv_w2"][l]).reshape(128, 384) for l in range(L)], axis=1))
    pk.add("a2", np.stack([np.asarray(inp["rwkv_a2"][l]).reshape(128, 384) for l in range(L)], axis=1))
    vw2 = np.zeros((128, 384), np.float32)
    vw2[0:32] = np.asarray(inp["rwkv_vres_w2"][0])
    pk.add("vw2", vw2)
    return pk


class StopBuild(Exception):
    pass


def build_program(CTX, TL, L, cst_pk, prm_pk, dbg=(), upto=None):
    NT = CTX + TL
    assert TL % 512 == 0 and CTX % 128 == 0 and CTX <= 512
    groups = [(0, CTX)] + [(CTX + 512 * j, 512) for j in range(TL // 512)]
    NG = len(groups)
    nc = bass.Bass("TRN2", target_bir_lowering=False)
    P = Prog(nc, 46000)

    xin = P.dram("xin", [NT, D], F32, kind="ExternalInput")
    prm_d = P.dram("prm", [128, prm_pk.n], F32, kind="ExternalInput")
    cst_d = P.dram("cst", [128, cst_pk.n], F32, kind="ExternalInput")
    rope_d = P.dram("rope", [128, 2, NT], F32, kind="ExternalInput")
    wext_d = P.dram("wext", [L, D, NCOL], F32, kind="ExternalInput")
    modw_d = P.dram("modw", [L, D, 3 * D], F32, kind="ExternalInput")
    wout_d = P.dram("wout", [L, D, D], F32, kind="ExternalInput")
    out_d = P.dram("out", [TL, D], F32, kind="ExternalOutput", dsem="out")
    raw_d = P.dram("raw", [NCOL, NT], F32, dsem="raw")
    feat_d = [P.dram("feat%d" % l, [9 * RW, NT], F32, dsem="feat") for l in range(L)]
    yf_d = P.dram("yf", [RW, NT], F32, dsem="yf")
    of_d = P.dram("of", [HW, NT], F32, dsem="yf")
    yt_d = P.dram("yt", [D, NT], BF16, dsem="yt")
    x1_d = P.dram("x1", [NT, D], F32, dsem="x1")
    dbg_d = {}

    cst = P.alloc("cst", [128, cst_pk.n], dsem="cst")
    prm = P.alloc("prm", [128, prm_pk.n], dsem="prm")
    P.dma(cst.ap, cst_d.ap, [], [cst], cst.dsem)
    P.dma(prm.ap, prm_d.ap, [], [prm], prm.dsem)

    def C(name):
        o, n, shp = cst_pk.off[name]
        ap = cst.ap[:, o:o + n]
        return ap

    def PR(name):
        o, n, shp = prm_pk.off[name]
        ap = prm.ap[:, o:o + n]
        if len(shp) == 2:
            ap = ap.rearrange("p (a b) -> p a b", b=shp[1])
        elif len(shp) == 3:
            ap = ap.rearrange("p (a b c) -> p a b c", b=shp[1], c=shp[2])
        return ap

    ident = C("ident")
    bo1, bo64 = C("bo1"), C("bo64")
    ones_f = C("ones")
    modT = P.alloc("modT", [128, 24, 2])
    g1 = P.alloc("g1", [128, 8, 2])
    c0 = P.alloc("c0", [128, 14])
    lb = P.alloc("lb", [128, 2, 2])
    oml = P.alloc("oml", [128, 2, 2])
    ones_b = P.alloc("ones_b", [128, 64], BF16)
    P.op("pool", lambda e: e.memset(ones_b.ap, 1.0), [], [ones_b])
    pers_mark = P.bump

    PSW = [P.ps("psw%d" % i, [128, 1024]) for i in range(3)]
    PSB2 = [P.ps("psb%d" % i, [128, 512]) for i in range(2)]

    class Bank:
        def __init__(self, buf, key, ap):
            self.buf, self.key, self.ap = buf, key, ap

        def ref(self, sub=None):
            if self.key is None:
                return self.buf
            return (self.buf, self.key)

    PSB = [Bank(PSW[2], 0, PSW[2].ap[:, 0:512]), Bank(PSW[2], 1, PSW[2].ap[:, 512:1024]),
           Bank(PSB2[0], None, PSB2[0].ap), Bank(PSB2[1], None, PSB2[1].ap)]

    def mm(out, lhsT, rhs, rd, wr, start=True, stop=True):
        P.op("pe", lambda e: e.matmul(out, lhsT=lhsT, rhs=rhs, start=start, stop=stop), rd, wr)

    def tr(out, in_, idn, rd, wr):
        P.op("pe", lambda e: e.transpose(out, in_, idn), rd, wr)

    def act(out, in_, func, rd, wr, bias=None, scale=None, accum=None):
        kw = {}
        if bias is not None:
            kw["bias"] = bias
        if scale is not None:
            kw["scale"] = scale
        if accum is not None:
            kw["accum_out"] = accum
        P.op("act", lambda e: e.activation(out=out, in_=in_, func=func, **kw), rd, wr)

    def tt(eng, out, in0, in1, op, rd, wr):
        P.op(eng, lambda e: e.tensor_tensor(out=out, in0=in0, in1=in1, op=op), rd, wr)

    def ts(eng, out, in0, s1, s2, op0, op1, rd, wr):
        if s2 is None:
            P.op(eng, lambda e: e.tensor_scalar(out=out, in0=in0, scalar1=s1, scalar2=None, op0=op0), rd, wr)
        else:
            P.op(eng, lambda e: e.tensor_scalar(out=out, in0=in0, scalar1=s1, scalar2=s2, op0=op0, op1=op1), rd, wr)

    def stt(out, in0, scalar, in1, op0, op1, rd, wr):
        P.op("dve", lambda e: e.scalar_tensor_tensor(out=out, in0=in0, scalar=scalar, in1=in1, op0=op0, op1=op1), rd, wr)

    def cp(eng, out, in_, rd, wr):
        if eng == "act":
            act(out, in_, AF.Copy, rd, wr)
        else:
            P.op(eng, lambda e: e.tensor_copy(out=out, in_=in_), rd, wr)

    def recip(out, in_, rd, wr):
        P.op("dve", lambda e: e.reciprocal(out=out, in_=in_), rd, wr)

    rr = {"i": 0}

    def rot(engs=("act", "dve")):
        rr["i"] += 1
        return engs[rr["i"] % len(engs)]

    def ckpt(name):
        if upto == name:
            raise StopBuild()

    def layer(l):
        last = (l == L - 1)
        P.barrier()
        P.bump = pers_mark
        sc = P.alloc("sc", [128, 8, 2])
        act(sc.ap, PR("c"), AF.Silu, [prm], [sc])
        mws = [P.alloc("mws%d" % i, [128, 8, 256], dsem="mws%d" % i) for i in range(2)]
        psm = PSB[2]
        for q in range(12):
            st = mws[q % 2]
            src = modw_d.ap[l].rearrange("(kc p) n -> p kc n", p=128)[:, :, q * 256:(q + 1) * 256]
            P.dma(st.ap, src, [], [st], st.dsem)
            for jj in range(2):
                jc = q * 2 + jj
                for kc in range(8):
                    mm(psm.ap[:, jc * 2:jc * 2 + 2], st.ap[:, kc, jj * 128:(jj + 1) * 128], sc.ap[:, kc, :],
                       [st, sc], [(psm, jc)], start=(kc == 0), stop=(kc == 7))
        tt("dve", modT.ap, psm.ap[:, 0:48].rearrange("p (a b) -> p a b", b=2),
           PR("mod_b")[:, l, :].unsqueeze(2).to_broadcast([128, 24, 2]), ALU.add, [psm, prm], [modT])
        ts("dve", g1.ap, modT.ap[:, 8:16, :], 1.0, None, ALU.add, None, [modT], [g1])
        tt("dve", g1.ap, g1.ap, PR("norm_w")[:, l, :].unsqueeze(2).to_broadcast([128, 8, 2]), ALU.mult, [g1, prm], [g1])
        tt("dve", c0.ap, PR("mu_p")[:, l, :], PR("mu_n")[:, l, :], ALU.add, [prm], [c0])
        ts("dve", c0.ap, c0.ap, -1.0, 1.0, ALU.mult, ALU.add, [c0], [c0])
        if l == 0:
            P.op("dve", lambda e: e.memset(lb.ap, 0.0), [], [lb])
        else:
            tt("dve", lb.ap, PR("lbl")[:, :, 1, :], PR("lbl")[:, :, 0, :], ALU.subtract, [prm], [lb])
            act(lb.ap, lb.ap, AF.Sigmoid, [lb], [lb])
        ts("dve", oml.ap, lb.ap, -1.0, 1.0, ALU.mult, ALU.add, [lb], [oml])
        shiftv = modT.ap[:, 0:8, :]
        gatev = modT.ap[:, 16:24, :]

        ckpt('M%d' % l)
        P.barrier()
        P.bump = pers_mark
        wbf = P.alloc("wbf", [128, 8, NCOL], BF16)
        wst = [P.alloc("wst%d" % i, [128, 8, 256], dsem="wst%d" % i) for i in range(2)]
        npiece = NCOL // 256 + (1 if NCOL % 256 else 0)
        for q in range(npiece):
            st = wst[q % 2]
            c_lo = q * 256
            w = min(256, NCOL - c_lo)
            src = wext_d.ap[l].rearrange("(kc p) n -> p kc n", p=128)[:, :, c_lo:c_lo + w]
            P.dma(st.ap[:, :, 0:w], src, [], [st], st.dsem)
            cp(rot(("act", "dve", "pool")), wbf.ap[:, :, c_lo:c_lo + w], st.ap[:, :, 0:w], [st], [(wbf, q)])
        a_mark = P.bump
        ckpt('W%d' % l)

        xsrc = xin if l == 0 else x1_d
        xt = [P.alloc("xt%d" % i, [128, D], dsem="xt%d" % i) for i in range(2)]
        junk = P.alloc("junk", [128, D])
        xn = [P.alloc("xn%d" % i, [128, D]) for i in range(2)]
        st_ss = [P.alloc("ss%d" % i, [128, 2]) for i in range(2)]
        hT = [P.alloc("hT%d" % i, [128, 8, 512], BF16) for i in range(2)]
        evs = [P.alloc("evs%d" % i, [128, 512], dsem="evs%d" % i) for i in range(4)]
        ti = 0
        ei = 0
        for gi, (t0, N) in enumerate(groups):
            w = 1 if gi == 0 else 0
            hg = hT[gi % 2]
            for s in range(N // 128):
                xb, xnb, ssb = xt[ti % 2], xn[ti % 2], st_ss[ti % 2]
                psT = PSW[ti % 2]
                ti += 1
                tok = t0 + s * 128
                P.dma(xb.ap, xsrc.ap[tok:tok + 128, :], [], [xb], xb.dsem)
                act(junk.ap, xb.ap, AF.Square, [xb], [junk, (ssb, 0)], accum=ssb.ap[:, 0:1])
                ts("dve", ssb.ap[:, 1:2], ssb.ap[:, 0:1], 1.0 / D, NORM_EPS, ALU.mult, ALU.add, [(ssb, 0)], [(ssb, 1)])
                act(ssb.ap[:, 1:2], ssb.ap[:, 1:2], AF.Sqrt, [(ssb, 1)], [(ssb, 1)])
                recip(ssb.ap[:, 1:2], ssb.ap[:, 1:2], [(ssb, 1)], [(ssb, 1)])
                ts("dve", xnb.ap, xb.ap, ssb.ap[:, 1:2], None, ALU.mult, None, [xb, (ssb, 1)], [xnb])
                for j in range(8):
                    tr(psT.ap[:, j * 128:(j + 1) * 128], xnb.ap[:, j * 128:(j + 1) * 128], ident, [xnb, cst], [(psT, j // 4)])
                for j in range(8):
                    o = hg.ap[:, j, s * 128:(s + 1) * 128]
                    i_ = psT.ap[:, j * 128:(j + 1) * 128]
                    if j % 2 == 0:
                        act(o, i_, AF.Identity, [(psT, j // 4), g1, modT], [(hg, s)], scale=g1.ap[:, j, w:w + 1], bias=shiftv[:, j, w:w + 1])
                    else:
                        ts("dve", o, i_, g1.ap[:, j, w:w + 1], shiftv[:, j, w:w + 1], ALU.mult, ALU.add, [(psT, j // 4), g1, modT], [(hg, s)])
            nch = NCH if l > 0 else NCH - 1
            for c in range(nch):
                pb = PSB[c % 4]
                for kc in range(8):
                    mm(pb.ap[:, 0:N], wbf.ap[:, kc, c * 128:(c + 1) * 128], hg.ap[:, kc, 0:N], [wbf, hg], [pb],
                       start=(kc == 0), stop=(kc == 7))
                ev = evs[ei % 4]
                ei += 1
                cp(rot(), ev.ap[:, 0:N], pb.ap[:, 0:N], [pb], [ev])
                P.dma(raw_d.ap[c * 128:(c + 1) * 128, t0:t0 + N], ev.ap[:, 0:N], [ev], [], ev.dsem)

        ckpt('A%d' % l)
        P.barrier()
        P.bump = pers_mark
        FT = feat_d[l]

        def frow(arr, j):
            return arr * RW + j * 128

        RH = [P.alloc("rh%d" % i, [128, 514], dsem="rh%d" % i) for i in range(4)]
        SH = [P.alloc("sh%d" % i, [128, 512], dsem="sh%d" % i) for i in range(14)]
        tmpA = [P.alloc("tmpA%d" % i, [128, 512], dsem="tmpA%d" % i) for i in range(4)]
        vf = [P.alloc("vf%d" % i, [128, 512], dsem="vf%d" % i) for i in range(2)]
        hv = P.alloc("hv", [32, 512], dsem="hv")
        w2v, a2v = PR("w2"), PR("a2")
        hi = 0
        tai = 0
        for gi, (t0, N) in enumerate(groups):
            seq_lo, seq_hi = (0, CTX) if gi == 0 else (CTX, NT)
            for c in range(14):
                rh = RH[hi % 4]
                hi += 1
                lo = max(t0 - 1, seq_lo)
                hi_ = min(t0 + N + 1, seq_hi)
                P.dma(rh.ap[:, (lo - (t0 - 1)):(hi_ - (t0 - 1))], raw_d.ap[c * 128:(c + 1) * 128, lo:hi_], [], [rh], rh.dsem)
                if lo != t0 - 1:
                    P.op("pool", lambda e, rh=rh: e.memset(rh.ap[:, 0:1], 0.0), [], [rh])
                if hi_ != t0 + N + 1:
                    P.op("pool", lambda e, rh=rh, N=N: e.memset(rh.ap[:, N + 1:N + 2], 0.0), [], [rh])
                sh = SH[c]
                act(sh.ap[:, 0:N], rh.ap[:, 1:N + 1], AF.Identity, [rh, c0], [sh], scale=c0.ap[:, c:c + 1])
                stt(sh.ap[:, 0:N], rh.ap[:, 0:N], PR("mu_p")[:, l, c:c + 1], sh.ap[:, 0:N], ALU.mult, ALU.add, [rh, sh, prm], [sh])
                stt(sh.ap[:, 0:N], rh.ap[:, 2:N + 2], PR("mu_n")[:, l, c:c + 1], sh.ap[:, 0:N], ALU.mult, ALU.add, [rh, sh, prm], [sh])
            act(SH[9].ap[:, 0:N], SH[9].ap[:, 0:N], AF.Tanh, [SH[9]], [SH[9]])
            for (src, wv, b0, arr0, is_w) in ((SH[9], w2v, PR("w0"), 5, True), (SH[10], a2v, PR("a0"), 7, False)):
                for z in range(2):
                    for j in range(3):
                        pb = PSB[(z * 3 + j) % 4]
                        mm(pb.ap[:, 0:N], wv[z * 64:(z + 1) * 64, l, j * 128:(j + 1) * 128], src.ap[z * 64:(z + 1) * 64, 0:N],
                           [prm, src], [pb])
                        ta = tmpA[tai % 4]
                        tai += 1
                        act(ta.ap[:, 0:N], pb.ap[:, 0:N], AF.Sigmoid, [pb, prm], [ta], bias=b0[:, l, z, j:j + 1])
                        if is_w:
                            ts("dve", ta.ap[:, 0:N], ta.ap[:, 0:N], DECAY_K, None, ALU.mult, None, [ta], [ta])
                        r0 = frow(arr0 + z, j)
                        P.dma(FT.ap[r0:r0 + 128, t0:t0 + N], ta.ap[:, 0:N], [ta], [], ta.dsem)
            if l > 0:
                P.dma(hv.ap[:, 0:N], raw_d.ap[38 * 128:38 * 128 + 32, t0:t0 + N], [], [hv], hv.dsem)
                for j in range(3):
                    pb = PSB[j % 4]
                    mm(pb.ap[:, 0:N], PR("vw2")[0:32, j * 128:(j + 1) * 128], hv.ap[:, 0:N], [prm, hv], [pb])
                    ta = tmpA[tai % 4]
                    tai += 1
                    act(ta.ap[:, 0:N], pb.ap[:, 0:N], AF.Sigmoid, [pb, prm], [ta], bias=PR("vres_b")[:, j:j + 1])
                    v1 = vf[j % 2]
                    r0 = frow(2, j)
                    P.dma(v1.ap[:, 0:N], feat_d[0].ap[r0:r0 + 128, t0:t0 + N], [], [v1], v1.dsem)
                    vv = SH[6 + j]
                    tt("dve", v1.ap[:, 0:N], v1.ap[:, 0:N], vv.ap[:, 0:N], ALU.subtract, [v1, vv], [v1])
                    tt("dve", v1.ap[:, 0:N], v1.ap[:, 0:N], ta.ap[:, 0:N], ALU.mult, [v1, ta], [v1])
                    tt("dve", vv.ap[:, 0:N], vv.ap[:, 0:N], v1.ap[:, 0:N], ALU.add, [vv, v1], [vv])
            for j in range(3):
                ta = tmpA[tai % 4]
                tb = tmpA[(tai + 1) % 4]
                tai += 2
                pb = PSB[(j + 2) % 4]
                ts("dve", ta.ap[:, 0:N], SH[3 + j].ap[:, 0:N], PR("k_k")[:, l, j:j + 1], None, ALU.mult, None, [SH[3 + j], prm], [ta])
                tt("dve", tb.ap[:, 0:N], ta.ap[:, 0:N], ta.ap[:, 0:N], ALU.mult, [ta], [tb])
                mm(pb.ap[:, 0:N], bo1, tb.ap[:, 0:N], [cst, tb], [pb])
                act(tb.ap[:, 0:N], pb.ap[:, 0:N], AF.Sqrt, [pb], [tb])
                ts("dve", tb.ap[:, 0:N], tb.ap[:, 0:N], 1e-12, None, ALU.max, None, [tb], [tb])
                recip(tb.ap[:, 0:N], tb.ap[:, 0:N], [tb], [tb])
                tt("dve", ta.ap[:, 0:N], ta.ap[:, 0:N], tb.ap[:, 0:N], ALU.mult, [ta, tb], [ta])
                r0 = frow(3, j)
                P.dma(FT.ap[r0:r0 + 128, t0:t0 + N], ta.ap[:, 0:N], [ta], [], ta.dsem)
            for (arr, c_lo) in ((0, 0), (1, 3), (2, 6), (4, 11)):
                for j in range(3):
                    r0 = frow(arr, j)
                    sh = SH[c_lo + j]
                    P.dma(FT.ap[r0:r0 + 128, t0:t0 + N], sh.ap[:, 0:N], [sh], [], sh.dsem)

        ckpt('B1%d' % l)
        def scan_pass(kind, z):
            rw = kind == "rw"
            Cn = 64 if rw else 32
            nhp = 3 if rw else 2
            yfd = yf_d if rw else of_d
            rs = C("rs64") if rw else C("rs32")
            mrw, mhg = C("mrw"), C("mhg")
            P.barrier()
            P.bump = pers_mark
            WS = []
            for hp in range(nhp):
                d = {}
                names = ["r", "k", "v", "vs", "kk", "a", "lw", "Lc", "E", "T1", "T2", "bt", "kt", "bh", "kh", "gg", "yfl", "yo"] if rw else \
                        ["r", "k", "v", "vs", "lw", "Lc", "E", "T1", "kt", "kh", "gg", "yfl", "yo"]
                for nm in names:
                    d[nm] = P.alloc("%s%d" % (nm, hp), [128, 512], dsem=("ws_%s%d" % (nm, hp)) if nm in ("r", "k", "v", "kk", "a", "lw", "gg", "yfl", "yo") else None)
                d["AR"] = P.alloc("AR%d" % hp, [128, 8, 128] if rw else [128, 16, 32])
                d["Pc"] = P.alloc("Pc%d" % hp, [128, 16])
                d["A"] = [P.alloc("A%d_%d" % (hp, i), [128, 64]) for i in range(2)]
                d["Z"] = P.alloc("Z%d" % hp, [128, 64])
                d["U"] = P.alloc("U%d" % hp, [128, 64])
                d["VBK"] = [P.alloc("VBK%d_%d" % (hp, i), [128, 192]) for i in range(2)]
                d["G"] = [P.alloc("G%d_%d" % (hp, i), [128, 320]) for i in range(2)]
                d["XX"] = [P.alloc("XX%d_%d" % (hp, i), [128, 128]) for i in range(2)]
                d["TT"] = [P.alloc("TT%d_%d" % (hp, i), [128, 64]) for i in range(2)]
                d["ai"] = 0
                P.op("pool", lambda e, d=d: e.memset(d["A"][0].ap, 0.0), [], [d["A"][0]])
                for i_ in range(2):
                    P.op("pool", lambda e, d=d, i_=i_: e.memset(d["VBK"][i_].ap, 0.0), [], [d["VBK"][i_]])
                    P.op("pool", lambda e, d=d, i_=i_: e.memset(d["G"][i_].ap, 0.0), [], [d["G"][i_]])
                WS.append(d)
            psS = [PSW[0], PSW[1], PSW[2]]
            def U_(pS, u, n=1):
                return pS.ap[:, u * 64:(u + n) * 64]

            order = list(range(NG)) if z == 0 else [0] + list(range(NG - 1, 0, -1))
            for gi in order:
                t0, N = groups[gi]
                nck = N // Cn
                want_y = not (last and gi == 0)

                def V(ap, N=N):
                    return ap[:, 0:N] if z == 0 else ap[:, 0:N][:, ::-1]

                def c3(ap, N=N):
                    return ap[:, 0:N].rearrange("p (c t) -> p c t", t=Cn)

                for hp in range(nhp):
                    d = WS[hp]
                    if rw:
                        srcs = (("r", 0), ("k", 1), ("v", 2), ("kk", 3), ("a", 7 + z), ("lw", 5 + z))
                        for nm, arr in srcs:
                            r0 = frow(arr, hp)
                            P.dma(d[nm].ap[:, 0:N], FT.ap[r0:r0 + 128, t0:t0 + N], [], [d[nm]], d[nm].dsem)
                    else:
                        for nm, ch in (("r", 14), ("lw", 16 + 2 * z), ("v", 20)):
                            r0 = (ch + hp) * 128
                            P.dma(d[nm].ap[:, 0:N], raw_d.ap[r0:r0 + 128, t0:t0 + N], [], [d[nm]], d[nm].dsem)
                        act(d["lw"].ap[:, 0:N], d["lw"].ap[:, 0:N], AF.Sigmoid, [d["lw"]], [d["lw"]])
                        ts("dve", d["lw"].ap[:, 0:N], d["lw"].ap[:, 0:N], oml.ap[:, z, hp:hp + 1], lb.ap[:, z, hp:hp + 1], ALU.mult, ALU.add,
                           [d["lw"], oml, lb], [d["lw"]])
                        ts("dve", d["k"].ap[:, 0:N], d["lw"].ap[:, 0:N], -1.0, 1.0, ALU.mult, ALU.add, [d["lw"]], [d["k"]])
                        act(d["lw"].ap[:, 0:N], d["lw"].ap[:, 0:N], AF.Ln, [d["lw"]], [d["lw"]])
                    if z == 1 and want_y:
                        r0 = hp * 128
                        P.dma(d["yfl"].ap[:, 0:N], yfd.ap[r0:r0 + 128, t0:t0 + N], [], [d["yfl"]], d["yfl"].dsem)
                        if rw:
                            r0 = frow(4, hp)
                            P.dma(d["gg"].ap[:, 0:N], FT.ap[r0:r0 + 128, t0:t0 + N], [], [d["gg"]], d["gg"].dsem)
                        else:
                            r0 = (22 + hp) * 128
                            P.dma(d["gg"].ap[:, 0:N], raw_d.ap[r0:r0 + 128, t0:t0 + N], [], [d["gg"]], d["gg"].dsem)
                    Lc, E, T1 = d["Lc"], d["E"], d["T1"]
                    cp("pool" if z == 0 else "dve", d["vs"].ap[:, 0:N], V(d["v"].ap), [d["v"]], [d["vs"]])
                    P.op("dve", lambda e, Lc=Lc, d=d, N=N, V=V: e.tensor_tensor_scan(out=Lc.ap[:, 0:N], data0=rs[:, 0:N], data1=V(d["lw"].ap),
                                                                                     initial=0.0, op0=ALU.mult, op1=ALU.add), [d["lw"], cst], [Lc])
                    AR = d["AR"]
                    bc = c3(Lc.ap)[:, :, Cn - 1:Cn].to_broadcast([128, nck, Cn])
                    if rw:
                        T2 = d["T2"]
                        act(E.ap[:, 0:N], Lc.ap[:, 0:N], AF.Exp, [Lc], [E])
                        tt("dve", AR.ap[:, 0:nck, 64:128], c3(V(d["r"].ap)), c3(E.ap), ALU.mult, [d["r"], E], [AR])
                        tt("dve", T1.ap[:, 0:N], Lc.ap[:, 0:N], V(d["lw"].ap), ALU.subtract, [Lc, d["lw"]], [T1])
                        act(T1.ap[:, 0:N], T1.ap[:, 0:N], AF.Exp, [T1], [T1])
                        stt(AR.ap[:, 0:nck, 0:64], c3(V(d["kk"].ap)), -1.0, c3(T1.ap), ALU.mult, ALU.mult, [d["kk"], T1], [AR])
                        ts("dve", T1.ap[:, 0:N], V(d["a"].ap), -1.0, PR("k_a")[:, l, hp:hp + 1], ALU.add, ALU.mult, [d["a"], prm], [T1])
                        stt(T1.ap[:, 0:N], T1.ap[:, 0:N], 1.0, V(d["k"].ap), ALU.add, ALU.mult, [T1, d["k"]], [T1])
                        tt("dve", T2.ap[:, 0:N], V(d["kk"].ap), V(d["a"].ap), ALU.mult, [d["kk"], d["a"]], [T2])
                        act(E.ap[:, 0:N], Lc.ap[:, 0:N], AF.Exp, [Lc], [E], scale=-1.0)
                        tt("dve", d["bt"].ap[:, 0:N], T2.ap[:, 0:N], E.ap[:, 0:N], ALU.mult, [T2, E], [d["bt"]])
                        tt("dve", d["kt"].ap[:, 0:N], T1.ap[:, 0:N], E.ap[:, 0:N], ALU.mult, [T1, E], [d["kt"]])
                        tt("dve", c3(E.ap), bc, c3(Lc.ap), ALU.subtract, [Lc], [E])
                        act(E.ap[:, 0:N], E.ap[:, 0:N], AF.Exp, [E], [E])
                        tt("dve", d["bh"].ap[:, 0:N], T2.ap[:, 0:N], E.ap[:, 0:N], ALU.mult, [T2, E], [d["bh"]])
                        tt("dve", d["kh"].ap[:, 0:N], T1.ap[:, 0:N], E.ap[:, 0:N], ALU.mult, [T1, E], [d["kh"]])
                    else:
                        act(E.ap[:, 0:N], Lc.ap[:, 0:N], AF.Exp, [Lc], [E])
                        tt("dve", AR.ap[:, 0:nck, :], c3(V(d["r"].ap)), c3(E.ap), ALU.mult, [d["r"], E], [AR])
                        act(E.ap[:, 0:N], Lc.ap[:, 0:N], AF.Exp, [Lc], [E], scale=-1.0)
                        tt("dve", d["kt"].ap[:, 0:N], V(d["k"].ap), E.ap[:, 0:N], ALU.mult, [d["k"], E], [d["kt"]])
                        tt("dve", c3(E.ap), bc, c3(Lc.ap), ALU.subtract, [Lc], [E])
                        act(E.ap[:, 0:N], E.ap[:, 0:N], AF.Exp, [E], [E])
                        tt("dve", d["kh"].ap[:, 0:N], V(d["k"].ap), E.ap[:, 0:N], ALU.mult, [d["k"], E], [d["kh"]])
                    act(d["Pc"].ap[:, 0:nck], c3(Lc.ap)[:, :, Cn - 1], AF.Exp, [Lc], [d["Pc"]])

                ckpt('s_prep')
                for ck in range(nck):
                    cs = slice(ck * Cn, (ck + 1) * Cn)
                    yu = 14 + (ck % 2)
                    for hp in range(nhp):
                        d = WS[hp]
                        pS = psS[hp]
                        vbk = d["VBK"][ck % 2]
                        G = d["G"][ck % 2]
                        tl = (d["vs"], d["bh"], d["kh"]) if rw else (d["vs"], d["kh"])
                        nt_ = len(tl)
                        for e_ in range(2):
                            ps_ = slice(e_ * 64, e_ * 64 + 64)
                            po_ = slice(e_ * 64, e_ * 64 + Cn)
                            for ti_, sb_ in enumerate(tl):
                                mm(pS.ap[po_, ti_ * 64:(ti_ + 1) * 64], sb_.ap[ps_, cs], ident[ps_, ps_], [sb_, cst], [(pS, 0)])
                        ckpt('s_a1')
                        cp("dve", vbk.ap[:, 0:nt_ * 64], pS.ap[:, 0:nt_ * 64], [(pS, 0)], [vbk])
                        ckpt('s_a2')
                        for e_ in range(2):
                            ps_ = slice(e_ * 64, e_ * 64 + 64)
                            po_ = slice(e_ * 64, e_ * 64 + Cn)
                            if rw:
                                arv = d["AR"].ap[ps_, ck, :]
                                mm(pS.ap[po_, 192:320], d["bt"].ap[ps_, cs], arv, [d["bt"], d["AR"]], [(pS, 0)])
                                mm(pS.ap[po_, 320:448], d["kt"].ap[ps_, cs], arv, [d["kt"], d["AR"]], [(pS, 0)])
                                mm(pS.ap[po_, 448:512], d["AR"].ap[ps_, ck, 0:64], d["bt"].ap[ps_, cs], [d["bt"], d["AR"]], [(pS, 0)])
                            else:
                                mm(pS.ap[po_, 192:192 + Cn], d["kt"].ap[ps_, cs], d["AR"].ap[ps_, ck, :], [d["kt"], d["AR"]], [(pS, 0)])
                        ckpt('s_a3')
                        if rw:
                            tt("dve", G.ap[:, 0:320], pS.ap[:, 192:512], mrw, ALU.mult, [(pS, 0), cst], [G])
                        else:
                            tt("dve", G.ap[:, 0:Cn], pS.ap[:, 192:192 + Cn], mhg, ALU.mult, [(pS, 0), cst], [G])
                    ckpt('s_a')
                    if rw:
                        Xc = [None] * nhp
                        XTc = [None] * nhp
                        for hp in range(nhp):
                            d = WS[hp]
                            G = d["G"][ck % 2]
                            tt("dve", d["TT"][0].ap, G.ap[:, 0:64], C("ident2"), ALU.add, [G, cst], [d["TT"][0]])
                            Xc[hp] = (G.ap[:, 256:320], G)
                            XTc[hp] = (G.ap[:, 0:64], G)
                        for lev in range(1, 6):
                            for hp in range(nhp):
                                pS = psS[hp]
                                Xa, Xb = Xc[hp]
                                XTa, XTb = XTc[hp]
                                for e_ in range(2):
                                    ps_ = slice(e_ * 64, e_ * 64 + 64)
                                    mm(pS.ap[ps_, 512:576], XTa[ps_, :], Xa[ps_, :], [XTb, Xb], [(pS, 1)])
                                    if lev < 5:
                                        mm(pS.ap[ps_, 576:640], Xa[ps_, :], XTa[ps_, :], [XTb, Xb], [(pS, 1)])
                            for hp in range(nhp):
                                d = WS[hp]
                                pS = psS[hp]
                                XXn = d["XX"][lev % 2]
                                if lev < 5:
                                    cp("dve", XXn.ap[:, 0:128], pS.ap[:, 512:640], [(pS, 1)], [XXn])
                                    XTc[hp] = (XXn.ap[:, 64:128], XXn)
                                else:
                                    cp("dve", XXn.ap[:, 0:64], U_(pS, 8), [(pS, 1)], [XXn])
                                Xc[hp] = (XXn.ap[:, 0:64], XXn)
                            for hp in range(nhp):
                                d = WS[hp]
                                pS = psS[hp]
                                Xa, Xb = Xc[hp]
                                TTo = d["TT"][(lev - 1) % 2]
                                for e_ in range(2):
                                    ps_ = slice(e_ * 64, e_ * 64 + 64)
                                    mm(pS.ap[ps_, 640:704], Xa[ps_, :], TTo.ap[ps_, :], [Xb, TTo], [(pS, 1)])
                            for hp in range(nhp):
                                d = WS[hp]
                                pS = psS[hp]
                                TTo = d["TT"][(lev - 1) % 2]
                                TTn = d["TT"][lev % 2]
                                tt("dve", TTn.ap, U_(pS, 10), TTo.ap, ALU.add, [(pS, 1), TTo], [TTn])
                    ckpt('s_b')
                    if rw:
                        for hp in range(nhp):
                            d = WS[hp]
                            pS = psS[hp]
                            A0 = d["A"][d["ai"] % 2]
                            vbk = d["VBK"][ck % 2]
                            G = d["G"][ck % 2]
                            for e_ in range(2):
                                ps_ = slice(e_ * 64, e_ * 64 + 64)
                                mm(pS.ap[ps_, 704:768], d["AR"].ap[ps_, ck, 0:64], A0.ap[ps_, :], [d["AR"], A0], [(pS, 1)], start=True, stop=False)
                                mm(pS.ap[ps_, 704:768], G.ap[ps_, 128:192], vbk.ap[ps_, 0:64], [G, vbk], [(pS, 1)], start=False, stop=True)
                        for hp in range(nhp):
                            d = WS[hp]
                            cp("dve", d["Z"].ap, U_(psS[hp], 11), [(psS[hp], 1)], [d["Z"]])
                        for hp in range(nhp):
                            d = WS[hp]
                            pS = psS[hp]
                            TTf = d["TT"][1]
                            for e_ in range(2):
                                ps_ = slice(e_ * 64, e_ * 64 + 64)
                                mm(pS.ap[ps_, 768:832], TTf.ap[ps_, :], d["Z"].ap[ps_, :], [TTf, d["Z"]], [(pS, 1)])
                        for hp in range(nhp):
                            d = WS[hp]
                            cp("dve", d["U"].ap, U_(psS[hp], 12), [(psS[hp], 1)], [d["U"]])
                    for hp in range(nhp):
                        d = WS[hp]
                        pS = psS[hp]
                        A0 = d["A"][d["ai"] % 2]
                        A1 = d["A"][(d["ai"] + 1) % 2]
                        vbk = d["VBK"][ck % 2]
                        G = d["G"][ck % 2]
                        ycol = slice(yu * 64, yu * 64 + Cn)
                        for e_ in range(2):
                            ps_ = slice(e_ * 64, e_ * 64 + 64)
                            po_ = slice(e_ * 64, e_ * 64 + Cn)
                            if rw:
                                if want_y:
                                    mm(pS.ap[ps_, ycol], A0.ap[ps_, :], d["AR"].ap[ps_, ck, 64:128], [A0, d["AR"]], [(pS, 1)], start=True, stop=False)
                                    mm(pS.ap[ps_, ycol], d["U"].ap[ps_, :], G.ap[ps_, 64:128], [d["U"], G], [(pS, 1)], start=False, stop=False)
                                    mm(pS.ap[ps_, ycol], vbk.ap[ps_, 0:64], G.ap[ps_, 192:256], [vbk, G], [(pS, 1)], start=False, stop=True)
                                mm(pS.ap[ps_, 832:896], vbk.ap[ps_, 64:128], d["U"].ap[ps_, :], [vbk, d["U"]], [(pS, 1)], start=True, stop=False)
                                mm(pS.ap[ps_, 832:896], vbk.ap[ps_, 128:192], vbk.ap[ps_, 0:64], [vbk], [(pS, 1)], start=False, stop=True)
                            else:
                                if want_y:
                                    mm(pS.ap[ps_, ycol], A0.ap[ps_, :], d["AR"].ap[ps_, ck, :], [A0, d["AR"]], [(pS, 1)], start=True, stop=False)
                                    mm(pS.ap[ps_, ycol], vbk.ap[po_, 0:64], G.ap[po_, 0:Cn], [vbk, G], [(pS, 1)], start=False, stop=True)
                                mm(pS.ap[ps_, 832:896], vbk.ap[po_, 64:128], vbk.ap[po_, 0:64], [vbk], [(pS, 1)])
                        stt(A1.ap, A0.ap, d["Pc"].ap[:, ck:ck + 1], U_(pS, 13), ALU.mult, ALU.add, [A0, d["Pc"], (pS, 1)], [A1])
                        d["ai"] += 1
                        if want_y:
                            yo = d["yo"]
                            if z == 0:
                                cp("dve", yo.ap[:, cs], pS.ap[:, ycol], [(pS, 1)], [(yo, ck)])
                            else:
                                nat = slice(N - (ck + 1) * Cn, N - ck * Cn)
                                cp("dve", yo.ap[:, nat], pS.ap[:, ycol][:, ::-1], [(pS, 1)], [(yo, ck)])

                ckpt('s_c')
                if not want_y:
                    continue
                for hp in range(nhp):
                    d = WS[hp]
                    yo = d["yo"]
                    if z == 0:
                        r0 = hp * 128
                        P.dma(yfd.ap[r0:r0 + 128, t0:t0 + N], yo.ap[:, 0:N], [yo], [], yo.dsem)
                        continue
                    T1, E = d["T1"], d["E"]
                    y = d["Lc"]
                    pR = psS[hp]
                    act(E.ap[:, 0:N], d["gg"].ap[:, 0:N], AF.Silu, [d["gg"]], [E])
                    tt("dve", y.ap[:, 0:N], d["yfl"].ap[:, 0:N], yo.ap[:, 0:N], ALU.add, [d["yfl"], yo], [y])
                    if rw:
                        mm(pR.ap[:, 0:N], bo64, y.ap[:, 0:N], [cst, y], [pR])
                        tt("dve", y.ap[:, 0:N], y.ap[:, 0:N], pR.ap[:, 0:N], ALU.subtract, [y, pR], [y])
                    act(T1.ap[:, 0:N], y.ap[:, 0:N], AF.Square, [y], [T1])
                    mm(pR.ap[:, 512:512 + N], bo64, T1.ap[:, 0:N], [cst, T1], [pR])
                    ts("dve", T1.ap[:, 0:N], pR.ap[:, 512:512 + N], GN_EPS if rw else NORM_EPS, None, ALU.add, None, [pR], [T1])
                    act(T1.ap[:, 0:N], T1.ap[:, 0:N], AF.Sqrt, [T1], [T1])
                    recip(T1.ap[:, 0:N], T1.ap[:, 0:N], [T1], [T1])
                    tt("dve", y.ap[:, 0:N], y.ap[:, 0:N], T1.ap[:, 0:N], ALU.mult, [y, T1], [y])
                    if rw:
                        ts("dve", y.ap[:, 0:N], y.ap[:, 0:N], PR("gn_w")[:, l, hp:hp + 1], PR("gn_b")[:, l, hp:hp + 1], ALU.mult, ALU.add, [y, prm], [y])
                        stt(T1.ap[:, 0:N], d["r"].ap[:, 0:N], PR("r_k")[:, l, hp:hp + 1], d["k"].ap[:, 0:N], ALU.mult, ALU.mult, [d["r"], d["k"], prm], [T1])
                        mm(pR.ap[:, 0:N], bo1, T1.ap[:, 0:N], [cst, T1], [pR])
                        tt("dve", T1.ap[:, 0:N], pR.ap[:, 0:N], d["v"].ap[:, 0:N], ALU.mult, [pR, d["v"]], [T1])
                        tt("dve", y.ap[:, 0:N], y.ap[:, 0:N], T1.ap[:, 0:N], ALU.add, [y, T1], [y])
                    else:
                        ts("dve", y.ap[:, 0:N], y.ap[:, 0:N], PR("hnw")[:, l, hp:hp + 1], None, ALU.mult, None, [y, prm], [y])
                    yob = yo.ap.bitcast(BF16)
                    tt("dve", yob[:, 0:N], y.ap[:, 0:N], E.ap[:, 0:N], ALU.mult, [y, E], [yo])
                    r0 = (hp if rw else 3 + hp) * 128
                    P.dma(yt_d.ap[r0:r0 + 128, t0:t0 + N], yob[:, 0:N], [yo], [], yo.dsem)

        scan_pass("rw", 0)
        ckpt('rw0%d' % l)
        scan_pass("rw", 1)
        ckpt('rw1%d' % l)
        scan_pass("hg", 0)
        ckpt('hg0%d' % l)
        scan_pass("hg", 1)
        ckpt('hg1%d' % l)

        P.barrier()
        P.bump = pers_mark
        Qr = P.alloc("Qr", [128, 3, NT], BF16)
        Kr = P.alloc("Kr", [128, 2, NT], BF16)
        Vt = P.alloc("Vt", [128, NT // 128, 128], BF16)
        Gs = P.alloc("Gs", [128, 3, NT], BF16)
        mx = P.alloc("mx", [128, 8])
        P.op("pool", lambda e: e.memset(mx.ap, 0.0), [], [mx])
        ld = [P.alloc("ald%d" % i, [128, 512], dsem="ald%d" % i) for i in range(4)]
        rp = [P.alloc("arp%d" % i, [128, 2, 512], dsem="arp%d" % i) for i in range(2)]
        tq = [P.alloc("atq%d" % i, [128, 512]) for i in range(3)]
        li = 0
        for gi, (t0, N) in enumerate(groups):
            rpb = rp[gi % 2]
            P.dma(rpb.ap[:, :, 0:N], rope_d.ap[:, :, t0:t0 + N], [], [rpb], rpb.dsem)
            for (dst, di, c_main, c_sw, mxi) in ((Qr, 0, 24, 32, 0), (Qr, 1, 25, 33, 1), (Qr, 2, 26, 34, 2), (Kr, 0, 27, 35, 3), (Kr, 1, 36, 37, 4)):
                a_ = ld[li % 4]
                b_ = ld[(li + 1) % 4]
                li += 2
                P.dma(a_.ap[:, 0:N], raw_d.ap[c_main * 128:(c_main + 1) * 128, t0:t0 + N], [], [a_], a_.dsem)
                P.dma(b_.ap[:, 0:N], raw_d.ap[c_sw * 128:(c_sw + 1) * 128, t0:t0 + N], [], [b_], b_.dsem)
                tt("dve", a_.ap[:, 0:N], a_.ap[:, 0:N], rpb.ap[:, 0, 0:N], ALU.mult, [a_, rpb], [a_])
                tt("dve", b_.ap[:, 0:N], b_.ap[:, 0:N], rpb.ap[:, 1, 0:N], ALU.mult, [b_, rpb], [b_])
                tt("dve", tq[0].ap[:, 0:N], a_.ap[:, 0:N], b_.ap[:, 0:N], ALU.add, [a_, b_], [tq[0]])
                cp("pool", dst.ap[:, di, t0:t0 + N], tq[0].ap[:, 0:N], [tq[0]], [(dst, (di, gi))])
                tt("dve", tq[1].ap[:, 0:N], tq[0].ap[:, 0:N], tq[0].ap[:, 0:N], ALU.mult, [tq[0]], [tq[1]])
                pb = PSB[mxi % 4]
                mm(pb.ap[:, 0:N], bo1, tq[1].ap[:, 0:N], [cst, tq[1]], [pb])
                P.op("dve", lambda e, pb=pb, N=N: e.reduce_max(out=tq[2].ap[:, 0:1], in_=pb.ap[:, 0:N], axis=AX.X), [pb], [tq[2]])
                tt("dve", mx.ap[:, mxi:mxi + 1], mx.ap[:, mxi:mxi + 1], tq[2].ap[:, 0:1], ALU.max, [mx, tq[2]], [mx])
            a_ = ld[li % 4]
            li += 1
            P.dma(a_.ap[:, 0:N], raw_d.ap[28 * 128:29 * 128, t0:t0 + N], [], [a_], a_.dsem)
            for s in range(N // 128):
                pb = PSB[s % 4]
                tr(pb.ap[:, 0:128], a_.ap[:, s * 128:(s + 1) * 128], ident, [a_, cst], [pb])
                cp("dve", Vt.ap[:, (t0 // 128) + s, :], pb.ap[:, 0:128], [pb], [(Vt, (t0 // 128) + s)])
            for j in range(3):
                a_ = ld[li % 4]
                li += 1
                P.dma(a_.ap[:, 0:N], raw_d.ap[(29 + j) * 128:(30 + j) * 128, t0:t0 + N], [], [a_], a_.dsem)
                act(Gs.ap[:, j, t0:t0 + N], a_.ap[:, 0:N], AF.Silu, [a_], [(Gs, (j, gi))])
        act(tq[2].ap[:, 1:2], mx.ap[:, 0:1], AF.Square, [mx], [tq[2]])
        act(tq[2].ap[:, 1:2], tq[2].ap[:, 1:2], AF.Sqrt, [tq[2]], [tq[2]])
        negM = P.alloc("negM", [128, 6])
        sinkE = P.alloc("sinkE", [128, 6])
        msum = P.alloc("msum", [128, 6])
        kAB = {0: 0, 1: 1, 2: 0, 3: 0, 4: 1, 5: 0}
        for h in range(6):
            tt("dve", msum.ap[:, h:h + 1], mx.ap[:, h // 2:h // 2 + 1], mx.ap[:, 3 + kAB[h]:4 + kAB[h]], ALU.add, [mx], [(msum, h)])
        pb = PSB[2]
        for h in range(6):
            sel = C("sel0") if h % 2 == 0 else C("sel1")
            mm(pb.ap[:, h:h + 1], sel, msum.ap[:, h:h + 1], [cst, msum], [(pb, h)])
        ts("dve", negM.ap, pb.ap[:, 0:6], -1.0 / 16.0, None, ALU.mult, None, [pb], [negM])
        tt("dve", sinkE.ap, PR("sink")[:, l, :], negM.ap, ALU.add, [prm, negM], [sinkE])
        act(sinkE.ap, sinkE.ap, AF.Exp, [sinkE], [sinkE])

        Eb = [P.alloc("Eb%d" % i, [128, 640], BF16) for i in range(3)]
        den = [P.alloc("den%d" % i, [128, 128]) for i in range(2)]
        yto = [P.alloc("yto%d" % i, [128, 128], BF16, dsem="yto%d" % i) for i in range(3)]
        mprev_b = P.alloc("mprev_b", [128, 128], BF16)
        mnext_b = P.alloc("mnext_b", [128, 128], BF16)
        cp("dve", mprev_b.ap, C("mprev"), [cst], [mprev_b])
        cp("dve", mnext_b.ap, C("mnext"), [cst], [mnext_b])
        nctx = CTX // 128
        nlat = TL // 128
        qblocks = [("l", n) for n in range(nlat)] + ([] if last else [("c", n) for n in range(nctx)])
        bi = 0
        for (kind_, n) in qblocks:
            if kind_ == "l":
                qtok = CTX + n * 128
                kb = []
                if n > 0:
                    kb.append((qtok - 128, "prev"))
                kb.append((qtok, None))
                if n < nlat - 1:
                    kb.append((qtok + 128, "next"))
                kb += [(c_ * 128, None) for c_ in range(nctx)]
            else:
                qtok = n * 128
                kb = [(c_ * 128, None) for c_ in range(nctx)]
            nk = len(kb)
            for hpair in range(3):
                pN = PSB[(bi) % 2]
                pD = PSB[2 + (bi % 2)]
                yb_ = yto[bi % 3]
                dn = den[bi % 2]
                for e_ in range(2):
                    h = hpair * 2 + e_
                    ps_ = slice(e_ * 64, e_ * 64 + 64)
                    pS = PSW[h % 2]
                    eb = Eb[h % 3]
                    for ki, (kt0, mk) in enumerate(kb):
                        mm(pS.ap[:, ki * 128:(ki + 1) * 128], Kr.ap[ps_, kAB[h], kt0:kt0 + 128], Qr.ap[ps_, hpair, qtok:qtok + 128],
                           [Kr, Qr], [(pS, ki // 4)])
                    act(eb.ap[:, 0:nk * 128], pS.ap[:, 0:nk * 128], AF.Exp, [pS, negM], [eb], scale=0.125, bias=negM.ap[:, h:h + 1])
                    for ki, (kt0, mk) in enumerate(kb):
                        if mk is not None:
                            mb = mprev_b if mk == "prev" else mnext_b
                            tt("dve", eb.ap[:, ki * 128:(ki + 1) * 128], eb.ap[:, ki * 128:(ki + 1) * 128], mb.ap, ALU.mult, [eb, mb], [eb])
                    kvh = h // 3
                    for ki, (kt0, mk) in enumerate(kb):
                        mm(pN.ap[ps_, 0:128], Vt.ap[:, kt0 // 128, kvh * 64:(kvh + 1) * 64], eb.ap[:, ki * 128:(ki + 1) * 128],
                           [Vt, eb], [pN], start=(ki == 0), stop=(ki == nk - 1))
                    for ki, (kt0, mk) in enumerate(kb):
                        mm(pD.ap[ps_, 0:128], ones_b.ap, eb.ap[:, ki * 128:(ki + 1) * 128],
                           [ones_b, eb], [(pD, e_)], start=(ki == 0), stop=(ki == nk - 1))
                    ts("dve", dn.ap[ps_, :], pD.ap[ps_, 0:128], sinkE.ap[ps_, h:h + 1], None, ALU.add, None, [(pD, e_), sinkE], [(dn, e_)])
                recip(dn.ap, dn.ap, [dn], [dn])
                tt("dve", dn.ap, dn.ap, pN.ap[:, 0:128], ALU.mult, [dn, pN], [dn])
                tt("dve", yb_.ap, dn.ap, Gs.ap[:, hpair, qtok:qtok + 128], ALU.mult, [dn, Gs], [yb_])
                r0 = (5 + hpair) * 128
                P.dma(yt_d.ap[r0:r0 + 128, qtok:qtok + 128], yb_.ap, [yb_], [], yb_.dsem)
                bi += 1

        ckpt('att%d' % l)
        P.barrier()
        P.bump = pers_mark
        gbc = P.alloc("gbc", [128, 2, D])
        dg = [P.alloc("dg%d" % i, [128, 128]) for i in range(2)]
        nw = 1 if last else 2
        for w in range(nw):
            for j in range(8):
                dgb = dg[j % 2]
                ts("dve", dgb.ap, ident, gatev[:, j, w:w + 1], None, ALU.mult, None, [cst, modT], [dgb])
                pw = PSW[w]
                mm(pw.ap[:, j * 128:(j + 1) * 128], ones_f, dgb.ap, [cst, dgb], [(pw, j // 4)])
            cp("dve", gbc.ap[:, w, :], PSW[w].ap, [PSW[w]], [(gbc, w)])
        fnb = P.alloc("fnb", [128, D])
        if last:
            for j in range(8):
                dgb = dg[j % 2]
                ts("dve", dgb.ap, ident, PR("fnw")[:, j:j + 1], None, ALU.mult, None, [cst, prm], [dgb])
                pw = PSW[1]
                mm(pw.ap[:, j * 128:(j + 1) * 128], ones_f, dgb.ap, [cst, dgb], [(pw, j // 4)])
            cp("dve", fnb.ap, PSW[1].ap, [PSW[1]], [fnb])
        wog = P.alloc("wog", [128, 2, 8, D], BF16)
        wos = [P.alloc("wos%d" % i, [128, 8, 256], dsem="wos%d" % i) for i in range(2)]
        for q in range(4):
            st = wos[q % 2]
            src = wout_d.ap[l].rearrange("(kc p) n -> p kc n", p=128)[:, :, q * 256:(q + 1) * 256]
            P.dma(st.ap, src, [], [st], st.dsem)
            for w in range(nw):
                for kc in range(8):
                    tt("dve", wog.ap[:, w, kc, q * 256:(q + 1) * 256], st.ap[:, kc, :], gbc.ap[:, w, q * 256:(q + 1) * 256], ALU.mult,
                       [st, (gbc, w)], [(wog, (w, q))])
        ytl = [P.alloc("ytl%d" % i, [128, 8, 128], BF16, dsem="ytl%d" % i) for i in range(2)]
        xc = [P.alloc("xc%d" % i, [128, D], dsem="xc%d" % i) for i in range(2)]
        xo = [P.alloc("xo%d" % i, [128, D], dsem="xo%d" % i) for i in range(2)]
        sq = P.alloc("csq", [128, D])
        cs2 = [P.alloc("cs2_%d" % i, [128, 2]) for i in range(2)]
        tiles = list(range(CTX // 128, NT // 128)) + ([] if last else list(range(CTX // 128)))
        for ii, tI in enumerate(tiles):
            tok = tI * 128
            w = 1 if tok < CTX else 0
            yl, xcb, xob, pw, ssb = ytl[ii % 2], xc[ii % 2], xo[ii % 2], PSW[ii % 2], cs2[ii % 2]
            P.dma(yl.ap, yt_d.ap.rearrange("(c p) t -> p c t", p=128)[:, :, tok:tok + 128], [], [yl], yl.dsem)
            P.dma(xcb.ap, xsrc.ap[tok:tok + 128, :], [], [xcb], xcb.dsem)
            for half in range(2):
                for kc in range(8):
                    mm(pw.ap[:, half * 512:(half + 1) * 512], yl.ap[:, kc, :], wog.ap[:, w, kc, half * 512:(half + 1) * 512], [yl, wog], [(pw, half)],
                       start=(kc == 0), stop=(kc == 7))
            tt("dve", xob.ap, pw.ap, xcb.ap, ALU.add, [pw, xcb], [xob])
            if not last:
                P.dma(x1_d.ap[tok:tok + 128, :], xob.ap, [xob], [], xob.dsem)
            else:
                act(sq.ap, xob.ap, AF.Square, [xob], [sq, (ssb, 0)], accum=ssb.ap[:, 0:1])
                ts("dve", ssb.ap[:, 1:2], ssb.ap[:, 0:1], 1.0 / D, NORM_EPS, ALU.mult, ALU.add, [(ssb, 0)], [(ssb, 1)])
                act(ssb.ap[:, 1:2], ssb.ap[:, 1:2], AF.Sqrt, [(ssb, 1)], [(ssb, 1)])
                recip(ssb.ap[:, 1:2], ssb.ap[:, 1:2], [(ssb, 1)], [(ssb, 1)])
                stt(xob.ap, xob.ap, ssb.ap[:, 1:2], fnb.ap, ALU.mult, ALU.mult, [xob, (ssb, 1), fnb], [xob])
                P.dma(out_d.ap[tok - CTX:tok - CTX + 128, :], xob.ap, [xob], [], xob.dsem)

    try:
        for l_ in range(L):
            layer(l_)
    except StopBuild:
        pass
    P.barrier(("sp",))
    P.emit()
    return nc


def prepare_inputs(inp, CTX, TL, L):
    idx = col_index()
    B = inp["x"].shape[0]
    cst_pk = build_consts()
    cst = cst_pk.build()
    rope = rope_tables(CTX, TL)
    wext = np.zeros((L, D, NCOL), np.float32)
    for l in range(L):
        wext[l, :, 0:idx.size] = np.asarray(inp["w_in"][l])[:, idx]
        if l > 0:
            wext[l, :, 38 * 128:38 * 128 + 32] = np.asarray(inp["rwkv_vres_w1"][l - 1])
    modw = np.ascontiguousarray(inp["mod_w"], dtype=np.float32)
    wout = np.ascontiguousarray(inp["w_out"], dtype=np.float32)
    maps = []
    prm_pk = None
    for b in range(B):
        prm_pk = build_params(inp, b, L)
        xin = np.ascontiguousarray(np.concatenate([np.asarray(inp["ctx"][b]), np.asarray(inp["x"][b])], axis=0), dtype=np.float32)
        maps.append({"xin": xin, "prm": prm_pk.build(), "cst": cst, "rope": rope, "wext": wext, "modw": modw, "wout": wout})
    return maps, cst_pk, prm_pk


def kernel(**inputs):
    inp = {k: np.asarray(v) for k, v in inputs.items()}
    B, TL, _ = inp["x"].shape
    CTX = inp["ctx"].shape[1]
    L = inp["mod_w"].shape[0]
    maps, cst_pk, prm_pk = prepare_inputs(inp, CTX, TL, L)
    nc = build_program(CTX, TL, L, cst_pk, prm_pk)
    res = run_bass_kernel_spmd(nc, maps, core_ids=list(range(B)))
    return np.stack([np.asarray(r["out"], dtype=np.float32) for r in res.results], axis=0)
```
